# Optimizing a Trainium2 kernel written in Bass

```python
import math
import jax
import jax.numpy as jnp
from jax import lax
import numpy as np

D_MODEL = 2048
BATCH = 4
SEQ = 2048
DEPTH = 2

GRID_W = 64
CTX_LEN = 256
HEAD_DIM = 128
HY_WIDTH = D_MODEL // 4
HY_SHORT = 3
HY_FILTER_HIDDEN = 64
HY_POS_BANDS = 16
HY_POS_EMB = 1 + 2 * HY_POS_BANDS
HY_DECAY_TARGET = 1e-2
HY_FAST_DECAY = 0.3
HY_SLOW_DECAY = 1.5
MLA_HEADS = (D_MODEL // 2) // HEAD_DIM
MLA_NOPE = HEAD_DIM
MLA_ROPE = 64
MLA_V = HEAD_DIM
Q_LORA = 3 * D_MODEL // 8
KV_LORA = D_MODEL // 4
MLA_BLOCK = 128
MLA_SCALE = (MLA_NOPE + MLA_ROPE) ** -0.5
NA_HEADS = (D_MODEL // 4) // HEAD_DIM
NA_KH = 8
NA_KW = 16
NA_QB = 16
NA_KB = 2 * NA_KW
NA_SCALE = HEAD_DIM ** -0.5
FFN_HIDDEN = ((8 * D_MODEL + 3 * 256 - 1) // (3 * 256)) * 256
ROPE_THETA = 10000.0
RMS_EPS = 1e-6
MASK_VALUE = -1e30
IN_SPLITS = (3 * HY_WIDTH, Q_LORA, KV_LORA, MLA_ROPE, 3 * NA_HEADS * HEAD_DIM)
IN_COLS = sum(IN_SPLITS)

kernel_name = 'hybrid_hyena_mla_natten_dit_block'


def rmsnorm(x, g):
    xf = x.astype(jnp.float32)
    y = xf * lax.rsqrt(jnp.mean(xf * xf, axis=-1, keepdims=True) + RMS_EPS)
    return (y * g.astype(jnp.float32)).astype(x.dtype)


def modulate(x, g, shift, scale):
    return rmsnorm(x, g) * (1 + scale) + shift


def split_in(p):
    idx = [int(i) for i in np.cumsum(IN_SPLITS)[:-1]]
    return jnp.split(p, idx, axis=-1)


def axial_rope(n_tok):
    t = jnp.arange(n_tok)
    row = (t // GRID_W).astype(jnp.float32)
    col = (t % GRID_W).astype(jnp.float32)
    n_freq = MLA_ROPE // 4
    inv = ROPE_THETA ** (-jnp.arange(n_freq, dtype=jnp.float32) / n_freq)
    ang = jnp.concatenate([row[:, None] * inv, col[:, None] * inv], axis=-1)
    return jnp.cos(ang), jnp.sin(ang)


def apply_rope(x, cos, sin):
    half = x.shape[-1] // 2
    xf = x.astype(jnp.float32)
    x1, x2 = xf[..., :half], xf[..., half:]
    return jnp.concatenate([x1 * cos - x2 * sin, x2 * cos + x1 * sin], axis=-1).astype(x.dtype)


def short_conv(u, w, b):
    n = u.shape[1]
    up = jnp.pad(u, ((0, 0), (1, 1), (0, 0)))
    return up[:, :n] * w[0] + up[:, 1:n + 1] * w[1] + up[:, 2:] * w[2] + b


def hyena_filters(n, f_w1, f_b1, f_w2, f_b2, f_w3, f_freq):
    pos = jnp.arange(n, dtype=jnp.float32)
    t = jnp.linspace(0.0, 1.0, n, dtype=jnp.float32)
    bands = jnp.linspace(1e-4, HY_POS_BANDS - 1, HY_POS_BANDS, dtype=jnp.float32)
    ang = (2.0 * math.pi / n) * pos[:, None] * bands[None, :]
    z = jnp.concatenate([t[:, None], jnp.cos(ang), -jnp.sin(ang)], axis=-1)
    fr = f_freq.astype(jnp.float32)
    h = jnp.sin(fr * (z @ f_w1.astype(jnp.float32) + f_b1.astype(jnp.float32)))
    h = jnp.sin(fr * (h @ f_w2.astype(jnp.float32) + f_b2.astype(jnp.float32)))
    h = h @ f_w3.astype(jnp.float32)
    deltas = jnp.abs(jnp.linspace(math.log(HY_DECAY_TARGET) / HY_FAST_DECAY, math.log(HY_DECAY_TARGET) / HY_SLOW_DECAY, HY_WIDTH, dtype=jnp.float32))
    decay = jnp.exp(-t[:, None] * deltas[None, :])
    h = h.reshape(n, 2, HY_WIDTH) * decay[:, None, :]
    return h[:, 0], h[:, 1]


def hyena_mix(p, conv_w, conv_b, f_w1, f_b1, f_w2, f_b2, f_w3, f_freq, bias):
    n = p.shape[1]
    u = short_conv(p, conv_w, conv_b)
    v, x1, x2 = jnp.split(u, 3, axis=-1)
    h_fwd, h_bwd = hyena_filters(n, f_w1, f_b1, f_w2, f_b2, f_w3, f_freq)
    k_two = jnp.concatenate([h_fwd, jnp.zeros((1, HY_WIDTH), jnp.float32), h_bwd[:0:-1]], axis=0)
    zin = (x1 * v).astype(jnp.float32)
    y = jnp.fft.irfft(jnp.fft.rfft(zin, n=2 * n, axis=1) * jnp.fft.rfft(k_two, axis=0)[None], n=2 * n, axis=1)[:, :n]
    y = y + zin * bias.astype(jnp.float32)
    return (x2.astype(jnp.float32) * y).astype(p.dtype)


def mla_queries(cq, g_q, w_uq):
    b, n = cq.shape[0], cq.shape[1]
    q = (rmsnorm(cq, g_q) @ w_uq).reshape(b, n, MLA_HEADS, MLA_NOPE + MLA_ROPE)
    return q[..., :MLA_NOPE], q[..., MLA_NOPE:]


def mla_keys(ckv, g_kv, w_ukv):
    b, n = ckv.shape[0], ckv.shape[1]
    kv = (rmsnorm(ckv, g_kv) @ w_ukv).reshape(b, n, MLA_HEADS, MLA_NOPE + MLA_V)
    return kv[..., :MLA_NOPE], kv[..., MLA_NOPE:]


def mla_attend(qn, qr, kn, kr, v):
    s = jnp.einsum('bqhd,bkhd->bhqk', qn, kn) + jnp.einsum('bqhr,bkr->bhqk', qr, kr)
    p = jax.nn.softmax(s.astype(jnp.float32) * MLA_SCALE, axis=-1).astype(v.dtype)
    return jnp.einsum('bhqk,bkhd->bqhd', p, v)


def mla_latent(qn, qr, kn, kr, v):
    b, n = qn.shape[0], qn.shape[1]
    nb = n // MLA_BLOCK
    qn_b = jnp.moveaxis(qn.reshape(b, nb, MLA_BLOCK, MLA_HEADS, MLA_NOPE), 1, 0)
    qr_b = jnp.moveaxis(qr.reshape(b, nb, MLA_BLOCK, MLA_HEADS, MLA_ROPE), 1, 0)
    out = lax.map(lambda qq: mla_attend(qq[0], qq[1], kn, kr, v), (qn_b, qr_b))
    return jnp.moveaxis(out, 0, 1).reshape(b, n, MLA_HEADS * MLA_V)


def dense_attend(q, k, v, scale):
    s = jnp.einsum('bqhd,bkhd->bhqk', q, k).astype(jnp.float32) * scale
    p = jax.nn.softmax(s, axis=-1).astype(v.dtype)
    return jnp.einsum('bhqk,bkhd->bqhd', p, v)


def natten_latent(q, k, v, k_ctx, v_ctx, rpb):
    b, n_tok = q.shape[0], q.shape[1]
    rows = n_tok // GRID_W
    kh = min(NA_KH, rows)
    n_cb = GRID_W // NA_QB
    q_col = np.arange(GRID_W).reshape(n_cb, NA_QB)
    c_start = np.clip(q_col - NA_KW // 2, 0, GRID_W - NA_KW)
    k_col = np.clip(c_start[:, :1], 0, GRID_W - NA_KB) + np.arange(NA_KB)
    col_ok = (k_col[:, None, :] >= c_start[:, :, None]) & (k_col[:, None, :] < c_start[:, :, None] + NA_KW)
    col_idx = np.clip(k_col[:, None, :] - q_col[:, :, None] + NA_KW - 1, 0, 2 * NA_KW - 2)
    rpb_cols = rpb.astype(jnp.float32)[:, :, col_idx]
    qg = q.reshape(b, rows, GRID_W, NA_HEADS, HEAD_DIM)
    kg = k.reshape(b, rows, GRID_W, NA_HEADS, HEAD_DIM)
    vg = v.reshape(b, rows, GRID_W, NA_HEADS, HEAD_DIM)
    n_loc = kh * NA_KB

    def one_row(r):
        r0 = jnp.clip(r - kh // 2, 0, rows - kh)
        k_blk = lax.dynamic_slice_in_dim(kg, r0, kh, axis=1)[:, :, k_col]
        v_blk = lax.dynamic_slice_in_dim(vg, r0, kh, axis=1)[:, :, k_col]
        q_blk = lax.dynamic_index_in_dim(qg, r, axis=1, keepdims=False).reshape(b, n_cb, NA_QB, NA_HEADS, HEAD_DIM)
        s_loc = jnp.einsum('bnqhd,bknchd->bhnqkc', q_blk, k_blk).astype(jnp.float32) * NA_SCALE
        bias = jnp.take(rpb_cols, r0 + jnp.arange(kh) - r + NA_KH - 1, axis=1).transpose(0, 2, 3, 1, 4)
        s_loc = jnp.where(col_ok[:, :, None, :], s_loc + bias, MASK_VALUE)
        s_ctx = jnp.einsum('bnqhd,bchd->bhnqc', q_blk, k_ctx).astype(jnp.float32) * NA_SCALE
        s_all = jnp.concatenate([s_loc.reshape(s_loc.shape[:4] + (n_loc,)), s_ctx], axis=-1)
        p = jax.nn.softmax(s_all, axis=-1).astype(v.dtype)
        p_loc = p[..., :n_loc].reshape(s_loc.shape)
        o = jnp.einsum('bhnqkc,bknchd->bnqhd', p_loc, v_blk) + jnp.einsum('bhnqc,bchd->bnqhd', p[..., n_loc:], v_ctx)
        return o.reshape(b, GRID_W, NA_HEADS, HEAD_DIM)

    out = lax.map(one_row, jnp.arange(rows))
    return jnp.moveaxis(out, 0, 1).reshape(b, n_tok, NA_HEADS * HEAD_DIM)


def swiglu(h, w_gate, w_up, w_down):
    return (jax.nn.silu(h @ w_gate) * (h @ w_up)) @ w_down


def trunk_layer(x, ctx, c, c_ctx, w_ada, b_ada, g_attn_pre, g_attn_post, g_ffn_pre, g_ffn_post, w_in,
                hy_conv_w, hy_conv_b, hy_f_w1, hy_f_b1, hy_f_w2, hy_f_b2, hy_f_w3, hy_f_freq, hy_bias,
                mla_g_q, mla_w_uq, mla_g_kv, mla_w_ukv, na_rpb, w_out, w_ffn_gate, w_ffn_up, w_ffn_down, ctx_out):
    b, n = x.shape[0], x.shape[1]
    n_ctx = ctx.shape[1]
    mx = [m[:, None, :] for m in jnp.split(jax.nn.silu(c) @ w_ada + b_ada, 6, axis=-1)]
    mc = [m[None, None, :] for m in jnp.split(jax.nn.silu(c_ctx) @ w_ada + b_ada, 6, axis=-1)]

    hx, cqx, ckvx, krx, nax = split_in(modulate(x, g_attn_pre, mx[0], mx[1]) @ w_in)
    hc, cqc, ckvc, krc, nac = split_in(modulate(ctx, g_attn_pre, mc[0], mc[1]) @ w_in)

    cos, sin = axial_rope(n)
    qn_x, qr_x = mla_queries(cqx, mla_g_q, mla_w_uq)
    qr_x = apply_rope(qr_x, cos[:, None, :], sin[:, None, :])
    kn_x, v_x = mla_keys(ckvx, mla_g_kv, mla_w_ukv)
    kr_x = apply_rope(krx, cos, sin)
    kn_c, v_c = mla_keys(ckvc, mla_g_kv, mla_w_ukv)
    mla_x = mla_latent(qn_x, qr_x, jnp.concatenate([kn_x, kn_c], axis=1), jnp.concatenate([kr_x, krc], axis=1),
                       jnp.concatenate([v_x, v_c], axis=1))

    q_na, k_na, v_na = [t.reshape(b, n, NA_HEADS, HEAD_DIM) for t in jnp.split(nax, 3, axis=-1)]
    qc_na, kc_na, vc_na = [t.reshape(b, n_ctx, NA_HEADS, HEAD_DIM) for t in jnp.split(nac, 3, axis=-1)]
    na_x = natten_latent(q_na, k_na, v_na, kc_na, vc_na, na_rpb)

    hy_args = (hy_conv_w, hy_conv_b, hy_f_w1, hy_f_b1, hy_f_w2, hy_f_b2, hy_f_w3, hy_f_freq, hy_bias)
    hy_x = hyena_mix(hx, *hy_args)

    y = jnp.concatenate([hy_x, mla_x, na_x], axis=-1) @ w_out
    x_new = x + mx[2] * rmsnorm(y, g_attn_post)
    x_new = x_new + mx[5] * rmsnorm(swiglu(modulate(x_new, g_ffn_pre, mx[3], mx[4]), w_ffn_gate, w_ffn_up, w_ffn_down), g_ffn_post)

    if ctx_out:
        qn_c, qr_c = mla_queries(cqc, mla_g_q, mla_w_uq)
        mla_c = mla_attend(qn_c, qr_c, kn_c, krc, v_c).reshape(b, n_ctx, MLA_HEADS * MLA_V)
        na_c = dense_attend(qc_na, kc_na, vc_na, NA_SCALE).reshape(b, n_ctx, NA_HEADS * HEAD_DIM)
        hy_c = hyena_mix(hc, *hy_args)
        yc = jnp.concatenate([hy_c, mla_c, na_c], axis=-1) @ w_out
        ctx = ctx + mc[2] * rmsnorm(yc, g_attn_post)
        ctx = ctx + mc[5] * rmsnorm(swiglu(modulate(ctx, g_ffn_pre, mc[3], mc[4]), w_ffn_gate, w_ffn_up, w_ffn_down), g_ffn_post)
    return x_new, ctx


def setup_inputs(seed: int = 0) -> dict:
    key = jax.random.key(seed)
    ks = iter(jax.random.split(key, 32))
    D = D_MODEL

    def nrm(shape, scale):
        return scale * jax.random.normal(next(ks), shape, jnp.float32)

    def gain(shape):
        return 1.0 + nrm(shape, 0.05)

    return {
        'x': nrm((BATCH, SEQ, D), 1.0),
        'c': nrm((BATCH, D), 1.0),
        'ctx': nrm((BATCH, CTX_LEN, D), 1.0),
        'c_ctx': nrm((D,), 1.0),
        'w_ada': nrm((DEPTH, D, 6 * D), 0.5 * D ** -0.5),
        'b_ada': nrm((DEPTH, 6 * D), 0.01),
        'g_attn_pre': gain((DEPTH, D)),
        'g_attn_post': gain((DEPTH, D)),
        'g_ffn_pre': gain((DEPTH, D)),
        'g_ffn_post': gain((DEPTH, D)),
        'w_in': nrm((DEPTH, D, IN_COLS), D ** -0.5),
        'hy_conv_w': nrm((DEPTH, HY_SHORT, 3 * HY_WIDTH), HY_SHORT ** -0.5),
        'hy_conv_b': nrm((DEPTH, 3 * HY_WIDTH), 0.01),
        'hy_f_w1': nrm((DEPTH, HY_POS_EMB, HY_FILTER_HIDDEN), HY_POS_EMB ** -0.5),
        'hy_f_b1': nrm((DEPTH, HY_FILTER_HIDDEN), 0.1),
        'hy_f_w2': nrm((DEPTH, HY_FILTER_HIDDEN, HY_FILTER_HIDDEN), HY_FILTER_HIDDEN ** -0.5),
        'hy_f_b2': nrm((DEPTH, HY_FILTER_HIDDEN), 0.1),
        'hy_f_w3': nrm((DEPTH, HY_FILTER_HIDDEN, 2 * HY_WIDTH), 0.02),
        'hy_f_freq': gain((DEPTH, HY_FILTER_HIDDEN)),
        'hy_bias': nrm((DEPTH, HY_WIDTH), 0.5),
        'mla_g_q': gain((DEPTH, Q_LORA)),
        'mla_w_uq': nrm((DEPTH, Q_LORA, MLA_HEADS * (MLA_NOPE + MLA_ROPE)), Q_LORA ** -0.5),
        'mla_g_kv': gain((DEPTH, KV_LORA)),
        'mla_w_ukv': nrm((DEPTH, KV_LORA, MLA_HEADS * (MLA_NOPE + MLA_V)), KV_LORA ** -0.5),
        'na_rpb': nrm((DEPTH, NA_HEADS, 2 * NA_KH - 1, 2 * NA_KW - 1), 0.1),
        'w_out': nrm((DEPTH, D, D), D ** -0.5),
        'w_ffn_gate': nrm((DEPTH, D, FFN_HIDDEN), D ** -0.5),
        'w_ffn_up': nrm((DEPTH, D, FFN_HIDDEN), D ** -0.5),
        'w_ffn_down': nrm((DEPTH, FFN_HIDDEN, D), FFN_HIDDEN ** -0.5),
    }


def reference(x, c, ctx, c_ctx, w_ada, b_ada, g_attn_pre, g_attn_post, g_ffn_pre, g_ffn_post, w_in,
              hy_conv_w, hy_conv_b, hy_f_w1, hy_f_b1, hy_f_w2, hy_f_b2, hy_f_w3, hy_f_freq, hy_bias,
              mla_g_q, mla_w_uq, mla_g_kv, mla_w_ukv, na_rpb, w_out, w_ffn_gate, w_ffn_up, w_ffn_down):
    for l in range(DEPTH):
        x, ctx = trunk_layer(x, ctx, c, c_ctx, w_ada[l], b_ada[l], g_attn_pre[l], g_attn_post[l], g_ffn_pre[l], g_ffn_post[l],
                             w_in[l], hy_conv_w[l], hy_conv_b[l], hy_f_w1[l], hy_f_b1[l], hy_f_w2[l], hy_f_b2[l], hy_f_w3[l],
                             hy_f_freq[l], hy_bias[l], mla_g_q[l], mla_w_uq[l], mla_g_kv[l], mla_w_ukv[l], na_rpb[l],
                             w_out[l], w_ffn_gate[l], w_ffn_up[l], w_ffn_down[l], l < DEPTH - 1)
    return x
```

```python
import math
import contextlib
import numpy as np
import ml_dtypes
import concourse.bass as bass
import concourse.mybir as mybir
from concourse.bass_utils import run_bass_kernel_spmd

F32 = mybir.dt.float32
BF16 = mybir.dt.bfloat16
AF = mybir.ActivationFunctionType
ALU = mybir.AluOpType

N_DMA_SEMS = 40
N_CORES = 4

D = 2048
NX = 2048
NCTX = 256
NTOK = NX + NCTX
DEPTH = 2
GRID_W = 64
HY_W = 512
Q_LORA = 768
KV_LORA = 512
MLA_H = 8
NA_H = 4
FFN = 5632
IN_COLS = 4416
MLA_SCALE = (128 + 64) ** -0.5
NA_SCALE = 128 ** -0.5
EPS = 1e-6
C_HX, C_CQ, C_CKV, C_KR, C_QNA, C_KNA, C_VNA = 0, 1536, 2304, 2816, 2880, 3392, 3904
MASKV = -30000.0
TWO_PI = 2.0 * math.pi


class V:
    __slots__ = ("t", "ap")

    def __init__(self, t, ap):
        self.t = t
        self.ap = ap

    def __getitem__(self, idx):
        return V(self.t, self.ap[idx])


class T:
    def __init__(self, ap, name=""):
        self.ap = ap
        self.name = name
        self.w = None
        self.r = []
        self.excl = False

    def __getitem__(self, idx):
        return V(self, self.ap[idx])

    @property
    def v(self):
        return V(self, self.ap)


class Op:
    __slots__ = ("eng", "fn", "deps", "idx", "is_dma", "flag", "semval", "dsem", "dval", "nop")

    def __init__(self, eng, fn, deps, idx, is_dma):
        self.eng = eng
        self.fn = fn
        self.deps = deps
        self.idx = idx
        self.is_dma = is_dma
        self.flag = False
        self.semval = 0
        self.dsem = None
        self.dval = 0
        self.nop = False


ENGS = ["pe", "act", "dve", "pool", "sp"]


class Prog:
    def __init__(self, nc):
        self.nc = nc
        self.ops = {e: [] for e in ENGS}
        self.n_dma = 0
        self.dma_last = [None] * N_DMA_SEMS
        self.dma_cnt = [0] * N_DMA_SEMS
        self.dma_since_bar = []

    def add(self, eng, fn, reads=(), writes=(), is_dma=False):
        deps = set()
        for v in reads:
            t = v.t
            if t.w is not None:
                deps.add(t.w)
            if t.excl:
                deps.update(r for r in t.r if r.eng != eng)
        for v in writes:
            t = v.t
            if t.w is not None:
                deps.add(t.w)
            deps.update(t.r)
        op = Op(eng, fn, deps, len(self.ops[eng]), is_dma)
        if is_dma:
            s = self.n_dma % N_DMA_SEMS
            self.n_dma += 1
            if self.dma_last[s] is not None:
                op.deps.add(self.dma_last[s])
            self.dma_last[s] = op
            self.dma_cnt[s] += 1
            op.dsem = s
            op.dval = 16 * self.dma_cnt[s]
            self.dma_since_bar.append(op)
        for v in reads:
            v.t.r.append(op)
        for v in writes:
            v.t.w = op
            v.t.r = []
        self.ops[eng].append(op)
        return op

    def wait_ops(self, eng, toks):
        op = self.add(eng, lambda e: None)
        op.nop = True
        op.deps.update(toks)
        return op

    def barrier(self):
        toks = []
        for e in ENGS:
            for op in reversed(self.ops[e]):
                if not op.is_dma and not op.nop:
                    toks.append(op)
                    break
        toks.extend(self.dma_since_bar)
        self.dma_since_bar = []
        for e in ENGS:
            self.wait_ops(e, toks)

    def mm(self, out, lhsT, rhs, start=True, stop=True):
        return self.add("pe", lambda e: e.matmul(out.ap, lhsT.ap, rhs.ap, start=start, stop=stop),
                        reads=[lhsT, rhs], writes=[out])

    def transpose(self, out, in_, ident):
        return self.add("pe", lambda e: e.transpose(out.ap, in_.ap, ident.ap),
                        reads=[in_, ident], writes=[out])

    def act(self, out, in_, func, bias=None, scale=None, accum=None):
        reads = [in_]
        writes = [out]
        kw = {}
        if bias is not None:
            if isinstance(bias, V):
                reads.append(bias)
                kw["bias"] = bias.ap
            else:
                kw["bias"] = float(bias)
        if scale is not None:
            if isinstance(scale, V):
                reads.append(scale)
                kw["scale"] = scale.ap
            else:
                kw["scale"] = float(scale)
        if accum is not None:
            writes.append(accum)
            kw["accum_out"] = accum.ap
        return self.add("act", lambda e: e.activation(out.ap, in_.ap, func, **kw), reads=reads, writes=writes)

    def tt(self, eng, out, in0, in1, op):
        return self.add(eng, lambda e: e.tensor_tensor(out.ap, in0.ap, in1.ap, op), reads=[in0, in1], writes=[out])

    def ts(self, eng, out, in0, s1, op0, s2=None, op1=None):
        reads = [in0]
        a1 = s1.ap if isinstance(s1, V) else float(s1)
        if isinstance(s1, V):
            reads.append(s1)
        a2 = None
        if s2 is not None:
            a2 = s2.ap if isinstance(s2, V) else float(s2)
            if isinstance(s2, V):
                reads.append(s2)
        kw = {}
        if op1 is not None:
            kw["op1"] = op1
        return self.add(eng, lambda e: e.tensor_scalar(out.ap, in0.ap, a1, a2, op0, **kw), reads=reads, writes=[out])

    def stt(self, eng, out, in0, scalar, in1, op0, op1):
        reads = [in0, in1]
        a = scalar.ap if isinstance(scalar, V) else float(scalar)
        if isinstance(scalar, V):
            reads.append(scalar)
        return self.add(eng, lambda e: e.scalar_tensor_tensor(out.ap, in0.ap, a, in1.ap, op0, op1),
                        reads=reads, writes=[out])

    def copy(self, eng, out, in_):
        if eng == "act":
            return self.add("act", lambda e: e.copy(out.ap, in_.ap), reads=[in_], writes=[out])
        return self.add(eng, lambda e: e.tensor_copy(out.ap, in_.ap), reads=[in_], writes=[out])

    def memset(self, eng, out, val):
        return self.add(eng, lambda e: e.memset(out.ap, val), writes=[out])

    def recip(self, out, in_):
        return self.add("dve", lambda e: e.reciprocal(out.ap, in_.ap), reads=[in_], writes=[out])

    def dma(self, q, out, in_, **kw):
        return self.add(q, lambda e: e.dma_start(out.ap, in_.ap, **kw), reads=[in_], writes=[out], is_dma=True)

    def emit(self):
        nc = self.nc
        ops = self.ops
        for e in ENGS:
            for op in ops[e]:
                for d in op.deps:
                    if d.is_dma:
                        continue
                    if d.eng == e and e == "pe" and not op.is_dma:
                        continue
                    d.flag = True
        for e in ENGS:
            c = 0
            for op in ops[e]:
                if op.flag and not op.is_dma:
                    c += 1
                op.semval = c
        with contextlib.ExitStack() as es:
            esem = {e: es.enter_context(nc.semaphore("s_" + e)) for e in ENGS if e != "sp"}
            dsem = [es.enter_context(nc.semaphore("d%d" % i)) for i in range(N_DMA_SEMS)]
            block = es.enter_context(nc.Block())

            def run(e, eng):
                seen = {}
                for op in ops[e]:
                    waits = {}
                    for d in op.deps:
                        if d.is_dma:
                            key = ("d", d.dsem)
                            val = d.dval
                            sem = dsem[d.dsem]
                        else:
                            if d.eng == e and e == "pe" and not op.is_dma:
                                continue
                            if d.eng == e and d.idx >= op.idx:
                                continue
                            key = ("e", d.eng)
                            val = d.semval
                            sem = esem[d.eng]
                        if seen.get(key, 0) >= val:
                            continue
                        if key not in waits or waits[key][1] < val:
                            waits[key] = (sem, val)
                    for key, (sem, val) in waits.items():
                        eng.wait_ge(sem, val)
                        seen[key] = val
                    ins = op.fn(eng)
                    if ins is None:
                        continue
                    if op.is_dma:
                        ins.then_inc(dsem[op.dsem], 16)
                    elif op.flag:
                        ins.then_inc(esem[e], 1)

            @block.tensor
            def _(eng):
                run("pe", eng)

            @block.scalar
            def _(eng):
                run("act", eng)

            @block.vector
            def _(eng):
                run("dve", eng)

            @block.gpsimd
            def _(eng):
                run("pool", eng)

            @block.sync
            def _(eng):
                run("sp", eng)


class Arena:
    def __init__(self, ap, nwords):
        self.ap = ap
        self.n = nwords
        self.off = 0
        self.live = []

    def reset(self, mark=0):
        self.off = mark

    def alloc(self, shape, dt, name=""):
        nel = int(np.prod(shape))
        nbytes = nel * (4 if dt == F32 else 2)
        nw = (nbytes + 31) // 32 * 8
        assert self.off + nw <= self.n, ("SBUF arena overflow", name, self.off, nw, self.n)
        s0, e0 = self.off, self.off + nw
        a = self.ap[:, s0:e0]
        self.off += nw
        if dt != F32:
            a = a.bitcast(dt)
        a = a[:, 0:nel]
        if len(shape) > 1:
            names = ["d%d" % i for i in range(len(shape))]
            kw = {n: int(s) for n, s in zip(names[:-1], shape[:-1])}
            a = a.rearrange("p (%s) -> p %s" % (" ".join(names), " ".join(names)), **kw)
        t = T(a, name)
        keep = []
        for (s1, e1, t1) in self.live:
            if s1 < e0 and s0 < e1:
                t.r.extend(t1.r)
                if t1.w is not None:
                    t.r.append(t1.w)
                if s0 <= s1 and e1 <= e0:
                    continue
            keep.append((s1, e1, t1))
        keep.append((s0, e0, t))
        self.live = keep
        return t


def _bf(a):
    return np.ascontiguousarray(a.astype(ml_dtypes.bfloat16))


def _dft_tables(n):
    N = 2 * n
    nt = n // 128
    idx = np.arange(n, dtype=np.int64)
    prod = (idx[:, None] * idx[None, :]) % N
    ang = prod.astype(np.float64) * (2.0 * np.pi / N)
    Cm = np.cos(ang)
    Sm = np.sin(ang)
    F = np.stack([Cm, Sm], 0).reshape(2, nt, 128, nt, 128)
    F = F.transpose(3, 2, 0, 1, 4)
    I = np.stack([Cm, Sm], 0).reshape(2, nt, 128, n).transpose(1, 2, 0, 3)
    wgt = np.full((128, nt), 2.0 / N, np.float32)
    wgt[0, 0] = 1.0 / N
    return _bf(F), _bf(I), wgt


def _filter_tables(n):
    pos = np.arange(n, dtype=np.float32)
    t = np.linspace(0.0, 1.0, n, dtype=np.float32)
    bands = np.linspace(1e-4, 15, 16, dtype=np.float32)
    ang = np.float32(2.0 * math.pi / n) * pos[:, None] * bands[None, :]
    z = np.concatenate([t[:, None], np.cos(ang), -np.sin(ang)], axis=-1).astype(np.float32)
    negt = (-t).reshape(n // 128, 128).T.copy()
    return np.ascontiguousarray(z.T), np.ascontiguousarray(negt.astype(np.float32))


def _const_tables():
    c = {}
    c["dftF"], c["dftI"], c["wgt"] = _dft_tables(NX)
    c["dftFc"], c["dftIc"], c["wgtc"] = _dft_tables(NCTX)
    c["ztab"], c["negt"] = _filter_tables(NX)
    c["ztabc"], c["negtc"] = _filter_tables(NCTX)
    deltas = np.abs(np.linspace(math.log(1e-2) / 0.3, math.log(1e-2) / 1.5, HY_W, dtype=np.float32))
    c["deltab"] = np.ascontiguousarray(np.broadcast_to(deltas[None, :], (128, HY_W)).astype(np.float32))
    alt = np.where(np.arange(128) % 2 == 0, 1.0, -1.0).astype(np.float32)
    c["altcol"] = _bf(alt.reshape(128, 1))
    c["altrow"] = _bf(np.where(np.arange(NX) % 2 == 0, 1.0, -1.0).astype(np.float32).reshape(1, NX))
    tt_ = np.arange(NX)
    row = (tt_ // GRID_W).astype(np.float32)
    col = (tt_ % GRID_W).astype(np.float32)
    inv = (10000.0 ** (-np.arange(16, dtype=np.float32) / 16)).astype(np.float32)
    ang = np.concatenate([row[:, None] * inv, col[:, None] * inv], axis=-1)
    cs, sn = np.cos(ang).T, np.sin(ang).T
    cos2 = np.concatenate([cs, cs], 0).astype(np.float32)
    sin2 = np.concatenate([-sn, sn], 0).astype(np.float32)
    c["rope"] = np.ascontiguousarray(np.stack([cos2, sin2, cos2 * MLA_SCALE, sin2 * MLA_SCALE], 1).astype(np.float32))
    c["ident"] = _bf(np.eye(128, dtype=np.float32))
    c["namask"] = _na_index()[3]
    return c


_NA_CLASSES = [0, 1, 2, 14, 15]


def _na_cls(i):
    if i <= 1:
        return i
    if i <= 13:
        return 2
    return i - 11


def _na_j0(i):
    return min(max(i - 2, 0), 11)


_NA_IDX = None


def _na_index():
    global _NA_IDX
    if _NA_IDX is not None:
        return _NA_IDX
    ri = np.zeros((128, 5, 5, 128), np.int64)
    ci = np.zeros((128, 5, 5, 128), np.int64)
    ok = np.zeros((128, 5, 5, 128), bool)
    for cls, i in enumerate(_NA_CLASSES):
        j0 = _na_j0(i)
        for qr in range(2):
            r = 2 * i + qr
            r0 = min(max(r - 4, 0), 24)
            for c in range(5):
                for kr2 in range(2):
                    R = 2 * (j0 + c) + kr2
                    if not (r0 <= R <= r0 + 7):
                        continue
                    for qc in range(64):
                        cs = min(max(qc - 8, 0), 48)
                        kc = np.arange(cs, cs + 16)
                        ri[kr2 * 64 + kc, cls, c, qr * 64 + qc] = R - r + 7
                        ci[kr2 * 64 + kc, cls, c, qr * 64 + qc] = kc - qc + 15
                        ok[kr2 * 64 + kc, cls, c, qr * 64 + qc] = True
    mask = np.where(ok, 0.0, MASKV).astype(np.float32).reshape(128, 3200)
    _NA_IDX = (ri, ci, ok, np.ascontiguousarray(mask))
    return _NA_IDX


def _na_gather(rpb):
    ri, ci, ok, _ = _na_index()
    g = rpb[:, :, ri, ci] * ok[None, None].astype(np.float32)
    return np.ascontiguousarray(g.reshape(rpb.shape[0], 4, 128, 3200).astype(np.float32))


def build_program(n_layers=DEPTH, dbg=(), phases=None):
    nc = bass.Bass("TRN2", target_bir_lowering=False)
    P = Prog(nc)
    dbg = set(dbg)

    def din(name, shape, dt=F32):
        h = nc.dram_tensor(name, list(shape), dt, kind="ExternalInput")
        return T(h.ap(), name)

    def dscr(name, shape, dt=F32):
        kind = "ExternalOutput" if name in dbg else "Internal"
        h = nc.dram_tensor(name, list(shape), dt, kind=kind)
        return T(h.ap(), name)

    I = {}
    I["xall"] = din("xall", [NTOK, D])
    I["cc"] = din("cc", [2, D])
    for nm, shp in [("w_ada", [DEPTH, D, 6 * D]), ("b_ada", [DEPTH, 6 * D]), ("g_attn_pre", [DEPTH, D]),
                    ("g_attn_post", [DEPTH, D]), ("g_ffn_pre", [DEPTH, D]), ("g_ffn_post", [DEPTH, D]),
                    ("w_in", [DEPTH, D, IN_COLS]), ("hy_conv_w", [DEPTH, 3, 1536]), ("hy_conv_b", [DEPTH, 1536]),
                    ("hy_f_w1", [DEPTH, 33, 64]), ("hy_f_b1", [DEPTH, 64]), ("hy_f_w2", [DEPTH, 64, 64]),
                    ("hy_f_b2", [DEPTH, 64]), ("hy_f_w3", [DEPTH, 64, 1024]), ("hy_f_freq", [DEPTH, 64]),
                    ("hy_bias", [DEPTH, 512]), ("mla_g_q", [DEPTH, Q_LORA]), ("mla_w_uq", [DEPTH, Q_LORA, 1536]),
                    ("mla_g_kv", [DEPTH, KV_LORA]), ("mla_w_ukv", [DEPTH, KV_LORA, 2048]),
                    ("w_out", [DEPTH, D, D]), ("w_ffn_gate", [DEPTH, D, FFN]), ("w_ffn_up", [DEPTH, D, FFN]),
                    ("w_ffn_down", [DEPTH, FFN, D]),
                    ("rpbg", [DEPTH, 4, 128, 3200]), ("namask", [128, 3200]),
                    ("ztab", [33, NX]), ("ztabc", [33, NCTX]), ("negt", [128, 16]), ("negtc", [128, 2]),
                    ("deltab", [128, HY_W]), ("wgt", [128, 16]), ("wgtc", [128, 2]), ("rope", [64, 4, NX])]:
        I[nm] = din(nm, shp)
    for nm, shp in [("dftF", [16, 128, 2, 16, 128]), ("dftI", [16, 128, 2, NX]), ("dftFc", [2, 128, 2, 2, 128]),
                    ("dftIc", [2, 128, 2, NCTX]), ("altcol", [128, 1]), ("altrow", [1, NX]), ("ident", [128, 128])]:
        I[nm] = din(nm, shp, BF16)
    OUT = T(nc.dram_tensor("out", [NX, D], F32, kind="ExternalOutput").ap(), "out")

    S = {}
    S["dv"] = dscr("S_dv", [DEPTH, 2, 6, D])
    S["modT"] = dscr("S_modT", [D, NTOK], BF16)
    S["hxT"] = dscr("S_hxT", [1536, NTOK])
    S["cqT"] = dscr("S_cqT", [Q_LORA, NTOK], BF16)
    S["ckvT"] = dscr("S_ckvT", [KV_LORA, NTOK], BF16)
    S["krT"] = dscr("S_krT", [64, NTOK], BF16)
    S["qnaT"] = dscr("S_qnaT", [512, NTOK], BF16)
    S["knaT"] = dscr("S_knaT", [512, NTOK], BF16)
    S["vna"] = dscr("S_vna", [NTOK, 512], BF16)
    S["catT"] = dscr("S_catT", [D, NTOK], BF16)
    S["X1"] = dscr("S_X1", [NTOK, D])
    S["X2"] = dscr("S_X2", [NTOK, D])

    es = contextlib.ExitStack()
    AR_WORDS = 52000
    arena_h = es.enter_context(nc.sbuf_tensor("arena", [128, AR_WORDS], F32))
    psum_h = es.enter_context(nc.psum_tensor("psum", [128, 4096], F32))
    A = Arena(arena_h, AR_WORDS)
    banks = [T(psum_h[:, i * 512:(i + 1) * 512], "bank%d" % i) for i in range(8)]
    for b_ in banks:
        b_.excl = True

    def bk(i, dt=F32):
        t = banks[i]
        return V(t, t.ap if dt == F32 else t.ap.bitcast(dt))

    def bcast_rows(t, row_off, n, parts=128):
        return V(t, bass.AP(t.ap.tensor, row_off, [[0, parts], [1, n]]))

    epsc_box = [None]

    def new_phase(base=0):
        A.reset(base)
        e_ = A.alloc([1], F32, "epsc")
        P.memset("pool", e_[:, :], EPS)
        epsc_box[0] = e_

    class _Eps:
        def __getitem__(self, idx):
            return epsc_box[0][idx]

    epsc = _Eps()

    def want(ph):
        return phases is None or ph in phases

    ADA_BASE = 52000 - 20640
    ada_end = [0]

    def adaln_gen(l, base, bank=6, nrow=2):
        save = A.off
        A.off = base
        ccx = A.alloc([16], F32, "ccx")
        ccc = A.alloc([16], F32, "ccc")
        scT = A.alloc([16, 2], BF16, "scT")
        wb = [A.alloc([16, 512], BF16, "wada%d" % i) for i in range(2)]
        mrow = [A.alloc([D], F32, "mrow%d" % i) for i in range(nrow)]
        grow = [A.alloc([D], F32, "grow%d" % i) for i in range(nrow)]
        drow = [A.alloc([D], F32, "drow%d" % i) for i in range(nrow)]
        ada_end[0] = A.off
        A.off = save
        P.dma("sp", ccx[:, :], V(I["cc"], I["cc"].ap[0, :].rearrange("(p j) -> p j", j=16)))
        P.dma("sp", ccc[:, :], V(I["cc"], I["cc"].ap[1, :].rearrange("(p j) -> p j", j=16)))
        P.act(scT[:, :, 0], ccx[:, :], AF.Silu)
        P.act(scT[:, :, 1], ccc[:, :], AF.Silu)
        wsrc = I["w_ada"].ap[l].rearrange("(p j) n -> p j n", j=16)
        plan = {0: (1, None, False), 1: (0, "g_attn_pre", True), 2: (2, "g_attn_post", False),
                3: (4, None, False), 4: (3, "g_ffn_pre", True), 5: (5, "g_ffn_post", False)}
        ps = bk(bank)
        for mi in range(6):
            mr, gr, dr = mrow[mi % nrow], grow[mi % nrow], drow[mi % nrow]
            slot, gname, addone = plan[mi]
            P.dma("sp", mr[0:2, :], bcast_rows(I["b_ada"], l * 6 * D + mi * D, D, 2))
            if gname is not None:
                P.dma("sp", gr[0:2, :], bcast_rows(I[gname], l * D, D, 2))
            for q in range(4):
                nb = mi * 4 + q
                w = wb[nb % 2]
                P.dma("pool", w[:, :, :], V(I["w_ada"], wsrc[:, :, nb * 512:(nb + 1) * 512]))
                for j in range(16):
                    P.mm(ps[0:2, :], scT[:, j, :], w[:, j, :], start=(j == 0), stop=(j == 15))
                P.tt("dve", mr[0:2, q * 512:(q + 1) * 512], ps[0:2, :], mr[0:2, q * 512:(q + 1) * 512], ALU.add)
                yield
            if gname is None:
                src_row = mr
            else:
                if addone:
                    P.stt("dve", dr[0:2, :], mr[0:2, :], 1.0, gr[0:2, :], ALU.add, ALU.mult)
                else:
                    P.tt("dve", dr[0:2, :], mr[0:2, :], gr[0:2, :], ALU.mult)
                src_row = dr
            P.dma("sp", V(S["dv"], S["dv"].ap[l, :, slot, :]), src_row[0:2, :])
            yield

    def phase_adaln(l):
        new_phase()
        for _ in adaln_gen(l, A.off):
            pass

    def load_dv(l, kind, idx, dst):
        off = ((l * 2 + kind) * 6 + idx) * D
        P.dma("sp", dst, bcast_rows(S["dv"], off, D, 128))

    def rstd_from_ss(ss, rstd, n):
        P.act(rstd, ss, AF.Sqrt, scale=1.0 / n, bias=epsc[:, :])
        P.recip(rstd, rstd)

    def load_cols(dst, src_rows, r, npart, ident_bf, bank=6):
        rowbuf = A.alloc([128], F32, "rowbuf")
        identf = A.alloc([128], F32, "identf")
        P.copy("dve", identf[:, :], ident_bf[:, :])
        P.dma("sp", rowbuf[0:r, 0:npart], src_rows)
        ps = bk(bank)
        P.mm(ps[0:npart, 0:r], rowbuf[0:r, 0:npart], identf[0:r, 0:r])
        P.copy("dve", dst, ps[0:npart, 0:r])

    def phase_norm(l, src, a_idx, b_idx, tiles, direct=None, base=0):
        new_phase(base)
        ident = A.alloc([128], BF16, "ident")
        P.dma("sp", ident[:, :], I["ident"][:, :])
        Ab = [A.alloc([D], F32, "Ab%d" % k) for k in range(2)]
        Bb = [A.alloc([D], F32, "Bb%d" % k) for k in range(2)]
        for k in range(2):
            load_dv(l, k, a_idx, Ab[k][:, :])
            load_dv(l, k, b_idx, Bb[k][:, :])
        xt = [A.alloc([D], F32, "xt%d" % i) for i in range(3)]
        xm = [A.alloc([D], F32, "xm%d" % i) for i in range(2)]
        xb = [A.alloc([D], BF16, "xb%d" % i) for i in range(2)]
        junk = A.alloc([D], BF16, "junk")
        ss = [A.alloc([1], F32, "ss%d" % i) for i in range(3)]
        rs = [A.alloc([1], F32, "rs%d" % i) for i in range(3)]
        stage = [A.alloc([16, 512], BF16, "stage%d" % i) for i in range(2)]
        dstT = S["modT"].ap.rearrange("(kc p) t -> p kc t", p=128)
        groups = [tiles[i:i + 4] for i in range(0, len(tiles), 4)]
        seq = []
        for gi, grp in enumerate(groups):
            for ti, tile in enumerate(grp):
                seq.append((gi, ti, tile, ti == len(grp) - 1, grp))

        def s1(n):
            (gi, ti, tile, lastg, grp) = seq[n]
            x = xt[n % 3]
            P.dma("sp", x[:, :], src[tile * 128:(tile + 1) * 128, :])
            P.memset("pool", ss[n % 3][:, :], 0.0)
            P.act(junk[:, :], x[:, :], AF.Square, accum=ss[n % 3][:, :])
            P.act(rs[n % 3][:, :], ss[n % 3][:, :], AF.Sqrt, scale=1.0 / D, bias=epsc[:, :])

        def s2(n):
            (gi, ti, tile, lastg, grp) = seq[n]
            st = stage[gi % 2]
            k = 0 if tile < 16 else 1
            x = xt[n % 3]
            b = xb[n % 2]
            xm_ = xm[n % 2]
            P.recip(rs[n % 3][:, :], rs[n % 3][:, :])
            P.stt("dve", xm_[:, :], x[:, :], rs[n % 3][:, :], Ab[k][:, :], ALU.mult, ALU.mult)
            P.tt("dve", b[:, 0:1024], xm_[:, 0:1024], Bb[k][:, 0:1024], ALU.add)
            P.tt("pool", b[:, 1024:2048], xm_[:, 1024:2048], Bb[k][:, 1024:2048], ALU.add)
            for g in range(4):
                pb = bk((n * 4 + g) % 4 + 4, BF16)
                for c in range(4):
                    kc = 4 * g + c
                    P.transpose(pb[:, c * 128:(c + 1) * 128], b[:, kc * 128:(kc + 1) * 128], ident[:, :])
                src_ps = V(pb.t, pb.ap[:, 0:512].rearrange("p (c t) -> p c t", c=4))
                if direct is not None:
                    P.copy("act", direct[:, 4 * g:4 * g + 4, tile * 128:(tile + 1) * 128], src_ps)
                else:
                    P.copy("act", st[:, 4 * g:4 * g + 4, ti * 128:(ti + 1) * 128], src_ps)
            if lastg and direct is None:
                t0 = grp[0] * 128
                nt = len(grp) * 128
                P.dma("pool", V(S["modT"], dstT[:, :, t0:t0 + nt]), st[:, :, 0:nt])

        assert direct is None or A.off <= 52000 - 18432 - 64, A.off
        s1(0)
        if len(seq) > 1:
            s1(1)
        for n in range(len(seq)):
            if n + 2 < len(seq):
                s1(n + 2)
            s2(n)

    def phase_inproj(l, mT, base):
        new_phase(base)
        wb = [A.alloc([16, 512], BF16, "win%d" % i) for i in range(2)]
        wsrc = I["w_in"].ap[l].rearrange("(kc p) n -> p kc n", p=128)
        ones = A.alloc([128], BF16, "ones")
        P.memset("dve", ones[:, :], 1.0)
        identb = A.alloc([128], BF16, "identb")
        P.dma("sp", identb[:, :], I["ident"][:, :])
        st32 = [A.alloc([NTOK], F32, "st32_%d" % i) for i in range(2)]
        raw = A.alloc([6, NTOK], BF16, "raw")
        sq = [A.alloc([512], BF16, "sq%d" % i) for i in range(2)]
        rstd = A.alloc([NTOK], F32, "rstdb")
        gcols = {"mla_g_q": A.alloc([8], F32, "gcolq"), "mla_g_kv": A.alloc([8], F32, "gcolkv")}
        for gname_, nch_ in (("mla_g_q", 6), ("mla_g_kv", 4)):
            load_cols(gcols[gname_][:, 0:nch_], V(I[gname_], I[gname_].ap[l].rearrange("(c p) -> c p", p=128)), nch_, 128, identb)
        rope = A.alloc([2, NX], F32, "ropek")
        P.dma("sp", rope[0:64, :, :], I["rope"][:, 0:2, :])
        stb = [A.alloc([NTOK], BF16, "stb%d" % i) for i in range(2)]
        vst = [A.alloc([512], BF16, "vst%d" % i) for i in range(2)]
        tmp = A.alloc([512], F32, "tmpc")
        tblocks = [(i * 512, 512) for i in range(4)] + [(2048, 256)]
        wcnt = [0]
        pcnt = [0]
        assert A.off + 1024 <= 52000 - 18432 - 64, A.off

        def load_w(c0, ncols, dst_c0=0, w=None):
            if w is None:
                w = wb[wcnt[0] % 2]
                wcnt[0] += 1
            P.dma("pool", w[:, :, dst_c0:dst_c0 + ncols], V(I["w_in"], wsrc[:, :, c0:c0 + ncols]))
            return w

        def proj_fm(w, wc0, m, tb0, tbn):
            ps = bk(pcnt[0] % 2)
            pcnt[0] += 1
            for kc in range(16):
                P.mm(ps[0:m, 0:tbn], w[:, kc, wc0:wc0 + m], mT[:, kc, tb0:tb0 + tbn], start=(kc == 0), stop=(kc == 15))
            return ps[0:m, 0:tbn]

        n = 0
        for g in (range(3) if want("ip_hx") else []):
            w = load_w(C_HX + g * 512, 512)
            for cc in range(4):
                st = st32[n % 2]
                for (tb0, tbn) in tblocks:
                    ps = proj_fm(w, cc * 128, 128, tb0, tbn)
                    P.copy("act" if (tb0 // 512) % 2 == 0 else "dve", st[:, tb0:tb0 + tbn], ps)
                r0 = (g * 4 + cc) * 128
                P.dma("sp", S["hxT"][r0:r0 + 128, :], st[:, :])
                n += 1

        def latent(c0, nch, gname, dst, nfeat):
            gcol = gcols[gname]
            ws = []
            for g in range((nch + 3) // 4):
                ncols = min(512, nch * 128 - g * 512)
                ws.append(load_w(c0 + g * 512, ncols))
            for (tb0, tbn) in tblocks:
                acc = bk(2 + (tb0 // 512) % 2)
                pend = None
                for ch in range(nch):
                    ps = proj_fm(ws[ch // 4], (ch % 4) * 128, 128, tb0, tbn)
                    s = sq[ch % 2]
                    P.copy("dve", raw[:, ch, tb0:tb0 + tbn], ps)
                    P.act(s[:, 0:tbn], raw[:, ch, tb0:tb0 + tbn], AF.Square)
                    if pend is not None:
                        P.mm(acc[:, 0:tbn], ones[:, :], pend[0][:, 0:tbn], start=(pend[1] == 0), stop=False)
                    pend = (s, ch)
                P.mm(acc[:, 0:tbn], ones[:, :], pend[0][:, 0:tbn], start=(pend[1] == 0), stop=True)
                P.ts("dve", rstd[:, tb0:tb0 + tbn], acc[:, 0:tbn], 1.0 / nfeat, ALU.mult, EPS, ALU.add)
                P.act(rstd[:, tb0:tb0 + tbn], rstd[:, tb0:tb0 + tbn], AF.Sqrt)
                P.recip(rstd[:, tb0:tb0 + tbn], rstd[:, tb0:tb0 + tbn])
            for ch in range(nch):
                sb_ = stb[ch % 2]
                P.stt("dve", sb_[:, :], raw[:, ch, :], gcol[:, ch:ch + 1], rstd[:, :], ALU.mult, ALU.mult)
                P.dma("sp", dst[ch * 128:(ch + 1) * 128, :], sb_[:, :])

        if want("ip_lat"):
            latent(C_CQ, 6, "mla_g_q", S["cqT"], Q_LORA)
            latent(C_CKV, 4, "mla_g_kv", S["ckvT"], KV_LORA)
        if not want("ip_rest"):
            return

        w = wb[wcnt[0] % 2]
        wcnt[0] += 1
        load_w(C_KR, 64, 0, w)
        load_w(C_KR + 32, 32, 64, w)
        load_w(C_KR, 32, 96, w)
        sb_ = stb[0]
        for (tb0, tbn) in tblocks:
            pa = proj_fm(w, 0, 64, tb0, tbn)
            if tb0 < NX:
                pbb = proj_fm(w, 64, 64, tb0, tbn)
                P.tt("dve", tmp[0:64, 0:tbn], pa, rope[0:64, 0, tb0:tb0 + tbn], ALU.mult)
                P.tt("dve", st32[0][0:64, 0:tbn], pbb, rope[0:64, 1, tb0:tb0 + tbn], ALU.mult)
                P.tt("dve", sb_[0:64, tb0:tb0 + tbn], tmp[0:64, 0:tbn], st32[0][0:64, 0:tbn], ALU.add)
            else:
                P.copy("dve", sb_[0:64, tb0:tb0 + tbn], pa)
        P.dma("sp", S["krT"][:, :], sb_[0:64, :])

        n = 0
        for (c0, dst, scale) in [(C_QNA, S["qnaT"], NA_SCALE), (C_KNA, S["knaT"], 1.0)]:
            w = load_w(c0, 512)
            for cc in range(4):
                sb_ = stb[n % 2]
                n += 1
                for (tb0, tbn) in tblocks:
                    ps = proj_fm(w, cc * 128, 128, tb0, tbn)
                    P.act(sb_[:, tb0:tb0 + tbn], ps, AF.Identity, scale=scale)
                P.dma("sp", dst[cc * 128:(cc + 1) * 128, :], sb_[:, :])

        w = load_w(C_VNA, 512)
        for tile in range(NTOK // 128):
            ps = bk(tile % 2)
            for kc in range(16):
                P.mm(ps[:, :], mT[:, kc, tile * 128:(tile + 1) * 128], w[:, kc, :], start=(kc == 0), stop=(kc == 15))
            vs = vst[tile % 2]
            P.copy("act" if tile % 2 == 0 else "dve", vs[:, :], ps[:, :])
            P.dma("sp", S["vna"][tile * 128:(tile + 1) * 128, :], vs[:, :])

    def sin_wrapped(dst, arg, tmpv):
        for _ in range(2):
            P.ts("dve", tmpv, arg, math.pi, ALU.is_gt, -TWO_PI, ALU.mult)
            P.tt("dve", arg, arg, tmpv, ALU.add)
            P.ts("dve", tmpv, arg, -math.pi, ALU.is_lt, TWO_PI, ALU.mult)
            P.tt("dve", arg, arg, tmpv, ALU.add)
        P.act(dst, arg, AF.Sin)

    def phase_hyena(l, n, tok0, ctxmode):
        new_phase()
        nt = n // 128
        NN = 2 * n
        zt_in = I["ztabc"] if ctxmode else I["ztab"]
        negt_in = I["negtc"] if ctxmode else I["negt"]
        wgt_in = I["wgtc"] if ctxmode else I["wgt"]
        dF = I["dftFc"] if ctxmode else I["dftF"]
        dI = I["dftIc"] if ctxmode else I["dftI"]
        nb_cols = min(512, n)
        nblk = n // nb_cols

        ident = A.alloc([128], BF16, "ident")
        P.dma("sp", ident[:, :], I["ident"][:, :])
        G = A.alloc([nt, 512], BF16, "G")
        Dd = A.alloc([nt, 512], BF16, "Dd")
        zin = A.alloc([nt, 512], BF16, "zin")
        Y = A.alloc([nt, 2, 512], BF16, "Y")
        x2u = A.alloc([4, n], F32, "x2u")
        altc = A.alloc([1], BF16, "altc")
        altr = A.alloc([n], BF16, "altr")
        yny = A.alloc([512], BF16, "yny")
        wgt = A.alloc([nt], F32, "wgt")
        negt = A.alloc([nt], F32, "negt")
        P.dma("sp", altc[:, :], I["altcol"][:, :])
        P.dma("sp", altr[0:1, :], I["altrow"][:, 0:n])
        P.dma("sp", wgt[:, :], wgt_in[:, :])
        P.dma("sp", negt[:, :], negt_in[:, :])
        mark = A.off

        zt = A.alloc([n], F32, "zt")
        h1 = A.alloc([n], F32, "h1")
        h2 = A.alloc([n], F32, "h2")
        w1 = A.alloc([64], F32, "w1")
        w2 = A.alloc([64], F32, "w2")
        w3 = A.alloc([1024], F32, "w3")
        vec = A.alloc([8], F32, "vec")
        arg = A.alloc([512], F32, "arg")
        tmpv = A.alloc([512], F32, "tmpv")
        delt = A.alloc([512], F32, "delt")
        dec = A.alloc([512], F32, "dec")
        hf = A.alloc([512], F32, "hf")
        hb = A.alloc([512], F32, "hb")
        brow = A.alloc([512], F32, "brow")
        P.dma("sp", zt[0:33, :], zt_in[:, :])
        P.dma("sp", w1[0:33, :], V(I["hy_f_w1"], I["hy_f_w1"].ap[l]))
        P.dma("sp", w2[0:64, :], V(I["hy_f_w2"], I["hy_f_w2"].ap[l]))
        P.dma("sp", w3[0:64, :], V(I["hy_f_w3"], I["hy_f_w3"].ap[l]))
        for i, nm in enumerate(["hy_f_freq", "hy_f_b1", "hy_f_b2"]):
            load_cols(vec[0:64, i:i + 1], V(I[nm], I[nm].ap[l:l + 1, :]), 1, 64, ident)
        P.tt("dve", vec[0:64, 3:4], vec[0:64, 0:1], vec[0:64, 1:2], ALU.mult)
        P.tt("dve", vec[0:64, 4:5], vec[0:64, 0:1], vec[0:64, 2:3], ALU.mult)
        P.dma("sp", delt[:, :], I["deltab"][:, :])
        P.dma("sp", brow[0:1, :], V(I["hy_bias"], I["hy_bias"].ap[l:l + 1, :]))
        for (wm, kdim, src, dst, bcol) in [(w1, 33, zt, h1, 3), (w2, 64, h1, h2, 4)]:
            for b in range(nblk):
                ps = bk(b % 2)
                cs = slice(b * nb_cols, (b + 1) * nb_cols)
                P.mm(ps[0:64, 0:nb_cols], wm[0:kdim, 0:64], src[0:kdim, cs])
                P.ts("dve", arg[0:64, 0:nb_cols], ps[0:64, 0:nb_cols], vec[0:64, 0:1], ALU.mult, vec[0:64, bcol:bcol + 1], ALU.add)
                sin_wrapped(dst[0:64, cs], arg[0:64, 0:nb_cols], tmpv[0:64, 0:nb_cols])
        for jt in range(nt):
            pf, pb_ = bk(2), bk(3)
            P.mm(pf[:, :], h2[0:64, jt * 128:(jt + 1) * 128], w3[0:64, 0:512])
            P.mm(pb_[:, :], h2[0:64, jt * 128:(jt + 1) * 128], w3[0:64, 512:1024])
            P.act(dec[:, :], delt[:, :], AF.Exp, scale=negt[:, jt:jt + 1])
            P.tt("dve", hf[:, :], pf[:, :], dec[:, :], ALU.mult)
            P.tt("dve", hb[:, :], pb_[:, :], dec[:, :], ALU.mult)
            P.tt("dve", G[:, jt, :], hf[:, :], hb[:, :], ALU.add)
            P.tt("dve", Dd[:, jt, :], hb[:, :], hf[:, :], ALU.subtract)
            if jt == 0:
                P.tt("dve", G[0:1, 0, :], hf[0:1, :], brow[0:1, :], ALU.add)
        if "S_G" in dbg and not ctxmode:
            S["G"] = dscr("S_G", [128, nt, 512], BF16)
            P.dma("sp", S["G"][:, :, :], G[:, :, :])

        A.off = mark
        cw = A.alloc([3, 12], F32, "cw")
        cb = A.alloc([12], F32, "cb")
        load_cols(V(cw, cw.ap.rearrange("p k m -> p (k m)")),
                  V(I["hy_conv_w"], I["hy_conv_w"].ap[l].rearrange("k (m p) -> (k m) p", p=128)), 36, 128, ident)
        load_cols(cb[:, :], V(I["hy_conv_b"], I["hy_conv_b"].ap[l].rearrange("(m p) -> m p", p=128)), 12, 128, ident)
        xr = [A.alloc([n], F32, "xr%d" % i) for i in range(2)]
        uv = A.alloc([n], F32, "uv")
        u1 = A.alloc([n], F32, "u1")
        zT = A.alloc([n], BF16, "zT")

        def sconv(dst, m, src):
            P.act(dst, src[:, :], AF.Identity, scale=cw[:, 1, m:m + 1], bias=cb[:, m:m + 1])
            P.stt("dve", dst[:, 1:n], src[:, 0:n - 1], cw[:, 0, m:m + 1], dst[:, 1:n], ALU.mult, ALU.add)
            P.stt("dve", dst[:, 0:n - 1], src[:, 1:n], cw[:, 2, m:m + 1], dst[:, 0:n - 1], ALU.mult, ALU.add)

        cnt = 0
        for c in range(4):
            for part, dst in [(0, uv[:, :]), (1, u1[:, :]), (2, x2u[:, c, :])]:
                m = part * 4 + c
                x = xr[cnt % 2]
                cnt += 1
                P.dma("sp", x[:, :], S["hxT"][m * 128:(m + 1) * 128, tok0:tok0 + n])
                sconv(dst, m, x)
            P.tt("dve", zT[:, :], u1[:, :], uv[:, :], ALU.mult)
            for st in range(nt):
                pb = bk(4 + st % 4, BF16)
                P.transpose(pb[:, 0:128], zT[:, st * 128:(st + 1) * 128], ident[:, :])
                P.copy("act", zin[:, st, c * 128:(c + 1) * 128], pb[:, 0:128])

        A.off = mark
        Fb = [A.alloc([2, nt, 128], BF16, "Fb%d" % i) for i in range(2)]
        kc_ = A.alloc([512], F32, "kc")
        ks_ = A.alloc([512], F32, "ks")
        t1 = A.alloc([512], F32, "t1")
        t2 = A.alloc([512], F32, "t2")
        t3 = A.alloc([512], F32, "t3")
        t4 = A.alloc([512], F32, "t4")
        for ft in range(nt):
            Fb_ = Fb[ft % 2]
            P.dma("sp", Fb_[:, :, :, :], V(dF, dF.ap[ft]))
            b0_ = (ft % 2) * 4
            zc, zs, kcp, ksp = bk(b0_), bk(b0_ + 1), bk(b0_ + 2), bk(b0_ + 3)
            for st in range(nt):
                f, la = (st == 0), (st == nt - 1)
                P.mm(zc[:, :], Fb_[:, 0, st, :], zin[:, st, :], start=f, stop=la)
                P.mm(kcp[:, :], Fb_[:, 0, st, :], G[:, st, :], start=f, stop=la)
                P.mm(zs[:, :], Fb_[:, 1, st, :], zin[:, st, :], start=f, stop=la)
                P.mm(ksp[:, :], Fb_[:, 1, st, :], Dd[:, st, :], start=f, stop=la)
            P.act(kc_[:, :], kcp[:, :], AF.Identity, scale=wgt[:, ft:ft + 1])
            P.act(ks_[:, :], ksp[:, :], AF.Identity, scale=wgt[:, ft:ft + 1])
            P.tt("dve", t1[:, :], zc[:, :], kc_[:, :], ALU.mult)
            P.tt("dve", t2[:, :], zs[:, :], ks_[:, :], ALU.mult)
            P.tt("dve", t3[:, :], zs[:, :], kc_[:, :], ALU.mult)
            P.tt("dve", t4[:, :], zc[:, :], ks_[:, :], ALU.mult)
            P.tt("pool", Y[:, ft, 0, :], t1[:, :], t2[:, :], ALU.add)
            P.tt("pool", Y[:, ft, 1, :], t3[:, :], t4[:, :], ALU.subtract)
        zn, kn = bk(0), bk(1)
        for st in range(nt):
            P.mm(zn[0:1, :], altc[:, 0:1], zin[:, st, :], start=(st == 0), stop=(st == nt - 1))
            P.mm(kn[0:1, :], altc[:, 0:1], G[:, st, :], start=(st == 0), stop=(st == nt - 1))
        P.act(t1[0:1, :], kn[0:1, :], AF.Identity, scale=1.0 / NN)
        P.tt("dve", yny[0:1, :], zn[0:1, :], t1[0:1, :], ALU.mult)

        A.off = mark
        Ib = [A.alloc([2, n], BF16, "Ib%d" % i) for i in range(2)]
        ost = [A.alloc([n], BF16, "ost%d" % i) for i in range(2)]
        ntb = n // nb_cols
        per_pass = max(1, 8 // ntb)
        per_pass = min(per_pass, 4)
        for p0 in range(0, 4, per_pass):
            cl_list = list(range(p0, min(4, p0 + per_pass)))
            for ft in range(nt):
                Ib_ = Ib[ft % 2]
                P.dma("sp", Ib_[:, :, :], V(dI, dI.ap[ft]))
                for ci, c in enumerate(cl_list):
                    for tb in range(ntb):
                        acc = bk(ci * ntb + tb)
                        ts_ = slice(tb * nb_cols, (tb + 1) * nb_cols)
                        P.mm(acc[:, 0:nb_cols], Y[:, ft, 0, c * 128:(c + 1) * 128], Ib_[:, 0, ts_], start=(ft == 0), stop=False)
                        P.mm(acc[:, 0:nb_cols], Y[:, ft, 1, c * 128:(c + 1) * 128], Ib_[:, 1, ts_], start=False, stop=False)
            for ci, c in enumerate(cl_list):
                o = ost[c % 2]
                for tb in range(ntb):
                    acc = bk(ci * ntb + tb)
                    ts_ = slice(tb * nb_cols, (tb + 1) * nb_cols)
                    P.mm(acc[:, 0:nb_cols], yny[0:1, c * 128:(c + 1) * 128], altr[0:1, ts_], start=False, stop=True)
                    P.tt("dve", o[:, ts_], acc[:, 0:nb_cols], x2u[:, c, ts_], ALU.mult)
                P.dma("pool", S["catT"][c * 128:(c + 1) * 128, tok0:tok0 + n], o[:, :])

    def attn_fin_a(po, obuf, rcp):
        P.recip(rcp[:, :], po[:, 128:129])
        P.act(obuf[:, :], po[:, 0:128], AF.Identity, scale=rcp[:, :])

    def attn_fin_b(obuf, dst_col, ident, stage):
        pt = bk(7, BF16)
        P.transpose(pt[:, 0:128], obuf[:, :], ident[:, :])
        P.copy("dve", stage[:, dst_col:dst_col + 128], pt[:, 0:128])

    def phase_mla(l, with_ctx_q, side_fn=None):
        new_phase()
        nq_tot = NTOK if with_ctx_q else NX
        ident = A.alloc([128], BF16, "ident")
        P.dma("sp", ident[:, :], I["ident"][:, :])
        cq = A.alloc([6, NTOK], BF16, "cq")
        ckv = A.alloc([4, NTOK], BF16, "ckv")
        kr = A.alloc([NTOK], BF16, "kr")
        rope = A.alloc([2, NX], F32, "ropeq")
        P.dma("sp", cq[:, :, :], V(S["cqT"], S["cqT"].ap.rearrange("(c p) t -> p c t", p=128)))
        P.dma("sp", ckv[:, :, :], V(S["ckvT"], S["ckvT"].ap.rearrange("(c p) t -> p c t", p=128)))
        P.dma("sp", kr[0:64, :], S["krT"][:, :])
        P.dma("sp", rope[0:64, :, :], I["rope"][:, 2:4, :])
        wq = [A.alloc([6, 256], BF16, "wq%d" % i) for i in range(2)]
        wkv = [A.alloc([4, 256], BF16, "wkv%d" % i) for i in range(2)]
        qn = A.alloc([NTOK], BF16, "qn")
        qr = A.alloc([NTOK], BF16, "qr")
        kn = A.alloc([NTOK], BF16, "kn")
        vv = A.alloc([18, 132], BF16, "vv")
        P.memset("dve", vv[:, :, 128:129], 1.0)
        PT = [A.alloc([18, 512], BF16, "PT%d" % i) for i in range(2)]
        tA = A.alloc([512], F32, "tA")
        tB = A.alloc([512], F32, "tB")
        obufs = [A.alloc([128], BF16, "obuf%d" % i) for i in range(4)]
        rcps = [A.alloc([1], F32, "rcp%d" % i) for i in range(4)]
        fcnt = [0]
        pend_fin = [None]
        stage = [A.alloc([NTOK], BF16, "ostage%d" % i) for i in range(2)]
        side = side_fn(A.off) if side_fn is not None else None
        uq_src = I["mla_w_uq"].ap[l].rearrange("(kc p) n -> p kc n", p=128)
        ukv_src = I["mla_w_ukv"].ap[l].rearrange("(kc p) n -> p kc n", p=128)
        tblocks = [(i * 512, 512) for i in range(4)] + ([(2048, 256)] if with_ctx_q else [])
        kblocks = [(i * 512, 512) for i in range(4)] + [(2048, 256)]
        pc = [0]
        ptc = [0]

        def pbank():
            pc[0] += 1
            return bk(pc[0] % 2)

        for h in range(MLA_H):
            wq_, wkv_ = wq[h % 2], wkv[h % 2]
            q0 = h * 192
            P.dma("pool", wq_[:, :, 0:192], V(I["mla_w_uq"], uq_src[:, :, q0:q0 + 192]))
            P.dma("pool", wq_[:, :, 192:224], V(I["mla_w_uq"], uq_src[:, :, q0 + 160:q0 + 192]))
            P.dma("pool", wq_[:, :, 224:256], V(I["mla_w_uq"], uq_src[:, :, q0 + 128:q0 + 160]))
            P.dma("pool", wkv_[:, :, :], V(I["mla_w_ukv"], ukv_src[:, :, h * 256:(h + 1) * 256]))
            for (tb0, tbn) in tblocks:
                ps = pbank()
                for kc in range(6):
                    P.mm(ps[:, 0:tbn], wq_[:, kc, 0:128], cq[:, kc, tb0:tb0 + tbn], start=(kc == 0), stop=(kc == 5))
                P.act(qn[:, tb0:tb0 + tbn], ps[:, 0:tbn], AF.Identity, scale=MLA_SCALE)
                pa = pbank()
                for kc in range(6):
                    P.mm(pa[0:64, 0:tbn], wq_[:, kc, 128:192], cq[:, kc, tb0:tb0 + tbn], start=(kc == 0), stop=(kc == 5))
                if tb0 < NX:
                    pb_ = pbank()
                    for kc in range(6):
                        P.mm(pb_[0:64, 0:tbn], wq_[:, kc, 192:256], cq[:, kc, tb0:tb0 + tbn], start=(kc == 0), stop=(kc == 5))
                    P.tt("dve", tA[0:64, 0:tbn], pa[0:64, 0:tbn], rope[0:64, 0, tb0:tb0 + tbn], ALU.mult)
                    P.tt("dve", tB[0:64, 0:tbn], pb_[0:64, 0:tbn], rope[0:64, 1, tb0:tb0 + tbn], ALU.mult)
                    P.tt("dve", qr[0:64, tb0:tb0 + tbn], tA[0:64, 0:tbn], tB[0:64, 0:tbn], ALU.add)
                else:
                    P.act(qr[0:64, tb0:tb0 + tbn], pa[0:64, 0:tbn], AF.Identity, scale=MLA_SCALE)
            for (tb0, tbn) in kblocks:
                ps = pbank()
                for kc in range(4):
                    P.mm(ps[:, 0:tbn], wkv_[:, kc, 0:128], ckv[:, kc, tb0:tb0 + tbn], start=(kc == 0), stop=(kc == 3))
                P.copy("act", kn[:, tb0:tb0 + tbn], ps[:, 0:tbn])
            for kt in range(18):
                ps = pbank()
                for kc in range(4):
                    P.mm(ps[:, 0:128], ckv[:, kc, kt * 128:(kt + 1) * 128], wkv_[:, kc, 128:256], start=(kc == 0), stop=(kc == 3))
                P.copy("dve", vv[:, kt, 0:128], ps[:, 0:128])
            stg = stage[h % 2]
            jobs = [(qb * 512, 512, list(range(18))) for qb in range(4)]
            if with_ctx_q:
                jobs.append((2048, 256, [16, 17]))
            def qk_stage(job):
                (q0_, qn_, ktiles) = job
                pt_ = PT[ptc[0] % 2]
                ptc[0] += 1
                for ki, kt in enumerate(ktiles):
                    ps = bk(2 + ki % 3)
                    P.mm(ps[:, 0:qn_], kn[:, kt * 128:(kt + 1) * 128], qn[:, q0_:q0_ + qn_], start=True, stop=False)
                    P.mm(ps[:, 0:qn_], kr[0:64, kt * 128:(kt + 1) * 128], qr[0:64, q0_:q0_ + qn_], start=False, stop=True)
                    P.act(pt_[:, ki, 0:qn_], ps[:, 0:qn_], AF.Exp)
                return pt_

            def pv_stage(job, pt_):
                (q0_, qn_, ktiles) = job
                for qb in range(qn_ // 128):
                    po = bk(5 + qb % 2)
                    for ki, kt in enumerate(ktiles):
                        P.mm(po[:, 0:129], pt_[:, ki, qb * 128:(qb + 1) * 128], vv[:, kt, 0:129],
                             start=(ki == 0), stop=(ki == len(ktiles) - 1))
                    ob = obufs[fcnt[0] % 4]
                    attn_fin_a(po, ob, rcps[fcnt[0] % 4])
                    fcnt[0] += 1
                    if pend_fin[0] is not None:
                        attn_fin_b(*pend_fin[0])
                    pend_fin[0] = (ob, q0_ + qb * 128, ident, stg)

            prev = None
            for job in jobs:
                cur = qk_stage(job)
                if prev is not None:
                    pv_stage(*prev)
                prev = (job, cur)
                if side is not None:
                    next(side, None)
            pv_stage(*prev)
            attn_fin_b(*pend_fin[0])
            pend_fin[0] = None
            r0 = 512 + h * 128
            P.dma("sp", S["catT"][r0:r0 + 128, 0:nq_tot], stg[:, 0:nq_tot])
        if side is not None:
            for _ in side:
                pass

    def phase_na(l, with_ctx_q, side=None):
        new_phase()
        nq_tot = NTOK if with_ctx_q else NX
        ident = A.alloc([128], BF16, "ident")
        P.dma("sp", ident[:, :], I["ident"][:, :])
        mask = A.alloc([3200], F32, "mask")
        P.dma("sp", mask[:, :], I["namask"][:, :])
        braw = A.alloc([3200], F32, "braw")
        bias = [A.alloc([5, 5, 128], BF16, "bias%d" % i) for i in range(2)]
        qT = [A.alloc([NTOK], BF16, "qT%d" % i) for i in range(2)]
        kT = [A.alloc([NTOK], BF16, "kT%d" % i) for i in range(2)]
        vv = [A.alloc([18, 132], BF16, "vv%d" % i) for i in range(2)]
        for i in range(2):
            P.memset("dve", vv[i][:, :, 128:129], 1.0)
        PT = [A.alloc([7, 128], BF16, "PT%d" % i) for i in range(2)]
        obufs = [A.alloc([128], BF16, "obuf%d" % i) for i in range(4)]
        rcps = [A.alloc([1], F32, "rcp%d" % i) for i in range(4)]
        fcnt = [0]
        pend_fin = [None]

        def fin(po, dst_col, stg_):
            ob = obufs[fcnt[0] % 4]
            attn_fin_a(po, ob, rcps[fcnt[0] % 4])
            fcnt[0] += 1
            if pend_fin[0] is not None:
                attn_fin_b(*pend_fin[0])
            pend_fin[0] = (ob, dst_col, ident, stg_)

        def fin_flush():
            if pend_fin[0] is not None:
                attn_fin_b(*pend_fin[0])
            pend_fin[0] = None

        stage = [A.alloc([NTOK], BF16, "ostage%d" % i) for i in range(2)]
        vsrc = S["vna"].ap.rearrange("(t p) c -> p t c", p=128)
        gi = 0
        def na_load(h):
            q_, k_, v_, b_ = qT[h % 2], kT[h % 2], vv[h % 2], bias[h % 2]
            P.dma("sp", q_[:, :], S["qnaT"][h * 128:(h + 1) * 128, :])
            P.dma("sp", k_[:, :], S["knaT"][h * 128:(h + 1) * 128, :])
            P.dma("sp", v_[:, :, 0:128], V(S["vna"], vsrc[:, :, h * 128:(h + 1) * 128]))
            P.dma("sp", braw[:, :], V(I["rpbg"], I["rpbg"].ap[l, h]))
            P.tt("pool", V(b_, b_.ap.rearrange("p a b c -> p (a b c)")), braw[:, :], mask[:, :], ALU.add)

        na_load(0)
        for h in range(NA_H):
            q_, k_, v_, b_ = qT[h % 2], kT[h % 2], vv[h % 2], bias[h % 2]
            if h + 1 < NA_H:
                na_load(h + 1)
            stg = stage[h % 2]
            def na_qk(i):
                cls, j0 = _na_cls(i), _na_j0(i)
                g = gic[0]
                gic[0] += 1
                pa, pb_ = bk(2 * (g % 2)), bk(2 * (g % 2) + 1)
                pt_ = PT[g % 2]
                qs = q_[:, i * 128:(i + 1) * 128]
                ktiles = [j0 + c for c in range(5)] + [16, 17]
                for c in range(7):
                    dst = pa[:, c * 128:(c + 1) * 128] if c < 4 else pb_[:, (c - 4) * 128:(c - 3) * 128]
                    kt = ktiles[c]
                    if c < 5:
                        P.mm(dst, k_[:, kt * 128:(kt + 1) * 128], qs, start=True, stop=False)
                        P.mm(dst, ident[:, :], b_[:, cls, c, :], start=False, stop=True)
                    else:
                        P.mm(dst, k_[:, kt * 128:(kt + 1) * 128], qs, start=True, stop=True)
                P.act(V(pt_, pt_.ap[:, 0:4, :].rearrange("p a b -> p (a b)")), pa[:, 0:512], AF.Exp)
                P.act(V(pt_, pt_.ap[:, 4:7, :].rearrange("p a b -> p (a b)")), pb_[:, 0:384], AF.Exp)
                return (i, g, pt_, ktiles)

            def na_pv(i, g, pt_, ktiles):
                po = bk(4 + g % 2)
                for c in range(7):
                    P.mm(po[:, 0:129], pt_[:, c, :], v_[:, ktiles[c], 0:129], start=(c == 0), stop=(c == 6))
                fin(po, i * 128, stg)

            gic = [gi]
            prev = None
            for i in range(16):
                cur = na_qk(i)
                if prev is not None:
                    na_pv(*prev)
                prev = cur
                if side is not None and i % 2 == 1:
                    next(side, None)
            na_pv(*prev)
            gi = gic[0]
            if with_ctx_q:
                for qt in (16, 17):
                    pa = bk(2 * (gi % 2))
                    pt_ = PT[gi % 2]
                    gi += 1
                    for c, kt in enumerate((16, 17)):
                        P.mm(pa[:, c * 128:(c + 1) * 128], k_[:, kt * 128:(kt + 1) * 128], q_[:, qt * 128:(qt + 1) * 128])
                    P.act(V(pt_, pt_.ap[:, 0:2, :].rearrange("p a b -> p (a b)")), pa[:, 0:256], AF.Exp)
                    po = bk(4 + gi % 2)
                    for c, kt in enumerate((16, 17)):
                        P.mm(po[:, 0:129], pt_[:, c, :], v_[:, kt, 0:129], start=(c == 0), stop=(c == 1))
                    fin(po, qt * 128, stg)
            fin_flush()
            r0 = 1536 + h * 128
            P.dma("sp", S["catT"][r0:r0 + 128, 0:nq_tot], stg[:, 0:nq_tot])
        if side is not None:
            for _ in side:
                pass

    def post_residual(y_views, Ab, xres_src, dst, sqj, ssv, rs, xt, ot, tmpc=None, preloaded=False, stq="pool"):
        P.memset("dve", ssv[:, 0:4], 0.0)
        for q, yv in enumerate(y_views):
            P.act(sqj[:, :], yv, AF.Square, accum=ssv[:, q:q + 1])
        P.tt("dve", ssv[:, 4:5], ssv[:, 0:1], ssv[:, 1:2], ALU.add)
        P.tt("dve", ssv[:, 5:6], ssv[:, 2:3], ssv[:, 3:4], ALU.add)
        P.tt("dve", ssv[:, 6:7], ssv[:, 4:5], ssv[:, 5:6], ALU.add)
        rstd_from_ss(ssv[:, 6:7], rs[:, :], D)
        if not preloaded:
            P.dma("sp", xt[:, :], xres_src)
        for q, yv in enumerate(y_views):
            cs = slice(q * 512, (q + 1) * 512)
            if ot is None:
                tc_ = tmpc[q % 2]
                P.stt("dve", tc_[:, :], yv, rs[:, :], Ab[:, cs], ALU.mult, ALU.mult)
                P.tt("pool", xt[:, cs], tc_[:, :], xt[:, cs], ALU.add)
            else:
                P.stt("dve", ot[:, cs], yv, rs[:, :], Ab[:, cs], ALU.mult, ALU.mult)
                P.tt("pool", ot[:, cs], ot[:, cs], xt[:, cs], ALU.add)
        P.dma(stq, dst, (xt if ot is None else ot)[:, :])

    def phase_outproj(l, xsrc, tiles):
        new_phase()
        wo = A.alloc([16, D], BF16, "wo")
        wsrc = I["w_out"].ap[l].rearrange("(kc p) n -> p kc n", p=128)
        for q in range(4):
            P.dma("pool", wo[:, :, q * 512:(q + 1) * 512], V(I["w_out"], wsrc[:, :, q * 512:(q + 1) * 512]))
        Ab = [A.alloc([D], F32, "A2_%d" % k) for k in range(2)]
        for k in range(2):
            load_dv(l, k, 2, Ab[k][:, :])
        cat = [A.alloc([16, 128], BF16, "cat%d" % i) for i in range(2)]
        xt = [A.alloc([D], F32, "xt%d" % i) for i in range(2)]
        ot = [A.alloc([D], F32, "ot%d" % i) for i in range(2)]
        sqj = A.alloc([512], BF16, "sqj")
        ssv = [A.alloc([8], F32, "ssv%d" % i) for i in range(2)]
        rs = [A.alloc([1], F32, "rs%d" % i) for i in range(2)]
        csrc = S["catT"].ap.rearrange("(kc p) t -> p kc t", p=128)
        for n, tile in enumerate(tiles):
            k = 0 if tile < 16 else 1
            c_ = cat[n % 2]
            P.dma("sp", c_[:, :, :], V(S["catT"], csrc[:, :, tile * 128:(tile + 1) * 128]))
            P.dma("sp", xt[n % 2][:, :], xsrc[tile * 128:(tile + 1) * 128, :])
            ys = []
            for q in range(4):
                ps = bk((n % 2) * 4 + q)
                for kc in range(16):
                    P.mm(ps[:, :], c_[:, kc, :], wo[:, kc, q * 512:(q + 1) * 512], start=(kc == 0), stop=(kc == 15))
                ys.append(ps[:, :])
            rows = slice(tile * 128, (tile + 1) * 128)
            post_residual(ys, Ab[k], xsrc[rows, :], S["X1"][rows, :], sqj, ssv[n % 2], rs[n % 2], xt[n % 2], ot[n % 2],
                          preloaded=True)

    def phase_ffn(l, blocks, final):
        new_phase()
        Ab = [A.alloc([D], F32, "A4_%d" % k) for k in range(2)]
        for k in range(2):
            load_dv(l, k, 5, Ab[k][:, :])
        TBMAX = max(b[1] for b in blocks)
        hT = A.alloc([44, TBMAX], BF16, "hT")
        xt = [A.alloc([D], F32, "xt%d" % i) for i in range(2)]
        tmpc = [A.alloc([512], F32, "tmpc%d" % i) for i in range(2)]
        sqj = A.alloc([512], BF16, "sqj")
        ssv = [A.alloc([8], F32, "ssv%d" % i) for i in range(2)]
        rs = [A.alloc([1], F32, "rs%d" % i) for i in range(2)]
        mT = A.alloc([16, TBMAX], BF16, "mT")
        ssq = A.alloc([32], F32, "ssq")
        mark = A.off
        msrc = S["modT"].ap.rearrange("(kc p) t -> p kc t", p=128)

        def load_mT(t0_, tbn_):
            for q in range(4):
                P.dma("sp", mT[:, 4 * q:4 * q + 4, 0:tbn_], V(S["modT"], msrc[:, 4 * q:4 * q + 4, t0_:t0_ + tbn_]))

        load_mT(blocks[0][0], blocks[0][1])
        gsrc = I["w_ffn_gate"].ap[l].rearrange("(kc p) n -> p kc n", p=128)
        usrc = I["w_ffn_up"].ap[l].rearrange("(kc p) n -> p kc n", p=128)
        dsrc = I["w_ffn_down"].ap[l].rearrange("(fc p) n -> p fc n", p=128)
        WSZ = 16 * 512 * 2 // 4
        off_pair = [mark, mark + 2 * WSZ]
        off_sg = mark + 4 * WSZ
        post_jobs = []

        def alloc_at(off, shape, dt, name):
            A.off = off
            return A.alloc(shape, dt, name)

        for bi, (t0, tbn, sub) in enumerate(blocks):
            ntile = tbn // 128
            sg = [alloc_at(off_sg + i * 512, [512], F32, "sg%d" % i) for i in range(2)]
            pairs = {}
            pcn = 0
            for fg in range(11):
                par = fg % 2
                if par not in pairs or fg < 2:
                    pairs[par] = (alloc_at(off_pair[par], [16, 512], BF16, "wg%d" % par),
                                  alloc_at(off_pair[par] + WSZ, [16, 512], BF16, "wu%d" % par))
                g_, u_ = pairs[par]
                P.dma("pool", g_[:, :, :], V(I["w_ffn_gate"], gsrc[:, :, fg * 512:(fg + 1) * 512]))
                P.dma("pool", u_[:, :, :], V(I["w_ffn_up"], usrc[:, :, fg * 512:(fg + 1) * 512]))
                for q4 in range(4):
                    fc = fg * 4 + q4
                    for (s0, sn) in sub:
                        pg, pu = bk((pcn % 2) * 2), bk((pcn % 2) * 2 + 1)
                        sg_ = sg[pcn % 2]
                        pcn += 1
                        cs = slice(s0, s0 + sn)
                        for kc in range(16):
                            P.mm(pg[:, 0:sn], g_[:, kc, q4 * 128:(q4 + 1) * 128], mT[:, kc, cs], start=(kc == 0), stop=(kc == 15))
                        for kc in range(16):
                            P.mm(pu[:, 0:sn], u_[:, kc, q4 * 128:(q4 + 1) * 128], mT[:, kc, cs], start=(kc == 0), stop=(kc == 15))
                        P.act(sg_[:, 0:sn], pg[:, 0:sn], AF.Silu)
                        P.tt("dve", hT[:, fc, cs], sg_[:, 0:sn], pu[:, 0:sn], ALU.mult)
                        if fg == 0 and post_jobs:
                            post_jobs.pop(0)()
                if fg == 0:
                    while post_jobs:
                        post_jobs.pop(0)()
            if bi + 1 < len(blocks):
                load_mT(blocks[bi + 1][0], blocks[bi + 1][1])
            wd = [alloc_at(off_pair[1] + i * 1024, [4, 512], BF16, "wd%d" % i) for i in range(2)]
            ybuf = alloc_at(off_pair[1] + 2048, [ntile, D], BF16, "ybuf")
            assert A.off <= off_sg
            dcn = 0
            P.memset("pool", ssq[:, :], 0.0)
            for nq in range(4):
                for f4 in range(11):
                    d_ = wd[dcn % 2]
                    dcn += 1
                    P.dma("pool", d_[:, :, :], V(I["w_ffn_down"], dsrc[:, f4 * 4:(f4 + 1) * 4, nq * 512:(nq + 1) * 512]))
                    for fi in range(4):
                        fc = f4 * 4 + fi
                        for tl in range(ntile):
                            P.mm(bk(tl)[:, :], hT[:, fc, tl * 128:(tl + 1) * 128], d_[:, fi, :], start=(fc == 0), stop=(fc == 43))
                for tl in range(ntile):
                    k_ = 0 if (t0 + tl * 128) < NX else 1
                    P.act(sqj[:, :], bk(tl)[:, :], AF.Square, accum=ssq[:, tl * 4 + nq:tl * 4 + nq + 1])
                    P.tt("dve", ybuf[:, tl, nq * 512:(nq + 1) * 512], bk(tl)[:, :], Ab[k_][:, nq * 512:(nq + 1) * 512], ALU.mult)

            def mk_job(tl, t0=t0, ntile=ntile, ybuf=ybuf):
                def job():
                    tok = t0 + tl * 128
                    k = 0 if tok < NX else 1
                    ys = [ybuf[:, tl, q * 512:(q + 1) * 512] for q in range(4)]
                    rows = slice(tok, tok + 128)
                    dst = OUT[rows, :] if final else S["X2"][rows, :]
                    if tl == 0:
                        P.dma("sp", xt[0][:, :], S["X1"][t0:t0 + 128, :])
                    if tl + 1 < ntile:
                        P.dma("sp", xt[(tl + 1) % 2][:, :], S["X1"][tok + 128:tok + 256, :])
                    sv, r_, x_ = ssv[tl % 2], rs[tl % 2], xt[tl % 2]
                    P.tt("dve", sv[:, 4:5], ssq[:, tl * 4:tl * 4 + 1], ssq[:, tl * 4 + 1:tl * 4 + 2], ALU.add)
                    P.tt("dve", sv[:, 5:6], ssq[:, tl * 4 + 2:tl * 4 + 3], ssq[:, tl * 4 + 3:tl * 4 + 4], ALU.add)
                    P.tt("dve", sv[:, 6:7], sv[:, 4:5], sv[:, 5:6], ALU.add)
                    rstd_from_ss(sv[:, 6:7], r_[:, :], D)
                    P.stt("dve", x_[:, :], ybuf[:, tl, :], r_[:, :], x_[:, :], ALU.mult, ALU.add)
                    P.dma("sp", dst, x_[:, :])
                return job

            post_jobs = [mk_job(tl) for tl in range(ntile)]
        while post_jobs:
            post_jobs.pop(0)()

    SUB768 = [(0, 512), (512, 256)]
    all_tiles = list(range(18))
    x_tiles = list(range(16))
    for l in range(n_layers):
        last = (l == DEPTH - 1)
        src = I["xall"] if l == 0 else S["X2"]
        if want("adaln") and l == 0:
            phase_adaln(l)
        MT_OFF = 52000 - 18432 - 64
        A.reset(MT_OFF)
        mTd = A.alloc([16, NTOK], BF16, "mTd")
        nbase = 0
        if want("norm1"):
            phase_norm(l, src, 0, 1, all_tiles, direct=mTd, base=nbase)
        if want("inproj"):
            phase_inproj(l, mTd, nbase)
        if want("hyena"):
            phase_hyena(l, NX, 0, False)
            if not last:
                phase_hyena(l, NCTX, NX, True)
        if want("mla"):
            sf = (lambda base, l=l: adaln_gen(l + 1, base, bank=7, nrow=1)) if (l + 1 < n_layers and want("adaln")) else None
            phase_mla(l, not last, sf)
        if want("na"):
            phase_na(l, not last, None)
        if want("outproj"):
            phase_outproj(l, src, x_tiles if last else all_tiles)
        if want("ffn"):
            phase_norm(l, S["X1"], 3, 4, x_tiles if last else all_tiles)
            if last:
                phase_ffn(l, [(0, 768, SUB768), (768, 768, SUB768), (1536, 512, [(0, 512)])], True)
            else:
                phase_ffn(l, [(0, 768, SUB768), (768, 768, SUB768), (1536, 768, SUB768)], False)
    P.barrier()
    P.emit()
    es.close()
    return nc


_WEIGHT_NAMES = ["w_ada", "b_ada", "g_attn_pre", "g_attn_post", "g_ffn_pre", "g_ffn_post", "w_in", "hy_conv_w",
                 "hy_conv_b", "hy_f_w1", "hy_f_b1", "hy_f_w2", "hy_f_b2", "hy_f_w3", "hy_f_freq", "hy_bias",
                 "mla_g_q", "mla_w_uq", "mla_g_kv", "mla_w_ukv", "w_out", "w_ffn_gate", "w_ffn_up", "w_ffn_down"]


def make_in_maps(inputs, cores):
    consts = _const_tables()
    shared = {k: np.ascontiguousarray(np.asarray(inputs[k], dtype=np.float32)) for k in _WEIGHT_NAMES}
    shared["rpbg"] = _na_gather(np.asarray(inputs["na_rpb"], dtype=np.float32))
    shared.update(consts)
    maps = []
    for b in cores:
        m = dict(shared)
        m["xall"] = np.ascontiguousarray(np.concatenate([inputs["x"][b], inputs["ctx"][b]], axis=0).astype(np.float32))
        m["cc"] = np.ascontiguousarray(np.stack([inputs["c"][b], inputs["c_ctx"]], axis=0).astype(np.float32))
        maps.append(m)
    return maps


def kernel(**inputs):
    inputs = {k: np.asarray(v) for k, v in inputs.items()}
    nc = build_program()
    maps = make_in_maps(inputs, list(range(N_CORES)))
    res = run_bass_kernel_spmd(nc, maps, core_ids=list(range(N_CORES)))
    return np.stack([np.asarray(r["out"], dtype=np.float32) for r in res.results], axis=0)
```

```python
import math
import contextlib
import numpy as np
import ml_dtypes
import concourse.bass as bass
import concourse.mybir as mybir
from concourse.bass_utils import run_bass_kernel_spmd

F32 = mybir.dt.float32
BF16 = mybir.dt.bfloat16
AF = mybir.ActivationFunctionType
ALU = mybir.AluOpType

N_DMA_SEMS = 40
N_CORES = 4

D = 2048
NX = 2048
NCTX = 256
NTOK = NX + NCTX
DEPTH = 2
GRID_W = 64
HY_W = 512
Q_LORA = 768
KV_LORA = 512
MLA_H = 8
NA_H = 4
FFN = 5632
IN_COLS = 4416
MLA_SCALE = (128 + 64) ** -0.5
NA_SCALE = 128 ** -0.5
EPS = 1e-6
C_HX, C_CQ, C_CKV, C_KR, C_QNA, C_KNA, C_VNA = 0, 1536, 2304, 2816, 2880, 3392, 3904
MASKV = -30000.0
TWO_PI = 2.0 * math.pi


class V:
    __slots__ = ("t", "ap")

    def __init__(self, t, ap):
        self.t = t
        self.ap = ap

    def __getitem__(self, idx):
        return V(self.t, self.ap[idx])


class T:
    def __init__(self, ap, name=""):
        self.ap = ap
        self.name = name
        self.w = None
        self.r = []
        self.excl = False

    def __getitem__(self, idx):
        return V(self, self.ap[idx])

    @property
    def v(self):
        return V(self, self.ap)


class Op:
    __slots__ = ("eng", "fn", "deps", "idx", "is_dma", "flag", "semval", "dsem", "dval", "nop")

    def __init__(self, eng, fn, deps, idx, is_dma):
        self.eng = eng
        self.fn = fn
        self.deps = deps
        self.idx = idx
        self.is_dma = is_dma
        self.flag = False
        self.semval = 0
        self.dsem = None
        self.dval = 0
        self.nop = False


ENGS = ["pe", "act", "dve", "pool", "sp"]


class Prog:
    def __init__(self, nc):
        self.nc = nc
        self.ops = {e: [] for e in ENGS}
        self.n_dma = 0
        self.dma_last = [None] * N_DMA_SEMS
        self.dma_cnt = [0] * N_DMA_SEMS
        self.dma_since_bar = []

    def add(self, eng, fn, reads=(), writes=(), is_dma=False):
        deps = set()
        for v in reads:
            t = v.t
            if t.w is not None:
                deps.add(t.w)
            if t.excl:
                deps.update(r for r in t.r if r.eng != eng)
        for v in writes:
            t = v.t
            if t.w is not None:
                deps.add(t.w)
            deps.update(t.r)
        op = Op(eng, fn, deps, len(self.ops[eng]), is_dma)
        if is_dma:
            s = self.n_dma % N_DMA_SEMS
            self.n_dma += 1
            if self.dma_last[s] is not None:
                op.deps.add(self.dma_last[s])
            self.dma_last[s] = op
            self.dma_cnt[s] += 1
            op.dsem = s
            op.dval = 16 * self.dma_cnt[s]
            self.dma_since_bar.append(op)
        for v in reads:
            v.t.r.append(op)
        for v in writes:
            v.t.w = op
            v.t.r = []
        self.ops[eng].append(op)
        return op

    def wait_ops(self, eng, toks):
        op = self.add(eng, lambda e: None)
        op.nop = True
        op.deps.update(toks)
        return op

    def barrier(self):
        toks = []
        for e in ENGS:
            for op in reversed(self.ops[e]):
                if not op.is_dma and not op.nop:
                    toks.append(op)
                    break
        toks.extend(self.dma_since_bar)
        self.dma_since_bar = []
        for e in ENGS:
            self.wait_ops(e, toks)

    def mm(self, out, lhsT, rhs, start=True, stop=True):
        return self.add("pe", lambda e: e.matmul(out.ap, lhsT.ap, rhs.ap, start=start, stop=stop),
                        reads=[lhsT, rhs], writes=[out])

    def transpose(self, out, in_, ident):
        return self.add("pe", lambda e: e.transpose(out.ap, in_.ap, ident.ap),
                        reads=[in_, ident], writes=[out])

    def act(self, out, in_, func, bias=None, scale=None, accum=None):
        reads = [in_]
        writes = [out]
        kw = {}
        if bias is not None:
            if isinstance(bias, V):
                reads.append(bias)
                kw["bias"] = bias.ap
            else:
                kw["bias"] = float(bias)
        if scale is not None:
            if isinstance(scale, V):
                reads.append(scale)
                kw["scale"] = scale.ap
            else:
                kw["scale"] = float(scale)
        if accum is not None:
            writes.append(accum)
            kw["accum_out"] = accum.ap
        return self.add("act", lambda e: e.activation(out.ap, in_.ap, func, **kw), reads=reads, writes=writes)

    def tt(self, eng, out, in0, in1, op):
        return self.add(eng, lambda e: e.tensor_tensor(out.ap, in0.ap, in1.ap, op), reads=[in0, in1], writes=[out])

    def ts(self, eng, out, in0, s1, op0, s2=None, op1=None):
        reads = [in0]
        a1 = s1.ap if isinstance(s1, V) else float(s1)
        if isinstance(s1, V):
            reads.append(s1)
        a2 = None
        if s2 is not None:
            a2 = s2.ap if isinstance(s2, V) else float(s2)
            if isinstance(s2, V):
                reads.append(s2)
        kw = {}
        if op1 is not None:
            kw["op1"] = op1
        return self.add(eng, lambda e: e.tensor_scalar(out.ap, in0.ap, a1, a2, op0, **kw), reads=reads, writes=[out])

    def stt(self, eng, out, in0, scalar, in1, op0, op1):
        reads = [in0, in1]
        a = scalar.ap if isinstance(scalar, V) else float(scalar)
        if isinstance(scalar, V):
            reads.append(scalar)
        return self.add(eng, lambda e: e.scalar_tensor_tensor(out.ap, in0.ap, a, in1.ap, op0, op1),
                        reads=reads, writes=[out])

    def copy(self, eng, out, in_):
        if eng == "act":
            return self.add("act", lambda e: e.copy(out.ap, in_.ap), reads=[in_], writes=[out])
        return self.add(eng, lambda e: e.tensor_copy(out.ap, in_.ap), reads=[in_], writes=[out])

    def memset(self, eng, out, val):
        return self.add(eng, lambda e: e.memset(out.ap, val), writes=[out])

    def recip(self, out, in_):
        return self.add("dve", lambda e: e.reciprocal(out.ap, in_.ap), reads=[in_], writes=[out])

    def dma(self, q, out, in_, **kw):
        return self.add(q, lambda e: e.dma_start(out.ap, in_.ap, **kw), reads=[in_], writes=[out], is_dma=True)

    def emit(self):
        nc = self.nc
        ops = self.ops
        for e in ENGS:
            for op in ops[e]:
                for d in op.deps:
                    if d.is_dma:
                        continue
                    if d.eng == e and e == "pe" and not op.is_dma:
                        continue
                    d.flag = True
        for e in ENGS:
            c = 0
            for op in ops[e]:
                if op.flag and not op.is_dma:
                    c += 1
                op.semval = c
        with contextlib.ExitStack() as es:
            esem = {e: es.enter_context(nc.semaphore("s_" + e)) for e in ENGS if e != "sp"}
            dsem = [es.enter_context(nc.semaphore("d%d" % i)) for i in range(N_DMA_SEMS)]
            block = es.enter_context(nc.Block())

            def run(e, eng):
                seen = {}
                for op in ops[e]:
                    waits = {}
                    for d in op.deps:
                        if d.is_dma:
                            key = ("d", d.dsem)
                            val = d.dval
                            sem = dsem[d.dsem]
                        else:
                            if d.eng == e and e == "pe" and not op.is_dma:
                                continue
                            if d.eng == e and d.idx >= op.idx:
                                continue
                            key = ("e", d.eng)
                            val = d.semval
                            sem = esem[d.eng]
                        if seen.get(key, 0) >= val:
                            continue
                        if key not in waits or waits[key][1] < val:
                            waits[key] = (sem, val)
                    for key, (sem, val) in waits.items():
                        eng.wait_ge(sem, val)
                        seen[key] = val
                    ins = op.fn(eng)
                    if ins is None:
                        continue
                    if op.is_dma:
                        ins.then_inc(dsem[op.dsem], 16)
                    elif op.flag:
                        ins.then_inc(esem[e], 1)

            @block.tensor
            def _(eng):
                run("pe", eng)

            @block.scalar
            def _(eng):
                run("act", eng)

            @block.vector
            def _(eng):
                run("dve", eng)

            @block.gpsimd
            def _(eng):
                run("pool", eng)

            @block.sync
            def _(eng):
                run("sp", eng)


class Arena:
    def __init__(self, ap, nwords):
        self.ap = ap
        self.n = nwords
        self.off = 0
        self.live = []

    def reset(self, mark=0):
        self.off = mark

    def alloc(self, shape, dt, name=""):
        nel = int(np.prod(shape))
        nbytes = nel * (4 if dt == F32 else 2)
        nw = (nbytes + 31) // 32 * 8
        assert self.off + nw <= self.n, ("SBUF arena overflow", name, self.off, nw, self.n)
        s0, e0 = self.off, self.off + nw
        a = self.ap[:, s0:e0]
        self.off += nw
        if dt != F32:
            a = a.bitcast(dt)
        a = a[:, 0:nel]
        if len(shape) > 1:
            names = ["d%d" % i for i in range(len(shape))]
            kw = {n: int(s) for n, s in zip(names[:-1], shape[:-1])}
            a = a.rearrange("p (%s) -> p %s" % (" ".join(names), " ".join(names)), **kw)
        t = T(a, name)
        keep = []
        for (s1, e1, t1) in self.live:
            if s1 < e0 and s0 < e1:
                t.r.extend(t1.r)
                if t1.w is not None:
                    t.r.append(t1.w)
                if s0 <= s1 and e1 <= e0:
                    continue
            keep.append((s1, e1, t1))
        keep.append((s0, e0, t))
        self.live = keep
        return t


def _bf(a):
    return np.ascontiguousarray(a.astype(ml_dtypes.bfloat16))


def _dft_tables(n):
    N = 2 * n
    nt = n // 128
    idx = np.arange(n, dtype=np.int64)
    prod = (idx[:, None] * idx[None, :]) % N
    ang = prod.astype(np.float64) * (2.0 * np.pi / N)
    Cm = np.cos(ang)
    Sm = np.sin(ang)
    F = np.stack([Cm, Sm], 0).reshape(2, nt, 128, nt, 128)
    F = F.transpose(3, 2, 0, 1, 4)
    I = np.stack([Cm, Sm], 0).reshape(2, nt, 128, n).transpose(1, 2, 0, 3)
    wgt = np.full((128, nt), 2.0 / N, np.float32)
    wgt[0, 0] = 1.0 / N
    return _bf(F), _bf(I), wgt


def _filter_tables(n):
    pos = np.arange(n, dtype=np.float32)
    t = np.linspace(0.0, 1.0, n, dtype=np.float32)
    bands = np.linspace(1e-4, 15, 16, dtype=np.float32)
    ang = np.float32(2.0 * math.pi / n) * pos[:, None] * bands[None, :]
    z = np.concatenate([t[:, None], np.cos(ang), -np.sin(ang)], axis=-1).astype(np.float32)
    negt = (-t).reshape(n // 128, 128).T.copy()
    return np.ascontiguousarray(z.T), np.ascontiguousarray(negt.astype(np.float32))


def _const_tables():
    c = {}
    c["dftF"], c["dftI"], c["wgt"] = _dft_tables(NX)
    c["dftFc"], c["dftIc"], c["wgtc"] = _dft_tables(NCTX)
    c["ztab"], c["negt"] = _filter_tables(NX)
    c["ztabc"], c["negtc"] = _filter_tables(NCTX)
    deltas = np.abs(np.linspace(math.log(1e-2) / 0.3, math.log(1e-2) / 1.5, HY_W, dtype=np.float32))
    c["deltab"] = np.ascontiguousarray(np.broadcast_to(deltas[None, :], (128, HY_W)).astype(np.float32))
    alt = np.where(np.arange(128) % 2 == 0, 1.0, -1.0).astype(np.float32)
    c["altcol"] = _bf(alt.reshape(128, 1))
    c["altrow"] = _bf(np.where(np.arange(NX) % 2 == 0, 1.0, -1.0).astype(np.float32).reshape(1, NX))
    tt_ = np.arange(NX)
    row = (tt_ // GRID_W).astype(np.float32)
    col = (tt_ % GRID_W).astype(np.float32)
    inv = (10000.0 ** (-np.arange(16, dtype=np.float32) / 16)).astype(np.float32)
    ang = np.concatenate([row[:, None] * inv, col[:, None] * inv], axis=-1)
    cs, sn = np.cos(ang).T, np.sin(ang).T
    cos2 = np.concatenate([cs, cs], 0).astype(np.float32)
    sin2 = np.concatenate([-sn, sn], 0).astype(np.float32)
    c["rope"] = np.ascontiguousarray(np.stack([cos2, sin2, cos2 * MLA_SCALE, sin2 * MLA_SCALE], 1).astype(np.float32))
    c["ident"] = _bf(np.eye(128, dtype=np.float32))
    c["namask"] = _na_index()[3]
    return c


_NA_CLASSES = [0, 1, 2, 14, 15]


def _na_cls(i):
    if i <= 1:
        return i
    if i <= 13:
        return 2
    return i - 11


def _na_j0(i):
    return min(max(i - 2, 0), 11)


_NA_IDX = None


def _na_index():
    global _NA_IDX
    if _NA_IDX is not None:
        return _NA_IDX
    ri = np.zeros((128, 5, 5, 128), np.int64)
    ci = np.zeros((128, 5, 5, 128), np.int64)
    ok = np.zeros((128, 5, 5, 128), bool)
    for cls, i in enumerate(_NA_CLASSES):
        j0 = _na_j0(i)
        for qr in range(2):
            r = 2 * i + qr
            r0 = min(max(r - 4, 0), 24)
            for c in range(5):
                for kr2 in range(2):
                    R = 2 * (j0 + c) + kr2
                    if not (r0 <= R <= r0 + 7):
                        continue
                    for qc in range(64):
                        cs = min(max(qc - 8, 0), 48)
                        kc = np.arange(cs, cs + 16)
                        ri[kr2 * 64 + kc, cls, c, qr * 64 + qc] = R - r + 7
                        ci[kr2 * 64 + kc, cls, c, qr * 64 + qc] = kc - qc + 15
                        ok[kr2 * 64 + kc, cls, c, qr * 64 + qc] = True
    mask = np.where(ok, 0.0, MASKV).astype(np.float32).reshape(128, 3200)
    _NA_IDX = (ri, ci, ok, np.ascontiguousarray(mask))
    return _NA_IDX


def _na_gather(rpb):
    ri, ci, ok, _ = _na_index()
    g = rpb[:, :, ri, ci] * ok[None, None].astype(np.float32)
    return np.ascontiguousarray(g.reshape(rpb.shape[0], 4, 128, 3200).astype(np.float32))


def build_program(n_layers=DEPTH, dbg=(), phases=None):
    nc = bass.Bass("TRN2", target_bir_lowering=False)
    P = Prog(nc)
    dbg = set(dbg)

    def din(name, shape, dt=F32):
        h = nc.dram_tensor(name, list(shape), dt, kind="ExternalInput")
        return T(h.ap(), name)

    def dscr(name, shape, dt=F32):
        kind = "ExternalOutput" if name in dbg else "Internal"
        h = nc.dram_tensor(name, list(shape), dt, kind=kind)
        return T(h.ap(), name)

    I = {}
    I["xall"] = din("xall", [NTOK, D])
    I["cc"] = din("cc", [2, D])
    for nm, shp in [("w_ada", [DEPTH, D, 6 * D]), ("b_ada", [DEPTH, 6 * D]), ("g_attn_pre", [DEPTH, D]),
                    ("g_attn_post", [DEPTH, D]), ("g_ffn_pre", [DEPTH, D]), ("g_ffn_post", [DEPTH, D]),
                    ("w_in", [DEPTH, D, IN_COLS]), ("hy_conv_w", [DEPTH, 3, 1536]), ("hy_conv_b", [DEPTH, 1536]),
                    ("hy_f_w1", [DEPTH, 33, 64]), ("hy_f_b1", [DEPTH, 64]), ("hy_f_w2", [DEPTH, 64, 64]),
                    ("hy_f_b2", [DEPTH, 64]), ("hy_f_w3", [DEPTH, 64, 1024]), ("hy_f_freq", [DEPTH, 64]),
                    ("hy_bias", [DEPTH, 512]), ("mla_g_q", [DEPTH, Q_LORA]), ("mla_w_uq", [DEPTH, Q_LORA, 1536]),
                    ("mla_g_kv", [DEPTH, KV_LORA]), ("mla_w_ukv", [DEPTH, KV_LORA, 2048]),
                    ("w_out", [DEPTH, D, D]), ("w_ffn_gate", [DEPTH, D, FFN]), ("w_ffn_up", [DEPTH, D, FFN]),
                    ("w_ffn_down", [DEPTH, FFN, D]),
                    ("rpbg", [DEPTH, 4, 128, 3200]), ("namask", [128, 3200]),
                    ("ztab", [33, NX]), ("ztabc", [33, NCTX]), ("negt", [128, 16]), ("negtc", [128, 2]),
                    ("deltab", [128, HY_W]), ("wgt", [128, 16]), ("wgtc", [128, 2]), ("rope", [64, 4, NX])]:
        I[nm] = din(nm, shp)
    for nm, shp in [("dftF", [16, 128, 2, 16, 128]), ("dftI", [16, 128, 2, NX]), ("dftFc", [2, 128, 2, 2, 128]),
                    ("dftIc", [2, 128, 2, NCTX]), ("altcol", [128, 1]), ("altrow", [1, NX]), ("ident", [128, 128])]:
        I[nm] = din(nm, shp, BF16)
    OUT = T(nc.dram_tensor("out", [NX, D], F32, kind="ExternalOutput").ap(), "out")

    S = {}
    S["dv"] = dscr("S_dv", [DEPTH, 2, 6, D])
    S["modT"] = dscr("S_modT", [D, NTOK], BF16)
    S["hxT"] = dscr("S_hxT", [1536, NTOK])
    S["cqT"] = dscr("S_cqT", [Q_LORA, NTOK], BF16)
    S["ckvT"] = dscr("S_ckvT", [KV_LORA, NTOK], BF16)
    S["krT"] = dscr("S_krT", [64, NTOK], BF16)
    S["qnaT"] = dscr("S_qnaT", [512, NTOK], BF16)
    S["knaT"] = dscr("S_knaT", [512, NTOK], BF16)
    S["vna"] = dscr("S_vna", [NTOK, 512], BF16)
    S["catT"] = dscr("S_catT", [D, NTOK], BF16)
    S["X1"] = dscr("S_X1", [NTOK, D])
    S["X2"] = dscr("S_X2", [NTOK, D])

    es = contextlib.ExitStack()
    AR_WORDS = 52000
    arena_h = es.enter_context(nc.sbuf_tensor("arena", [128, AR_WORDS], F32))
    psum_h = es.enter_context(nc.psum_tensor("psum", [128, 4096], F32))
    A = Arena(arena_h, AR_WORDS)
    banks = [T(psum_h[:, i * 512:(i + 1) * 512], "bank%d" % i) for i in range(8)]
    for b_ in banks:
        b_.excl = True

    def bk(i, dt=F32):
        t = banks[i]
        return V(t, t.ap if dt == F32 else t.ap.bitcast(dt))

    def bcast_rows(t, row_off, n, parts=128):
        return V(t, bass.AP(t.ap.tensor, row_off, [[0, parts], [1, n]]))

    epsc_box = [None]

    def new_phase(base=0):
        A.reset(base)
        e_ = A.alloc([1], F32, "epsc")
        P.memset("pool", e_[:, :], EPS)
        epsc_box[0] = e_

    class _Eps:
        def __getitem__(self, idx):
            return epsc_box[0][idx]

    epsc = _Eps()

    def want(ph):
        return phases is None or ph in phases

    ADA_BASE = 52000 - 20640
    ada_end = [0]

    def adaln_gen(l, base, bank=6, nrow=2):
        save = A.off
        A.off = base
        ccx = A.alloc([16], F32, "ccx")
        ccc = A.alloc([16], F32, "ccc")
        scT = A.alloc([16, 2], BF16, "scT")
        wb = [A.alloc([16, 512], BF16, "wada%d" % i) for i in range(2)]
        mrow = [A.alloc([D], F32, "mrow%d" % i) for i in range(nrow)]
        grow = [A.alloc([D], F32, "grow%d" % i) for i in range(nrow)]
        drow = [A.alloc([D], F32, "drow%d" % i) for i in range(nrow)]
        ada_end[0] = A.off
        A.off = save
        P.dma("sp", ccx[:, :], V(I["cc"], I["cc"].ap[0, :].rearrange("(p j) -> p j", j=16)))
        P.dma("sp", ccc[:, :], V(I["cc"], I["cc"].ap[1, :].rearrange("(p j) -> p j", j=16)))
        P.act(scT[:, :, 0], ccx[:, :], AF.Silu)
        P.act(scT[:, :, 1], ccc[:, :], AF.Silu)
        wsrc = I["w_ada"].ap[l].rearrange("(p j) n -> p j n", j=16)
        plan = {0: (1, None, False), 1: (0, "g_attn_pre", True), 2: (2, "g_attn_post", False),
                3: (4, None, False), 4: (3, "g_ffn_pre", True), 5: (5, "g_ffn_post", False)}
        ps = bk(bank)
        for mi in range(6):
            mr, gr, dr = mrow[mi % nrow], grow[mi % nrow], drow[mi % nrow]
            slot, gname, addone = plan[mi]
            P.dma("sp", mr[0:2, :], bcast_rows(I["b_ada"], l * 6 * D + mi * D, D, 2))
            if gname is not None:
                P.dma("sp", gr[0:2, :], bcast_rows(I[gname], l * D, D, 2))
            for q in range(4):
                nb = mi * 4 + q
                w = wb[nb % 2]
                P.dma("pool", w[:, :, :], V(I["w_ada"], wsrc[:, :, nb * 512:(nb + 1) * 512]))
                for j in range(16):
                    P.mm(ps[0:2, :], scT[:, j, :], w[:, j, :], start=(j == 0), stop=(j == 15))
                P.tt("dve", mr[0:2, q * 512:(q + 1) * 512], ps[0:2, :], mr[0:2, q * 512:(q + 1) * 512], ALU.add)
                yield
            if gname is None:
                src_row = mr
            else:
                if addone:
                    P.stt("dve", dr[0:2, :], mr[0:2, :], 1.0, gr[0:2, :], ALU.add, ALU.mult)
                else:
                    P.tt("dve", dr[0:2, :], mr[0:2, :], gr[0:2, :], ALU.mult)
                src_row = dr
            P.dma("sp", V(S["dv"], S["dv"].ap[l, :, slot, :]), src_row[0:2, :])
            yield

    def phase_adaln(l):
        new_phase()
        for _ in adaln_gen(l, A.off):
            pass

    def load_dv(l, kind, idx, dst):
        off = ((l * 2 + kind) * 6 + idx) * D
        P.dma("sp", dst, bcast_rows(S["dv"], off, D, 128))

    def rstd_from_ss(ss, rstd, n):
        P.act(rstd, ss, AF.Sqrt, scale=1.0 / n, bias=epsc[:, :])
        P.recip(rstd, rstd)

    def load_cols(dst, src_rows, r, npart, ident_bf, bank=6):
        rowbuf = A.alloc([128], F32, "rowbuf")
        identf = A.alloc([128], F32, "identf")
        P.copy("dve", identf[:, :], ident_bf[:, :])
        P.dma("sp", rowbuf[0:r, 0:npart], src_rows)
        ps = bk(bank)
        P.mm(ps[0:npart, 0:r], rowbuf[0:r, 0:npart], identf[0:r, 0:r])
        P.copy("dve", dst, ps[0:npart, 0:r])

    def phase_norm(l, src, a_idx, b_idx, tiles, direct=None, base=0):
        new_phase(base)
        ident = A.alloc([128], BF16, "ident")
        P.dma("sp", ident[:, :], I["ident"][:, :])
        Ab = [A.alloc([D], F32, "Ab%d" % k) for k in range(2)]
        Bb = [A.alloc([D], F32, "Bb%d" % k) for k in range(2)]
        for k in range(2):
            load_dv(l, k, a_idx, Ab[k][:, :])
            load_dv(l, k, b_idx, Bb[k][:, :])
        xt = [A.alloc([D], F32, "xt%d" % i) for i in range(3)]
        xm = [A.alloc([D], F32, "xm%d" % i) for i in range(2)]
        xb = [A.alloc([D], BF16, "xb%d" % i) for i in range(2)]
        junk = A.alloc([D], BF16, "junk")
        ss = [A.alloc([1], F32, "ss%d" % i) for i in range(3)]
        rs = [A.alloc([1], F32, "rs%d" % i) for i in range(3)]
        stage = [A.alloc([16, 512], BF16, "stage%d" % i) for i in range(2)]
        dstT = S["modT"].ap.rearrange("(kc p) t -> p kc t", p=128)
        groups = [tiles[i:i + 4] for i in range(0, len(tiles), 4)]
        seq = []
        for gi, grp in enumerate(groups):
            for ti, tile in enumerate(grp):
                seq.append((gi, ti, tile, ti == len(grp) - 1, grp))

        def s1(n):
            (gi, ti, tile, lastg, grp) = seq[n]
            x = xt[n % 3]
            P.dma("sp", x[:, :], src[tile * 128:(tile + 1) * 128, :])
            P.memset("pool", ss[n % 3][:, :], 0.0)
            P.act(junk[:, :], x[:, :], AF.Square, accum=ss[n % 3][:, :])
            P.act(rs[n % 3][:, :], ss[n % 3][:, :], AF.Sqrt, scale=1.0 / D, bias=epsc[:, :])

        def s2(n):
            (gi, ti, tile, lastg, grp) = seq[n]
            st = stage[gi % 2]
            k = 0 if tile < 16 else 1
            x = xt[n % 3]
            b = xb[n % 2]
            xm_ = xm[n % 2]
            P.recip(rs[n % 3][:, :], rs[n % 3][:, :])
            P.stt("dve", xm_[:, :], x[:, :], rs[n % 3][:, :], Ab[k][:, :], ALU.mult, ALU.mult)
            P.tt("dve", b[:, 0:1024], xm_[:, 0:1024], Bb[k][:, 0:1024], ALU.add)
            P.tt("pool", b[:, 1024:2048], xm_[:, 1024:2048], Bb[k][:, 1024:2048], ALU.add)
            for g in range(4):
                pb = bk((n * 4 + g) % 4 + 4, BF16)
                for c in range(4):
                    kc = 4 * g + c
                    P.transpose(pb[:, c * 128:(c + 1) * 128], b[:, kc * 128:(kc + 1) * 128], ident[:, :])
                src_ps = V(pb.t, pb.ap[:, 0:512].rearrange("p (c t) -> p c t", c=4))
                if direct is not None:
                    P.copy("act", direct[:, 4 * g:4 * g + 4, tile * 128:(tile + 1) * 128], src_ps)
                else:
                    P.copy("act", st[:, 4 * g:4 * g + 4, ti * 128:(ti + 1) * 128], src_ps)
            if lastg and direct is None:
                t0 = grp[0] * 128
                nt = len(grp) * 128
                P.dma("pool", V(S["modT"], dstT[:, :, t0:t0 + nt]), st[:, :, 0:nt])

        assert direct is None or A.off <= 52000 - 18432 - 64, A.off
        s1(0)
        if len(seq) > 1:
            s1(1)
        for n in range(len(seq)):
            if n + 2 < len(seq):
                s1(n + 2)
            s2(n)

    def phase_inproj(l, mT, base):
        new_phase(base)
        wb = [A.alloc([16, 512], BF16, "win%d" % i) for i in range(2)]
        wsrc = I["w_in"].ap[l].rearrange("(kc p) n -> p kc n", p=128)
        ones = A.alloc([128], BF16, "ones")
        P.memset("dve", ones[:, :], 1.0)
        identb = A.alloc([128], BF16, "identb")
        P.dma("sp", identb[:, :], I["ident"][:, :])
        st32 = [A.alloc([NTOK], F32, "st32_%d" % i) for i in range(2)]
        raw = A.alloc([6, NTOK], BF16, "raw")
        sq = [A.alloc([512], BF16, "sq%d" % i) for i in range(2)]
        rstd = A.alloc([NTOK], F32, "rstdb")
        gcols = {"mla_g_q": A.alloc([8], F32, "gcolq"), "mla_g_kv": A.alloc([8], F32, "gcolkv")}
        for gname_, nch_ in (("mla_g_q", 6), ("mla_g_kv", 4)):
            load_cols(gcols[gname_][:, 0:nch_], V(I[gname_], I[gname_].ap[l].rearrange("(c p) -> c p", p=128)), nch_, 128, identb)
        rope = A.alloc([2, NX], F32, "ropek")
        P.dma("sp", rope[0:64, :, :], I["rope"][:, 0:2, :])
        stb = [A.alloc([NTOK], BF16, "stb%d" % i) for i in range(2)]
        vst = [A.alloc([512], BF16, "vst%d" % i) for i in range(2)]
        tmp = A.alloc([512], F32, "tmpc")
        tblocks = [(i * 512, 512) for i in range(4)] + [(2048, 256)]
        wcnt = [0]
        pcnt = [0]
        assert A.off + 1024 <= 52000 - 18432 - 64, A.off

        def load_w(c0, ncols, dst_c0=0, w=None):
            if w is None:
                w = wb[wcnt[0] % 2]
                wcnt[0] += 1
            P.dma("pool", w[:, :, dst_c0:dst_c0 + ncols], V(I["w_in"], wsrc[:, :, c0:c0 + ncols]))
            return w

        def proj_fm(w, wc0, m, tb0, tbn):
            ps = bk(pcnt[0] % 2)
            pcnt[0] += 1
            for kc in range(16):
                P.mm(ps[0:m, 0:tbn], w[:, kc, wc0:wc0 + m], mT[:, kc, tb0:tb0 + tbn], start=(kc == 0), stop=(kc == 15))
            return ps[0:m, 0:tbn]

        n = 0
        for g in (range(3) if want("ip_hx") else []):
            w = load_w(C_HX + g * 512, 512)
            for cc in range(4):
                st = st32[n % 2]
                for (tb0, tbn) in tblocks:
                    ps = proj_fm(w, cc * 128, 128, tb0, tbn)
                    P.copy("act" if (tb0 // 512) % 2 == 0 else "dve", st[:, tb0:tb0 + tbn], ps)
                r0 = (g * 4 + cc) * 128
                P.dma("sp", S["hxT"][r0:r0 + 128, :], st[:, :])
                n += 1

        def latent(c0, nch, gname, dst, nfeat):
            gcol = gcols[gname]
            ws = []
            for g in range((nch + 3) // 4):
                ncols = min(512, nch * 128 - g * 512)
                ws.append(load_w(c0 + g * 512, ncols))
            for (tb0, tbn) in tblocks:
                acc = bk(2 + (tb0 // 512) % 2)
                pend = None
                for ch in range(nch):
                    ps = proj_fm(ws[ch // 4], (ch % 4) * 128, 128, tb0, tbn)
                    s = sq[ch % 2]
                    P.copy("dve", raw[:, ch, tb0:tb0 + tbn], ps)
                    P.act(s[:, 0:tbn], raw[:, ch, tb0:tb0 + tbn], AF.Square)
                    if pend is not None:
                        P.mm(acc[:, 0:tbn], ones[:, :], pend[0][:, 0:tbn], start=(pend[1] == 0), stop=False)
                    pend = (s, ch)
                P.mm(acc[:, 0:tbn], ones[:, :], pend[0][:, 0:tbn], start=(pend[1] == 0), stop=True)
                P.ts("dve", rstd[:, tb0:tb0 + tbn], acc[:, 0:tbn], 1.0 / nfeat, ALU.mult, EPS, ALU.add)
                P.act(rstd[:, tb0:tb0 + tbn], rstd[:, tb0:tb0 + tbn], AF.Sqrt)
                P.recip(rstd[:, tb0:tb0 + tbn], rstd[:, tb0:tb0 + tbn])
            for ch in range(nch):
                sb_ = stb[ch % 2]
                P.stt("dve", sb_[:, :], raw[:, ch, :], gcol[:, ch:ch + 1], rstd[:, :], ALU.mult, ALU.mult)
                P.dma("sp", dst[ch * 128:(ch + 1) * 128, :], sb_[:, :])

        if want("ip_lat"):
            latent(C_CQ, 6, "mla_g_q", S["cqT"], Q_LORA)
            latent(C_CKV, 4, "mla_g_kv", S["ckvT"], KV_LORA)
        if not want("ip_rest"):
            return

        w = wb[wcnt[0] % 2]
        wcnt[0] += 1
        load_w(C_KR, 64, 0, w)
        load_w(C_KR + 32, 32, 64, w)
        load_w(C_KR, 32, 96, w)
        sb_ = stb[0]
        for (tb0, tbn) in tblocks:
            pa = proj_fm(w, 0, 64, tb0, tbn)
            if tb0 < NX:
                pbb = proj_fm(w, 64, 64, tb0, tbn)
                P.tt("dve", tmp[0:64, 0:tbn], pa, rope[0:64, 0, tb0:tb0 + tbn], ALU.mult)
                P.tt("dve", st32[0][0:64, 0:tbn], pbb, rope[0:64, 1, tb0:tb0 + tbn], ALU.mult)
                P.tt("dve", sb_[0:64, tb0:tb0 + tbn], tmp[0:64, 0:tbn], st32[0][0:64, 0:tbn], ALU.add)
            else:
                P.copy("dve", sb_[0:64, tb0:tb0 + tbn], pa)
        P.dma("sp", S["krT"][:, :], sb_[0:64, :])

        n = 0
        for (c0, dst, scale) in [(C_QNA, S["qnaT"], NA_SCALE), (C_KNA, S["knaT"], 1.0)]:
            w = load_w(c0, 512)
            for cc in range(4):
                sb_ = stb[n % 2]
                n += 1
                for (tb0, tbn) in tblocks:
                    ps = proj_fm(w, cc * 128, 128, tb0, tbn)
                    P.act(sb_[:, tb0:tb0 + tbn], ps, AF.Identity, scale=scale)
                P.dma("sp", dst[cc * 128:(cc + 1) * 128, :], sb_[:, :])

        w = load_w(C_VNA, 512)
        for tile in range(NTOK // 128):
            ps = bk(tile % 2)
            for kc in range(16):
                P.mm(ps[:, :], mT[:, kc, tile * 128:(tile + 1) * 128], w[:, kc, :], start=(kc == 0), stop=(kc == 15))
            vs = vst[tile % 2]
            P.copy("act" if tile % 2 == 0 else "dve", vs[:, :], ps[:, :])
            P.dma("sp", S["vna"][tile * 128:(tile + 1) * 128, :], vs[:, :])

    def sin_wrapped(dst, arg, tmpv):
        for _ in range(2):
            P.ts("dve", tmpv, arg, math.pi, ALU.is_gt, -TWO_PI, ALU.mult)
            P.tt("dve", arg, arg, tmpv, ALU.add)
            P.ts("dve", tmpv, arg, -math.pi, ALU.is_lt, TWO_PI, ALU.mult)
            P.tt("dve", arg, arg, tmpv, ALU.add)
        P.act(dst, arg, AF.Sin)

    def phase_hyena(l, n, tok0, ctxmode):
        new_phase()
        nt = n // 128
        NN = 2 * n
        zt_in = I["ztabc"] if ctxmode else I["ztab"]
        negt_in = I["negtc"] if ctxmode else I["negt"]
        wgt_in = I["wgtc"] if ctxmode else I["wgt"]
        dF = I["dftFc"] if ctxmode else I["dftF"]
        dI = I["dftIc"] if ctxmode else I["dftI"]
        nb_cols = min(512, n)
        nblk = n // nb_cols

        ident = A.alloc([128], BF16, "ident")
        P.dma("sp", ident[:, :], I["ident"][:, :])
        G = A.alloc([nt, 512], BF16, "G")
        Dd = A.alloc([nt, 512], BF16, "Dd")
        zin = A.alloc([nt, 512], BF16, "zin")
        Y = A.alloc([nt, 2, 512], BF16, "Y")
        x2u = A.alloc([4, n], F32, "x2u")
        altc = A.alloc([1], BF16, "altc")
        altr = A.alloc([n], BF16, "altr")
        yny = A.alloc([512], BF16, "yny")
        wgt = A.alloc([nt], F32, "wgt")
        negt = A.alloc([nt], F32, "negt")
        P.dma("sp", altc[:, :], I["altcol"][:, :])
        P.dma("sp", altr[0:1, :], I["altrow"][:, 0:n])
        P.dma("sp", wgt[:, :], wgt_in[:, :])
        P.dma("sp", negt[:, :], negt_in[:, :])
        mark = A.off

        zt = A.alloc([n], F32, "zt")
        h1 = A.alloc([n], F32, "h1")
        h2 = A.alloc([n], F32, "h2")
        w1 = A.alloc([64], F32, "w1")
        w2 = A.alloc([64], F32, "w2")
        w3 = A.alloc([1024], F32, "w3")
        vec = A.alloc([8], F32, "vec")
        arg = A.alloc([512], F32, "arg")
        tmpv = A.alloc([512], F32, "tmpv")
        delt = A.alloc([512], F32, "delt")
        dec = A.alloc([512], F32, "dec")
        hf = A.alloc([512], F32, "hf")
        hb = A.alloc([512], F32, "hb")
        brow = A.alloc([512], F32, "brow")
        P.dma("sp", zt[0:33, :], zt_in[:, :])
        P.dma("sp", w1[0:33, :], V(I["hy_f_w1"], I["hy_f_w1"].ap[l]))
        P.dma("sp", w2[0:64, :], V(I["hy_f_w2"], I["hy_f_w2"].ap[l]))
        P.dma("sp", w3[0:64, :], V(I["hy_f_w3"], I["hy_f_w3"].ap[l]))
        for i, nm in enumerate(["hy_f_freq", "hy_f_b1", "hy_f_b2"]):
            load_cols(vec[0:64, i:i + 1], V(I[nm], I[nm].ap[l:l + 1, :]), 1, 64, ident)
        P.tt("dve", vec[0:64, 3:4], vec[0:64, 0:1], vec[0:64, 1:2], ALU.mult)
        P.tt("dve", vec[0:64, 4:5], vec[0:64, 0:1], vec[0:64, 2:3], ALU.mult)
        P.dma("sp", delt[:, :], I["deltab"][:, :])
        P.dma("sp", brow[0:1, :], V(I["hy_bias"], I["hy_bias"].ap[l:l + 1, :]))
        for (wm, kdim, src, dst, bcol) in [(w1, 33, zt, h1, 3), (w2, 64, h1, h2, 4)]:
            for b in range(nblk):
                ps = bk(b % 2)
                cs = slice(b * nb_cols, (b + 1) * nb_cols)
                P.mm(ps[0:64, 0:nb_cols], wm[0:kdim, 0:64], src[0:kdim, cs])
                P.ts("dve", arg[0:64, 0:nb_cols], ps[0:64, 0:nb_cols], vec[0:64, 0:1], ALU.mult, vec[0:64, bcol:bcol + 1], ALU.add)
                sin_wrapped(dst[0:64, cs], arg[0:64, 0:nb_cols], tmpv[0:64, 0:nb_cols])
        for jt in range(nt):
            pf, pb_ = bk(2), bk(3)
            P.mm(pf[:, :], h2[0:64, jt * 128:(jt + 1) * 128], w3[0:64, 0:512])
            P.mm(pb_[:, :], h2[0:64, jt * 128:(jt + 1) * 128], w3[0:64, 512:1024])
            P.act(dec[:, :], delt[:, :], AF.Exp, scale=negt[:, jt:jt + 1])
            P.tt("dve", hf[:, :], pf[:, :], dec[:, :], ALU.mult)
            P.tt("dve", hb[:, :], pb_[:, :], dec[:, :], ALU.mult)
            P.tt("dve", G[:, jt, :], hf[:, :], hb[:, :], ALU.add)
            P.tt("dve", Dd[:, jt, :], hb[:, :], hf[:, :], ALU.subtract)
            if jt == 0:
                P.tt("dve", G[0:1, 0, :], hf[0:1, :], brow[0:1, :], ALU.add)
        if "S_G" in dbg and not ctxmode:
            S["G"] = dscr("S_G", [128, nt, 512], BF16)
            P.dma("sp", S["G"][:, :, :], G[:, :, :])

        A.off = mark
        cw = A.alloc([3, 12], F32, "cw")
        cb = A.alloc([12], F32, "cb")
        load_cols(V(cw, cw.ap.rearrange("p k m -> p (k m)")),
                  V(I["hy_conv_w"], I["hy_conv_w"].ap[l].rearrange("k (m p) -> (k m) p", p=128)), 36, 128, ident)
        load_cols(cb[:, :], V(I["hy_conv_b"], I["hy_conv_b"].ap[l].rearrange("(m p) -> m p", p=128)), 12, 128, ident)
        xr = [A.alloc([n], F32, "xr%d" % i) for i in range(2)]
        uv = A.alloc([n], F32, "uv")
        u1 = A.alloc([n], F32, "u1")
        zT = A.alloc([n], BF16, "zT")

        def sconv(dst, m, src):
            P.act(dst, src[:, :], AF.Identity, scale=cw[:, 1, m:m + 1], bias=cb[:, m:m + 1])
            P.stt("dve", dst[:, 1:n], src[:, 0:n - 1], cw[:, 0, m:m + 1], dst[:, 1:n], ALU.mult, ALU.add)
            P.stt("dve", dst[:, 0:n - 1], src[:, 1:n], cw[:, 2, m:m + 1], dst[:, 0:n - 1], ALU.mult, ALU.add)

        cnt = 0
        for c in range(4):
            for part, dst in [(0, uv[:, :]), (1, u1[:, :]), (2, x2u[:, c, :])]:
                m = part * 4 + c
                x = xr[cnt % 2]
                cnt += 1
                P.dma("sp", x[:, :], S["hxT"][m * 128:(m + 1) * 128, tok0:tok0 + n])
                sconv(dst, m, x)
            P.tt("dve", zT[:, :], u1[:, :], uv[:, :], ALU.mult)
            for st in range(nt):
                pb = bk(4 + st % 4, BF16)
                P.transpose(pb[:, 0:128], zT[:, st * 128:(st + 1) * 128], ident[:, :])
                P.copy("act", zin[:, st, c * 128:(c + 1) * 128], pb[:, 0:128])

        A.off = mark
        Fb = [A.alloc([2, nt, 128], BF16, "Fb%d" % i) for i in range(2)]
        kc_ = A.alloc([512], F32, "kc")
        ks_ = A.alloc([512], F32, "ks")
        t1 = A.alloc([512], F32, "t1")
        t2 = A.alloc([512], F32, "t2")
        t3 = A.alloc([512], F32, "t3")
        t4 = A.alloc([512], F32, "t4")
        for ft in range(nt):
            Fb_ = Fb[ft % 2]
            P.dma("sp", Fb_[:, :, :, :], V(dF, dF.ap[ft]))
            b0_ = (ft % 2) * 4
            zc, zs, kcp, ksp = bk(b0_), bk(b0_ + 1), bk(b0_ + 2), bk(b0_ + 3)
            for st in range(nt):
                f, la = (st == 0), (st == nt - 1)
                P.mm(zc[:, :], Fb_[:, 0, st, :], zin[:, st, :], start=f, stop=la)
                P.mm(kcp[:, :], Fb_[:, 0, st, :], G[:, st, :], start=f, stop=la)
                P.mm(zs[:, :], Fb_[:, 1, st, :], zin[:, st, :], start=f, stop=la)
                P.mm(ksp[:, :], Fb_[:, 1, st, :], Dd[:, st, :], start=f, stop=la)
            P.act(kc_[:, :], kcp[:, :], AF.Identity, scale=wgt[:, ft:ft + 1])
            P.act(ks_[:, :], ksp[:, :], AF.Identity, scale=wgt[:, ft:ft + 1])
            P.tt("dve", t1[:, :], zc[:, :], kc_[:, :], ALU.mult)
            P.tt("dve", t2[:, :], zs[:, :], ks_[:, :], ALU.mult)
            P.tt("dve", t3[:, :], zs[:, :], kc_[:, :], ALU.mult)
            P.tt("dve", t4[:, :], zc[:, :], ks_[:, :], ALU.mult)
            P.tt("pool", Y[:, ft, 0, :], t1[:, :], t2[:, :], ALU.add)
            P.tt("pool", Y[:, ft, 1, :], t3[:, :], t4[:, :], ALU.subtract)
        zn, kn = bk(0), bk(1)
        for st in range(nt):
            P.mm(zn[0:1, :], altc[:, 0:1], zin[:, st, :], start=(st == 0), stop=(st == nt - 1))
            P.mm(kn[0:1, :], altc[:, 0:1], G[:, st, :], start=(st == 0), stop=(st == nt - 1))
        P.act(t1[0:1, :], kn[0:1, :], AF.Identity, scale=1.0 / NN)
        P.tt("dve", yny[0:1, :], zn[0:1, :], t1[0:1, :], ALU.mult)

        A.off = mark
        Ib = [A.alloc([2, n], BF16, "Ib%d" % i) for i in range(2)]
        ost = [A.alloc([n], BF16, "ost%d" % i) for i in range(2)]
        ntb = n // nb_cols
        per_pass = max(1, 8 // ntb)
        per_pass = min(per_pass, 4)
        for p0 in range(0, 4, per_pass):
            cl_list = list(range(p0, min(4, p0 + per_pass)))
            for ft in range(nt):
                Ib_ = Ib[ft % 2]
                P.dma("sp", Ib_[:, :, :], V(dI, dI.ap[ft]))
                for ci, c in enumerate(cl_list):
                    for tb in range(ntb):
                        acc = bk(ci * ntb + tb)
                        ts_ = slice(tb * nb_cols, (tb + 1) * nb_cols)
                        P.mm(acc[:, 0:nb_cols], Y[:, ft, 0, c * 128:(c + 1) * 128], Ib_[:, 0, ts_], start=(ft == 0), stop=False)
                        P.mm(acc[:, 0:nb_cols], Y[:, ft, 1, c * 128:(c + 1) * 128], Ib_[:, 1, ts_], start=False, stop=False)
            for ci, c in enumerate(cl_list):
                o = ost[c % 2]
                for tb in range(ntb):
                    acc = bk(ci * ntb + tb)
                    ts_ = slice(tb * nb_cols, (tb + 1) * nb_cols)
                    P.mm(acc[:, 0:nb_cols], yny[0:1, c * 128:(c + 1) * 128], altr[0:1, ts_], start=False, stop=True)
                    P.tt("dve", o[:, ts_], acc[:, 0:nb_cols], x2u[:, c, ts_], ALU.mult)
                P.dma("pool", S["catT"][c * 128:(c + 1) * 128, tok0:tok0 + n], o[:, :])

    def attn_fin_a(po, obuf, rcp):
        P.recip(rcp[:, :], po[:, 128:129])
        P.act(obuf[:, :], po[:, 0:128], AF.Identity, scale=rcp[:, :])

    def attn_fin_b(obuf, dst_col, ident, stage):
        pt = bk(7, BF16)
        P.transpose(pt[:, 0:128], obuf[:, :], ident[:, :])
        P.copy("dve", stage[:, dst_col:dst_col + 128], pt[:, 0:128])

    def phase_mla(l, with_ctx_q, side_fn=None):
        new_phase()
        nq_tot = NTOK if with_ctx_q else NX
        ident = A.alloc([128], BF16, "ident")
        P.dma("sp", ident[:, :], I["ident"][:, :])
        cq = A.alloc([6, NTOK], BF16, "cq")
        ckv = A.alloc([4, NTOK], BF16, "ckv")
        kr = A.alloc([NTOK], BF16, "kr")
        rope = A.alloc([2, NX], F32, "ropeq")
        P.dma("sp", cq[:, :, :], V(S["cqT"], S["cqT"].ap.rearrange("(c p) t -> p c t", p=128)))
        P.dma("sp", ckv[:, :, :], V(S["ckvT"], S["ckvT"].ap.rearrange("(c p) t -> p c t", p=128)))
        P.dma("sp", kr[0:64, :], S["krT"][:, :])
        P.dma("sp", rope[0:64, :, :], I["rope"][:, 2:4, :])
        wq = [A.alloc([6, 256], BF16, "wq%d" % i) for i in range(2)]
        wkv = [A.alloc([4, 256], BF16, "wkv%d" % i) for i in range(2)]
        qn = A.alloc([NTOK], BF16, "qn")
        qr = A.alloc([NTOK], BF16, "qr")
        kn = A.alloc([NTOK], BF16, "kn")
        vv = A.alloc([18, 132], BF16, "vv")
        P.memset("dve", vv[:, :, 128:129], 1.0)
        PT = [A.alloc([18, 512], BF16, "PT%d" % i) for i in range(2)]
        tA = A.alloc([512], F32, "tA")
        tB = A.alloc([512], F32, "tB")
        obufs = [A.alloc([128], BF16, "obuf%d" % i) for i in range(4)]
        rcps = [A.alloc([1], F32, "rcp%d" % i) for i in range(4)]
        fcnt = [0]
        pend_fin = [None]
        stage = [A.alloc([NTOK], BF16, "ostage%d" % i) for i in range(2)]
        side = side_fn(A.off) if side_fn is not None else None
        uq_src = I["mla_w_uq"].ap[l].rearrange("(kc p) n -> p kc n", p=128)
        ukv_src = I["mla_w_ukv"].ap[l].rearrange("(kc p) n -> p kc n", p=128)
        tblocks = [(i * 512, 512) for i in range(4)] + ([(2048, 256)] if with_ctx_q else [])
        kblocks = [(i * 512, 512) for i in range(4)] + [(2048, 256)]
        pc = [0]
        ptc = [0]

        def pbank():
            pc[0] += 1
            return bk(pc[0] % 2)

        for h in range(MLA_H):
            wq_, wkv_ = wq[h % 2], wkv[h % 2]
            q0 = h * 192
            P.dma("pool", wq_[:, :, 0:192], V(I["mla_w_uq"], uq_src[:, :, q0:q0 + 192]))
            P.dma("pool", wq_[:, :, 192:224], V(I["mla_w_uq"], uq_src[:, :, q0 + 160:q0 + 192]))
            P.dma("pool", wq_[:, :, 224:256], V(I["mla_w_uq"], uq_src[:, :, q0 + 128:q0 + 160]))
            P.dma("pool", wkv_[:, :, :], V(I["mla_w_ukv"], ukv_src[:, :, h * 256:(h + 1) * 256]))
            for (tb0, tbn) in tblocks:
                ps = pbank()
                for kc in range(6):
                    P.mm(ps[:, 0:tbn], wq_[:, kc, 0:128], cq[:, kc, tb0:tb0 + tbn], start=(kc == 0), stop=(kc == 5))
                P.act(qn[:, tb0:tb0 + tbn], ps[:, 0:tbn], AF.Identity, scale=MLA_SCALE)
                pa = pbank()
                for kc in range(6):
                    P.mm(pa[0:64, 0:tbn], wq_[:, kc, 128:192], cq[:, kc, tb0:tb0 + tbn], start=(kc == 0), stop=(kc == 5))
                if tb0 < NX:
                    pb_ = pbank()
                    for kc in range(6):
                        P.mm(pb_[0:64, 0:tbn], wq_[:, kc, 192:256], cq[:, kc, tb0:tb0 + tbn], start=(kc == 0), stop=(kc == 5))
                    P.tt("dve", tA[0:64, 0:tbn], pa[0:64, 0:tbn], rope[0:64, 0, tb0:tb0 + tbn], ALU.mult)
                    P.tt("dve", tB[0:64, 0:tbn], pb_[0:64, 0:tbn], rope[0:64, 1, tb0:tb0 + tbn], ALU.mult)
                    P.tt("dve", qr[0:64, tb0:tb0 + tbn], tA[0:64, 0:tbn], tB[0:64, 0:tbn], ALU.add)
                else:
                    P.act(qr[0:64, tb0:tb0 + tbn], pa[0:64, 0:tbn], AF.Identity, scale=MLA_SCALE)
            for (tb0, tbn) in kblocks:
                ps = pbank()
                for kc in range(4):
                    P.mm(ps[:, 0:tbn], wkv_[:, kc, 0:128], ckv[:, kc, tb0:tb0 + tbn], start=(kc == 0), stop=(kc == 3))
                P.copy("act", kn[:, tb0:tb0 + tbn], ps[:, 0:tbn])
            for kt in range(18):
                ps = pbank()
                for kc in range(4):
                    P.mm(ps[:, 0:128], ckv[:, kc, kt * 128:(kt + 1) * 128], wkv_[:, kc, 128:256], start=(kc == 0), stop=(kc == 3))
                P.copy("dve", vv[:, kt, 0:128], ps[:, 0:128])
            stg = stage[h % 2]
            jobs = [(qb * 512, 512, list(range(18))) for qb in range(4)]
            if with_ctx_q:
                jobs.append((2048, 256, [16, 17]))
            def qk_stage(job):
                (q0_, qn_, ktiles) = job
                pt_ = PT[ptc[0] % 2]
                ptc[0] += 1
                for ki, kt in enumerate(ktiles):
                    ps = bk(2 + ki % 3)
                    P.mm(ps[:, 0:qn_], kn[:, kt * 128:(kt + 1) * 128], qn[:, q0_:q0_ + qn_], start=True, stop=False)
                    P.mm(ps[:, 0:qn_], kr[0:64, kt * 128:(kt + 1) * 128], qr[0:64, q0_:q0_ + qn_], start=False, stop=True)
                    P.act(pt_[:, ki, 0:qn_], ps[:, 0:qn_], AF.Exp)
                return pt_

            def pv_stage(job, pt_):
                (q0_, qn_, ktiles) = job
                for qb in range(qn_ // 128):
                    po = bk(5 + qb % 2)
                    for ki, kt in enumerate(ktiles):
                        P.mm(po[:, 0:129], pt_[:, ki, qb * 128:(qb + 1) * 128], vv[:, kt, 0:129],
                             start=(ki == 0), stop=(ki == len(ktiles) - 1))
                    ob = obufs[fcnt[0] % 4]
                    attn_fin_a(po, ob, rcps[fcnt[0] % 4])
                    fcnt[0] += 1
                    if pend_fin[0] is not None:
                        attn_fin_b(*pend_fin[0])
                    pend_fin[0] = (ob, q0_ + qb * 128, ident, stg)

            prev = None
            for job in jobs:
                cur = qk_stage(job)
                if prev is not None:
                    pv_stage(*prev)
                prev = (job, cur)
                if side is not None:
                    next(side, None)
            pv_stage(*prev)
            attn_fin_b(*pend_fin[0])
            pend_fin[0] = None
            r0 = 512 + h * 128
            P.dma("sp", S["catT"][r0:r0 + 128, 0:nq_tot], stg[:, 0:nq_tot])
        if side is not None:
            for _ in side:
                pass

    def phase_na(l, with_ctx_q, side=None):
        new_phase()
        nq_tot = NTOK if with_ctx_q else NX
        ident = A.alloc([128], BF16, "ident")
        P.dma("sp", ident[:, :], I["ident"][:, :])
        mask = A.alloc([3200], F32, "mask")
        P.dma("sp", mask[:, :], I["namask"][:, :])
        braw = A.alloc([3200], F32, "braw")
        bias = [A.alloc([5, 5, 128], BF16, "bias%d" % i) for i in range(2)]
        qT = [A.alloc([NTOK], BF16, "qT%d" % i) for i in range(2)]
        kT = [A.alloc([NTOK], BF16, "kT%d" % i) for i in range(2)]
        vv = [A.alloc([18, 132], BF16, "vv%d" % i) for i in range(2)]
        for i in range(2):
            P.memset("dve", vv[i][:, :, 128:129], 1.0)
        PT = [A.alloc([7, 128], BF16, "PT%d" % i) for i in range(2)]
        obufs = [A.alloc([128], BF16, "obuf%d" % i) for i in range(4)]
        rcps = [A.alloc([1], F32, "rcp%d" % i) for i in range(4)]
        fcnt = [0]
        pend_fin = [None]

        def fin(po, dst_col, stg_):
            ob = obufs[fcnt[0] % 4]
            attn_fin_a(po, ob, rcps[fcnt[0] % 4])
            fcnt[0] += 1
            if pend_fin[0] is not None:
                attn_fin_b(*pend_fin[0])
            pend_fin[0] = (ob, dst_col, ident, stg_)

        def fin_flush():
            if pend_fin[0] is not None:
                attn_fin_b(*pend_fin[0])
            pend_fin[0] = None

        stage = [A.alloc([NTOK], BF16, "ostage%d" % i) for i in range(2)]
        vsrc = S["vna"].ap.rearrange("(t p) c -> p t c", p=128)
        gi = 0
        def na_load(h):
            q_, k_, v_, b_ = qT[h % 2], kT[h % 2], vv[h % 2], bias[h % 2]
            P.dma("sp", q_[:, :], S["qnaT"][h * 128:(h + 1) * 128, :])
            P.dma("sp", k_[:, :], S["knaT"][h * 128:(h + 1) * 128, :])
            P.dma("sp", v_[:, :, 0:128], V(S["vna"], vsrc[:, :, h * 128:(h + 1) * 128]))
            P.dma("sp", braw[:, :], V(I["rpbg"], I["rpbg"].ap[l, h]))
            P.tt("pool", V(b_, b_.ap.rearrange("p a b c -> p (a b c)")), braw[:, :], mask[:, :], ALU.add)

        na_load(0)
        for h in range(NA_H):
            q_, k_, v_, b_ = qT[h % 2], kT[h % 2], vv[h % 2], bias[h % 2]
            if h + 1 < NA_H:
                na_load(h + 1)
            stg = stage[h % 2]
            def na_qk(i):
                cls, j0 = _na_cls(i), _na_j0(i)
                g = gic[0]
                gic[0] += 1
                pa, pb_ = bk(2 * (g % 2)), bk(2 * (g % 2) + 1)
                pt_ = PT[g % 2]
                qs = q_[:, i * 128:(i + 1) * 128]
                ktiles = [j0 + c for c in range(5)] + [16, 17]
                for c in range(7):
                    dst = pa[:, c * 128:(c + 1) * 128] if c < 4 else pb_[:, (c - 4) * 128:(c - 3) * 128]
                    kt = ktiles[c]
                    if c < 5:
                        P.mm(dst, k_[:, kt * 128:(kt + 1) * 128], qs, start=True, stop=False)
                        P.mm(dst, ident[:, :], b_[:, cls, c, :], start=False, stop=True)
                    else:
                        P.mm(dst, k_[:, kt * 128:(kt + 1) * 128], qs, start=True, stop=True)
                P.act(V(pt_, pt_.ap[:, 0:4, :].rearrange("p a b -> p (a b)")), pa[:, 0:512], AF.Exp)
                P.act(V(pt_, pt_.ap[:, 4:7, :].rearrange("p a b -> p (a b)")), pb_[:, 0:384], AF.Exp)
                return (i, g, pt_, ktiles)

            def na_pv(i, g, pt_, ktiles):
                po = bk(4 + g % 2)
                for c in range(7):
                    P.mm(po[:, 0:129], pt_[:, c, :], v_[:, ktiles[c], 0:129], start=(c == 0), stop=(c == 6))
                fin(po, i * 128, stg)

            gic = [gi]
            prev = None
            for i in range(16):
                cur = na_qk(i)
                if prev is not None:
                    na_pv(*prev)
                prev = cur
                if side is not None and i % 2 == 1:
                    next(side, None)
            na_pv(*prev)
            gi = gic[0]
            if with_ctx_q:
                for qt in (16, 17):
                    pa = bk(2 * (gi % 2))
                    pt_ = PT[gi % 2]
                    gi += 1
                    for c, kt in enumerate((16, 17)):
                        P.mm(pa[:, c * 128:(c + 1) * 128], k_[:, kt * 128:(kt + 1) * 128], q_[:, qt * 128:(qt + 1) * 128])
                    P.act(V(pt_, pt_.ap[:, 0:2, :].rearrange("p a b -> p (a b)")), pa[:, 0:256], AF.Exp)
                    po = bk(4 + gi % 2)
                    for c, kt in enumerate((16, 17)):
                        P.mm(po[:, 0:129], pt_[:, c, :], v_[:, kt, 0:129], start=(c == 0), stop=(c == 1))
                    fin(po, qt * 128, stg)
            fin_flush()
            r0 = 1536 + h * 128
            P.dma("sp", S["catT"][r0:r0 + 128, 0:nq_tot], stg[:, 0:nq_tot])
        if side is not None:
            for _ in side:
                pass

    def post_residual(y_views, Ab, xres_src, dst, sqj, ssv, rs, xt, ot, tmpc=None, preloaded=False, stq="pool"):
        P.memset("dve", ssv[:, 0:4], 0.0)
        for q, yv in enumerate(y_views):
            P.act(sqj[:, :], yv, AF.Square, accum=ssv[:, q:q + 1])
        P.tt("dve", ssv[:, 4:5], ssv[:, 0:1], ssv[:, 1:2], ALU.add)
        P.tt("dve", ssv[:, 5:6], ssv[:, 2:3], ssv[:, 3:4], ALU.add)
        P.tt("dve", ssv[:, 6:7], ssv[:, 4:5], ssv[:, 5:6], ALU.add)
        rstd_from_ss(ssv[:, 6:7], rs[:, :], D)
        if not preloaded:
            P.dma("sp", xt[:, :], xres_src)
        for q, yv in enumerate(y_views):
            cs = slice(q * 512, (q + 1) * 512)
            if ot is None:
                tc_ = tmpc[q % 2]
                P.stt("dve", tc_[:, :], yv, rs[:, :], Ab[:, cs], ALU.mult, ALU.mult)
                P.tt("pool", xt[:, cs], tc_[:, :], xt[:, cs], ALU.add)
            else:
                P.stt("dve", ot[:, cs], yv, rs[:, :], Ab[:, cs], ALU.mult, ALU.mult)
                P.tt("pool", ot[:, cs], ot[:, cs], xt[:, cs], ALU.add)
        P.dma(stq, dst, (xt if ot is None else ot)[:, :])

    def phase_outproj(l, xsrc, tiles):
        new_phase()
        wo = A.alloc([16, D], BF16, "wo")
        wsrc = I["w_out"].ap[l].rearrange("(kc p) n -> p kc n", p=128)
        for q in range(4):
            P.dma("pool", wo[:, :, q * 512:(q + 1) * 512], V(I["w_out"], wsrc[:, :, q * 512:(q + 1) * 512]))
        Ab = [A.alloc([D], F32, "A2_%d" % k) for k in range(2)]
        for k in range(2):
            load_dv(l, k, 2, Ab[k][:, :])
        cat = [A.alloc([16, 128], BF16, "cat%d" % i) for i in range(2)]
        xt = [A.alloc([D], F32, "xt%d" % i) for i in range(2)]
        ot = [A.alloc([D], F32, "ot%d" % i) for i in range(2)]
        sqj = A.alloc([512], BF16, "sqj")
        ssv = [A.alloc([8], F32, "ssv%d" % i) for i in range(2)]
        rs = [A.alloc([1], F32, "rs%d" % i) for i in range(2)]
        csrc = S["catT"].ap.rearrange("(kc p) t -> p kc t", p=128)
        for n, tile in enumerate(tiles):
            k = 0 if tile < 16 else 1
            c_ = cat[n % 2]
            P.dma("sp", c_[:, :, :], V(S["catT"], csrc[:, :, tile * 128:(tile + 1) * 128]))
            P.dma("sp", xt[n % 2][:, :], xsrc[tile * 128:(tile + 1) * 128, :])
            ys = []
            for q in range(4):
                ps = bk((n % 2) * 4 + q)
                for kc in range(16):
                    P.mm(ps[:, :], c_[:, kc, :], wo[:, kc, q * 512:(q + 1) * 512], start=(kc == 0), stop=(kc == 15))
                ys.append(ps[:, :])
            rows = slice(tile * 128, (tile + 1) * 128)
            post_residual(ys, Ab[k], xsrc[rows, :], S["X1"][rows, :], sqj, ssv[n % 2], rs[n % 2], xt[n % 2], ot[n % 2],
                          preloaded=True)

    def phase_ffn(l, blocks, final):
        new_phase()
        Ab = [A.alloc([D], F32, "A4_%d" % k) for k in range(2)]
        for k in range(2):
            load_dv(l, k, 5, Ab[k][:, :])
        TBMAX = max(b[1] for b in blocks)
        hT = A.alloc([44, TBMAX], BF16, "hT")
        xt = [A.alloc([D], F32, "xt%d" % i) for i in range(2)]
        tmpc = [A.alloc([512], F32, "tmpc%d" % i) for i in range(2)]
        sqj = A.alloc([512], BF16, "sqj")
        ssv = [A.alloc([8], F32, "ssv%d" % i) for i in range(2)]
        rs = [A.alloc([1], F32, "rs%d" % i) for i in range(2)]
        mT = A.alloc([16, TBMAX], BF16, "mT")
        ssq = A.alloc([32], F32, "ssq")
        mark = A.off
        msrc = S["modT"].ap.rearrange("(kc p) t -> p kc t", p=128)

        def load_mT(t0_, tbn_):
            for q in range(4):
                P.dma("sp", mT[:, 4 * q:4 * q + 4, 0:tbn_], V(S["modT"], msrc[:, 4 * q:4 * q + 4, t0_:t0_ + tbn_]))

        load_mT(blocks[0][0], blocks[0][1])
        gsrc = I["w_ffn_gate"].ap[l].rearrange("(kc p) n -> p kc n", p=128)
        usrc = I["w_ffn_up"].ap[l].rearrange("(kc p) n -> p kc n", p=128)
        dsrc = I["w_ffn_down"].ap[l].rearrange("(fc p) n -> p fc n", p=128)
        WSZ = 16 * 512 * 2 // 4
        off_pair = [mark, mark + 2 * WSZ]
        off_sg = mark + 4 * WSZ
        post_jobs = []

        def alloc_at(off, shape, dt, name):
            A.off = off
            return A.alloc(shape, dt, name)

        for bi, (t0, tbn, sub) in enumerate(blocks):
            ntile = tbn // 128
            sg = [alloc_at(off_sg + i * 512, [512], F32, "sg%d" % i) for i in range(2)]
            pairs = {}
            pcn = 0
            for fg in range(11):
                par = fg % 2
                if par not in pairs or fg < 2:
                    pairs[par] = (alloc_at(off_pair[par], [16, 512], BF16, "wg%d" % par),
                                  alloc_at(off_pair[par] + WSZ, [16, 512], BF16, "wu%d" % par))
                g_, u_ = pairs[par]
                P.dma("pool", g_[:, :, :], V(I["w_ffn_gate"], gsrc[:, :, fg * 512:(fg + 1) * 512]))
                P.dma("pool", u_[:, :, :], V(I["w_ffn_up"], usrc[:, :, fg * 512:(fg + 1) * 512]))
                for q4 in range(4):
                    fc = fg * 4 + q4
                    for (s0, sn) in sub:
                        pg, pu = bk((pcn % 2) * 2), bk((pcn % 2) * 2 + 1)
                        sg_ = sg[pcn % 2]
                        pcn += 1
                        cs = slice(s0, s0 + sn)
                        for kc in range(16):
                            P.mm(pg[:, 0:sn], g_[:, kc, q4 * 128:(q4 + 1) * 128], mT[:, kc, cs], start=(kc == 0), stop=(kc == 15))
                        for kc in range(16):
                            P.mm(pu[:, 0:sn], u_[:, kc, q4 * 128:(q4 + 1) * 128], mT[:, kc, cs], start=(kc == 0), stop=(kc == 15))
                        P.act(sg_[:, 0:sn], pg[:, 0:sn], AF.Silu)
                        P.tt("dve", hT[:, fc, cs], sg_[:, 0:sn], pu[:, 0:sn], ALU.mult)
                        if fg == 0:
                            for _ in range(2):
                                if post_jobs:
                                    post_jobs.pop(0)()
                if fg == 0:
                    while post_jobs:
                        post_jobs.pop(0)()
            if bi + 1 < len(blocks):
                load_mT(blocks[bi + 1][0], blocks[bi + 1][1])
            wd = [alloc_at(off_pair[1] + i * 1024, [4, 512], BF16, "wd%d" % i) for i in range(2)]
            ybuf = alloc_at(off_pair[1] + 2048, [ntile, D], BF16, "ybuf")
            assert A.off <= off_sg
            dcn = 0
            P.memset("pool", ssq[:, :], 0.0)
            for nq in range(4):
                for f4 in range(11):
                    d_ = wd[dcn % 2]
                    dcn += 1
                    P.dma("pool", d_[:, :, :], V(I["w_ffn_down"], dsrc[:, f4 * 4:(f4 + 1) * 4, nq * 512:(nq + 1) * 512]))
                    for fi in range(4):
                        fc = f4 * 4 + fi
                        for tl in range(ntile):
                            P.mm(bk((nq * ntile + tl) % 8)[:, :], hT[:, fc, tl * 128:(tl + 1) * 128], d_[:, fi, :],
                                 start=(fc == 0), stop=(fc == 43))
                for tl in range(ntile):
                    k_ = 0 if (t0 + tl * 128) < NX else 1
                    acc_ = bk((nq * ntile + tl) % 8)
                    P.act(sqj[:, :], acc_[:, :], AF.Square, accum=ssq[:, tl * 4 + nq:tl * 4 + nq + 1])
                    P.tt("dve", ybuf[:, tl, nq * 512:(nq + 1) * 512], acc_[:, :], Ab[k_][:, nq * 512:(nq + 1) * 512], ALU.mult)

            def mk_job(tl, t0=t0, ntile=ntile, ybuf=ybuf):
                def job():
                    tok = t0 + tl * 128
                    k = 0 if tok < NX else 1
                    ys = [ybuf[:, tl, q * 512:(q + 1) * 512] for q in range(4)]
                    rows = slice(tok, tok + 128)
                    dst = OUT[rows, :] if final else S["X2"][rows, :]
                    if tl == 0:
                        P.dma("sp", xt[0][:, :], S["X1"][t0:t0 + 128, :])
                    if tl + 1 < ntile:
                        P.dma("sp", xt[(tl + 1) % 2][:, :], S["X1"][tok + 128:tok + 256, :])
                    sv, r_, x_ = ssv[tl % 2], rs[tl % 2], xt[tl % 2]
                    P.tt("dve", sv[:, 4:5], ssq[:, tl * 4:tl * 4 + 1], ssq[:, tl * 4 + 1:tl * 4 + 2], ALU.add)
                    P.tt("dve", sv[:, 5:6], ssq[:, tl * 4 + 2:tl * 4 + 3], ssq[:, tl * 4 + 3:tl * 4 + 4], ALU.add)
                    P.tt("dve", sv[:, 6:7], sv[:, 4:5], sv[:, 5:6], ALU.add)
                    rstd_from_ss(sv[:, 6:7], r_[:, :], D)
                    P.stt("dve", x_[:, :], ybuf[:, tl, :], r_[:, :], x_[:, :], ALU.mult, ALU.add)
                    P.dma("sp", dst, x_[:, :])
                return job

            post_jobs = [mk_job(tl) for tl in range(ntile)]
        while post_jobs:
            post_jobs.pop(0)()

    SUB768 = [(0, 512), (512, 256)]
    all_tiles = list(range(18))
    x_tiles = list(range(16))
    for l in range(n_layers):
        last = (l == DEPTH - 1)
        src = I["xall"] if l == 0 else S["X2"]
        if want("adaln") and l == 0:
            phase_adaln(l)
        MT_OFF = 52000 - 18432 - 64
        A.reset(MT_OFF)
        mTd = A.alloc([16, NTOK], BF16, "mTd")
        nbase = 0
        if want("norm1"):
            phase_norm(l, src, 0, 1, all_tiles, direct=mTd, base=nbase)
        if want("inproj"):
            phase_inproj(l, mTd, nbase)
        if want("hyena"):
            phase_hyena(l, NX, 0, False)
            if not last:
                phase_hyena(l, NCTX, NX, True)
        if want("mla"):
            sf = (lambda base, l=l: adaln_gen(l + 1, base, bank=7, nrow=1)) if (l + 1 < n_layers and want("adaln")) else None
            phase_mla(l, not last, sf)
        if want("na"):
            phase_na(l, not last, None)
        if want("outproj"):
            phase_outproj(l, src, x_tiles if last else all_tiles)
        if want("ffn"):
            phase_norm(l, S["X1"], 3, 4, x_tiles if last else all_tiles)
            if last:
                phase_ffn(l, [(0, 768, SUB768), (768, 768, SUB768), (1536, 512, [(0, 512)])], True)
            else:
                phase_ffn(l, [(0, 768, SUB768), (768, 768, SUB768), (1536, 768, SUB768)], False)
    P.barrier()
    P.emit()
    es.close()
    return nc


_WEIGHT_NAMES = ["w_ada", "b_ada", "g_attn_pre", "g_attn_post", "g_ffn_pre", "g_ffn_post", "w_in", "hy_conv_w",
                 "hy_conv_b", "hy_f_w1", "hy_f_b1", "hy_f_w2", "hy_f_b2", "hy_f_w3", "hy_f_freq", "hy_bias",
                 "mla_g_q", "mla_w_uq", "mla_g_kv", "mla_w_ukv", "w_out", "w_ffn_gate", "w_ffn_up", "w_ffn_down"]


def make_in_maps(inputs, cores):
    consts = _const_tables()
    shared = {k: np.ascontiguousarray(np.asarray(inputs[k], dtype=np.float32)) for k in _WEIGHT_NAMES}
    shared["rpbg"] = _na_gather(np.asarray(inputs["na_rpb"], dtype=np.float32))
    shared.update(consts)
    maps = []
    for b in cores:
        m = dict(shared)
        m["xall"] = np.ascontiguousarray(np.concatenate([inputs["x"][b], inputs["ctx"][b]], axis=0).astype(np.float32))
        m["cc"] = np.ascontiguousarray(np.stack([inputs["c"][b], inputs["c_ctx"]], axis=0).astype(np.float32))
        maps.append(m)
    return maps


def kernel(**inputs):
    inputs = {k: np.asarray(v) for k, v in inputs.items()}
    nc = build_program()
    maps = make_in_maps(inputs, list(range(N_CORES)))
    res = run_bass_kernel_spmd(nc, maps, core_ids=list(range(N_CORES)))
    return np.stack([np.asarray(r["out"], dtype=np.float32) for r in res.results], axis=0)
```

```python
import math
import contextlib
import numpy as np
import ml_dtypes
import concourse.bass as bass
import concourse.mybir as mybir
from concourse.bass_utils import run_bass_kernel_spmd

F32 = mybir.dt.float32
BF16 = mybir.dt.bfloat16
AF = mybir.ActivationFunctionType
ALU = mybir.AluOpType

N_DMA_SEMS = 40
N_CORES = 4

D = 2048
NX = 2048
NCTX = 256
NTOK = NX + NCTX
DEPTH = 2
GRID_W = 64
HY_W = 512
Q_LORA = 768
KV_LORA = 512
MLA_H = 8
NA_H = 4
FFN = 5632
IN_COLS = 4416
MLA_SCALE = (128 + 64) ** -0.5
NA_SCALE = 128 ** -0.5
EPS = 1e-6
C_HX, C_CQ, C_CKV, C_KR, C_QNA, C_KNA, C_VNA = 0, 1536, 2304, 2816, 2880, 3392, 3904
MASKV = -30000.0
TWO_PI = 2.0 * math.pi


class V:
    __slots__ = ("t", "ap")

    def __init__(self, t, ap):
        self.t = t
        self.ap = ap

    def __getitem__(self, idx):
        return V(self.t, self.ap[idx])


class T:
    def __init__(self, ap, name=""):
        self.ap = ap
        self.name = name
        self.w = None
        self.r = []
        self.excl = False

    def __getitem__(self, idx):
        return V(self, self.ap[idx])

    @property
    def v(self):
        return V(self, self.ap)


class Op:
    __slots__ = ("eng", "fn", "deps", "idx", "is_dma", "flag", "semval", "dsem", "dval", "nop")

    def __init__(self, eng, fn, deps, idx, is_dma):
        self.eng = eng
        self.fn = fn
        self.deps = deps
        self.idx = idx
        self.is_dma = is_dma
        self.flag = False
        self.semval = 0
        self.dsem = None
        self.dval = 0
        self.nop = False


ENGS = ["pe", "act", "dve", "pool", "sp"]


class Prog:
    def __init__(self, nc):
        self.nc = nc
        self.ops = {e: [] for e in ENGS}
        self.n_dma = 0
        self.dma_last = [None] * N_DMA_SEMS
        self.dma_cnt = [0] * N_DMA_SEMS
        self.dma_since_bar = []

    def add(self, eng, fn, reads=(), writes=(), is_dma=False):
        deps = set()
        for v in reads:
            t = v.t
            if t.w is not None:
                deps.add(t.w)
            if t.excl:
                deps.update(r for r in t.r if r.eng != eng)
        for v in writes:
            t = v.t
            if t.w is not None:
                deps.add(t.w)
            deps.update(t.r)
        op = Op(eng, fn, deps, len(self.ops[eng]), is_dma)
        if is_dma:
            s = self.n_dma % N_DMA_SEMS
            self.n_dma += 1
            if self.dma_last[s] is not None:
                op.deps.add(self.dma_last[s])
            self.dma_last[s] = op
            self.dma_cnt[s] += 1
            op.dsem = s
            op.dval = 16 * self.dma_cnt[s]
            self.dma_since_bar.append(op)
        for v in reads:
            v.t.r.append(op)
        for v in writes:
            v.t.w = op
            v.t.r = []
        self.ops[eng].append(op)
        return op

    def wait_ops(self, eng, toks):
        op = self.add(eng, lambda e: None)
        op.nop = True
        op.deps.update(toks)
        return op

    def barrier(self):
        toks = []
        for e in ENGS:
            for op in reversed(self.ops[e]):
                if not op.is_dma and not op.nop:
                    toks.append(op)
                    break
        toks.extend(self.dma_since_bar)
        self.dma_since_bar = []
        for e in ENGS:
            self.wait_ops(e, toks)

    def mm(self, out, lhsT, rhs, start=True, stop=True):
        return self.add("pe", lambda e: e.matmul(out.ap, lhsT.ap, rhs.ap, start=start, stop=stop),
                        reads=[lhsT, rhs], writes=[out])

    def transpose(self, out, in_, ident):
        return self.add("pe", lambda e: e.transpose(out.ap, in_.ap, ident.ap),
                        reads=[in_, ident], writes=[out])

    def act(self, out, in_, func, bias=None, scale=None, accum=None):
        reads = [in_]
        writes = [out]
        kw = {}
        if bias is not None:
            if isinstance(bias, V):
                reads.append(bias)
                kw["bias"] = bias.ap
            else:
                kw["bias"] = float(bias)
        if scale is not None:
            if isinstance(scale, V):
                reads.append(scale)
                kw["scale"] = scale.ap
            else:
                kw["scale"] = float(scale)
        if accum is not None:
            writes.append(accum)
            kw["accum_out"] = accum.ap
        return self.add("act", lambda e: e.activation(out.ap, in_.ap, func, **kw), reads=reads, writes=writes)

    def tt(self, eng, out, in0, in1, op):
        return self.add(eng, lambda e: e.tensor_tensor(out.ap, in0.ap, in1.ap, op), reads=[in0, in1], writes=[out])

    def ts(self, eng, out, in0, s1, op0, s2=None, op1=None):
        reads = [in0]
        a1 = s1.ap if isinstance(s1, V) else float(s1)
        if isinstance(s1, V):
            reads.append(s1)
        a2 = None
        if s2 is not None:
            a2 = s2.ap if isinstance(s2, V) else float(s2)
            if isinstance(s2, V):
                reads.append(s2)
        kw = {}
        if op1 is not None:
            kw["op1"] = op1
        return self.add(eng, lambda e: e.tensor_scalar(out.ap, in0.ap, a1, a2, op0, **kw), reads=reads, writes=[out])

    def stt(self, eng, out, in0, scalar, in1, op0, op1):
        reads = [in0, in1]
        a = scalar.ap if isinstance(scalar, V) else float(scalar)
        if isinstance(scalar, V):
            reads.append(scalar)
        return self.add(eng, lambda e: e.scalar_tensor_tensor(out.ap, in0.ap, a, in1.ap, op0, op1),
                        reads=reads, writes=[out])

    def copy(self, eng, out, in_):
        if eng == "act":
            return self.add("act", lambda e: e.copy(out.ap, in_.ap), reads=[in_], writes=[out])
        return self.add(eng, lambda e: e.tensor_copy(out.ap, in_.ap), reads=[in_], writes=[out])

    def memset(self, eng, out, val):
        return self.add(eng, lambda e: e.memset(out.ap, val), writes=[out])

    def recip(self, out, in_):
        return self.add("dve", lambda e: e.reciprocal(out.ap, in_.ap), reads=[in_], writes=[out])

    def dma(self, q, out, in_, **kw):
        return self.add(q, lambda e: e.dma_start(out.ap, in_.ap, **kw), reads=[in_], writes=[out], is_dma=True)

    def emit(self):
        nc = self.nc
        ops = self.ops
        for e in ENGS:
            for op in ops[e]:
                for d in op.deps:
                    if d.is_dma:
                        continue
                    if d.eng == e and e == "pe" and not op.is_dma:
                        continue
                    d.flag = True
        for e in ENGS:
            c = 0
            for op in ops[e]:
                if op.flag and not op.is_dma:
                    c += 1
                op.semval = c
        with contextlib.ExitStack() as es:
            esem = {e: es.enter_context(nc.semaphore("s_" + e)) for e in ENGS if e != "sp"}
            dsem = [es.enter_context(nc.semaphore("d%d" % i)) for i in range(N_DMA_SEMS)]
            block = es.enter_context(nc.Block())

            def run(e, eng):
                seen = {}
                for op in ops[e]:
                    waits = {}
                    for d in op.deps:
                        if d.is_dma:
                            key = ("d", d.dsem)
                            val = d.dval
                            sem = dsem[d.dsem]
                        else:
                            if d.eng == e and e == "pe" and not op.is_dma:
                                continue
                            if d.eng == e and d.idx >= op.idx:
                                continue
                            key = ("e", d.eng)
                            val = d.semval
                            sem = esem[d.eng]
                        if seen.get(key, 0) >= val:
                            continue
                        if key not in waits or waits[key][1] < val:
                            waits[key] = (sem, val)
                    for key, (sem, val) in waits.items():
                        eng.wait_ge(sem, val)
                        seen[key] = val
                    ins = op.fn(eng)
                    if ins is None:
                        continue
                    if op.is_dma:
                        ins.then_inc(dsem[op.dsem], 16)
                    elif op.flag:
                        ins.then_inc(esem[e], 1)

            @block.tensor
            def _(eng):
                run("pe", eng)

            @block.scalar
            def _(eng):
                run("act", eng)

            @block.vector
            def _(eng):
                run("dve", eng)

            @block.gpsimd
            def _(eng):
                run("pool", eng)

            @block.sync
            def _(eng):
                run("sp", eng)


class Arena:
    def __init__(self, ap, nwords):
        self.ap = ap
        self.n = nwords
        self.off = 0
        self.live = []

    def reset(self, mark=0):
        self.off = mark

    def alloc(self, shape, dt, name=""):
        nel = int(np.prod(shape))
        nbytes = nel * (4 if dt == F32 else 2)
        nw = (nbytes + 31) // 32 * 8
        assert self.off + nw <= self.n, ("SBUF arena overflow", name, self.off, nw, self.n)
        s0, e0 = self.off, self.off + nw
        a = self.ap[:, s0:e0]
        self.off += nw
        if dt != F32:
            a = a.bitcast(dt)
        a = a[:, 0:nel]
        if len(shape) > 1:
            names = ["d%d" % i for i in range(len(shape))]
            kw = {n: int(s) for n, s in zip(names[:-1], shape[:-1])}
            a = a.rearrange("p (%s) -> p %s" % (" ".join(names), " ".join(names)), **kw)
        t = T(a, name)
        keep = []
        for (s1, e1, t1) in self.live:
            if s1 < e0 and s0 < e1:
                t.r.extend(t1.r)
                if t1.w is not None:
                    t.r.append(t1.w)
                if s0 <= s1 and e1 <= e0:
                    continue
            keep.append((s1, e1, t1))
        keep.append((s0, e0, t))
        self.live = keep
        return t


def _bf(a):
    return np.ascontiguousarray(a.astype(ml_dtypes.bfloat16))


def _dft_tables(n):
    N = 2 * n
    nt = n // 128
    idx = np.arange(n, dtype=np.int64)
    prod = (idx[:, None] * idx[None, :]) % N
    ang = prod.astype(np.float64) * (2.0 * np.pi / N)
    Cm = np.cos(ang)
    Sm = np.sin(ang)
    F = np.stack([Cm, Sm], 0).reshape(2, nt, 128, nt, 128)
    F = F.transpose(3, 2, 0, 1, 4)
    I = np.stack([Cm, Sm], 0).reshape(2, nt, 128, n).transpose(1, 2, 0, 3)
    wgt = np.full((128, nt), 2.0 / N, np.float32)
    wgt[0, 0] = 1.0 / N
    return _bf(F), _bf(I), wgt


def _filter_tables(n):
    pos = np.arange(n, dtype=np.float32)
    t = np.linspace(0.0, 1.0, n, dtype=np.float32)
    bands = np.linspace(1e-4, 15, 16, dtype=np.float32)
    ang = np.float32(2.0 * math.pi / n) * pos[:, None] * bands[None, :]
    z = np.concatenate([t[:, None], np.cos(ang), -np.sin(ang)], axis=-1).astype(np.float32)
    negt = (-t).reshape(n // 128, 128).T.copy()
    return np.ascontiguousarray(z.T), np.ascontiguousarray(negt.astype(np.float32))


def _const_tables():
    c = {}
    c["dftF"], c["dftI"], c["wgt"] = _dft_tables(NX)
    c["dftFc"], c["dftIc"], c["wgtc"] = _dft_tables(NCTX)
    c["ztab"], c["negt"] = _filter_tables(NX)
    c["ztabc"], c["negtc"] = _filter_tables(NCTX)
    deltas = np.abs(np.linspace(math.log(1e-2) / 0.3, math.log(1e-2) / 1.5, HY_W, dtype=np.float32))
    c["deltab"] = np.ascontiguousarray(np.broadcast_to(deltas[None, :], (128, HY_W)).astype(np.float32))
    alt = np.where(np.arange(128) % 2 == 0, 1.0, -1.0).astype(np.float32)
    c["altcol"] = _bf(alt.reshape(128, 1))
    c["altrow"] = _bf(np.where(np.arange(NX) % 2 == 0, 1.0, -1.0).astype(np.float32).reshape(1, NX))
    tt_ = np.arange(NX)
    row = (tt_ // GRID_W).astype(np.float32)
    col = (tt_ % GRID_W).astype(np.float32)
    inv = (10000.0 ** (-np.arange(16, dtype=np.float32) / 16)).astype(np.float32)
    ang = np.concatenate([row[:, None] * inv, col[:, None] * inv], axis=-1)
    cs, sn = np.cos(ang).T, np.sin(ang).T
    cos2 = np.concatenate([cs, cs], 0).astype(np.float32)
    sin2 = np.concatenate([-sn, sn], 0).astype(np.float32)
    c["rope"] = np.ascontiguousarray(np.stack([cos2, sin2, cos2 * MLA_SCALE, sin2 * MLA_SCALE], 1).astype(np.float32))
    c["ident"] = _bf(np.eye(128, dtype=np.float32))
    c["namask"] = _na_index()[3]
    return c


_NA_CLASSES = [0, 1, 2, 14, 15]


def _na_cls(i):
    if i <= 1:
        return i
    if i <= 13:
        return 2
    return i - 11


def _na_j0(i):
    return min(max(i - 2, 0), 11)


_NA_IDX = None


def _na_index():
    global _NA_IDX
    if _NA_IDX is not None:
        return _NA_IDX
    ri = np.zeros((128, 5, 5, 128), np.int64)
    ci = np.zeros((128, 5, 5, 128), np.int64)
    ok = np.zeros((128, 5, 5, 128), bool)
    for cls, i in enumerate(_NA_CLASSES):
        j0 = _na_j0(i)
        for qr in range(2):
            r = 2 * i + qr
            r0 = min(max(r - 4, 0), 24)
            for c in range(5):
                for kr2 in range(2):
                    R = 2 * (j0 + c) + kr2
                    if not (r0 <= R <= r0 + 7):
                        continue
                    for qc in range(64):
                        cs = min(max(qc - 8, 0), 48)
                        kc = np.arange(cs, cs + 16)
                        ri[kr2 * 64 + kc, cls, c, qr * 64 + qc] = R - r + 7
                        ci[kr2 * 64 + kc, cls, c, qr * 64 + qc] = kc - qc + 15
                        ok[kr2 * 64 + kc, cls, c, qr * 64 + qc] = True
    mask = np.where(ok, 0.0, MASKV).astype(np.float32).reshape(128, 3200)
    _NA_IDX = (ri, ci, ok, np.ascontiguousarray(mask))
    return _NA_IDX


def _na_gather(rpb):
    ri, ci, ok, _ = _na_index()
    g = rpb[:, :, ri, ci] * ok[None, None].astype(np.float32)
    return np.ascontiguousarray(g.reshape(rpb.shape[0], 4, 128, 3200).astype(np.float32))


def build_program(n_layers=DEPTH, dbg=(), phases=None):
    nc = bass.Bass("TRN2", target_bir_lowering=False)
    P = Prog(nc)
    dbg = set(dbg)

    def din(name, shape, dt=F32):
        h = nc.dram_tensor(name, list(shape), dt, kind="ExternalInput")
        return T(h.ap(), name)

    def dscr(name, shape, dt=F32):
        kind = "ExternalOutput" if name in dbg else "Internal"
        h = nc.dram_tensor(name, list(shape), dt, kind=kind)
        return T(h.ap(), name)

    I = {}
    I["xall"] = din("xall", [NTOK, D])
    I["cc"] = din("cc", [2, D])
    for nm, shp in [("w_ada", [DEPTH, D, 6 * D]), ("b_ada", [DEPTH, 6 * D]), ("g_attn_pre", [DEPTH, D]),
                    ("g_attn_post", [DEPTH, D]), ("g_ffn_pre", [DEPTH, D]), ("g_ffn_post", [DEPTH, D]),
                    ("w_in", [DEPTH, D, IN_COLS]), ("hy_conv_w", [DEPTH, 3, 1536]), ("hy_conv_b", [DEPTH, 1536]),
                    ("hy_f_w1", [DEPTH, 33, 64]), ("hy_f_b1", [DEPTH, 64]), ("hy_f_w2", [DEPTH, 64, 64]),
                    ("hy_f_b2", [DEPTH, 64]), ("hy_f_w3", [DEPTH, 64, 1024]), ("hy_f_freq", [DEPTH, 64]),
                    ("hy_bias", [DEPTH, 512]), ("mla_g_q", [DEPTH, Q_LORA]), ("mla_w_uq", [DEPTH, Q_LORA, 1536]),
                    ("mla_g_kv", [DEPTH, KV_LORA]), ("mla_w_ukv", [DEPTH, KV_LORA, 2048]),
                    ("w_out", [DEPTH, D, D]), ("w_ffn_gate", [DEPTH, D, FFN]), ("w_ffn_up", [DEPTH, D, FFN]),
                    ("w_ffn_down", [DEPTH, FFN, D]),
                    ("rpbg", [DEPTH, 4, 128, 3200]), ("namask", [128, 3200]),
                    ("ztab", [33, NX]), ("ztabc", [33, NCTX]), ("negt", [128, 16]), ("negtc", [128, 2]),
                    ("deltab", [128, HY_W]), ("wgt", [128, 16]), ("wgtc", [128, 2]), ("rope", [64, 4, NX])]:
        I[nm] = din(nm, shp)
    for nm, shp in [("dftF", [16, 128, 2, 16, 128]), ("dftI", [16, 128, 2, NX]), ("dftFc", [2, 128, 2, 2, 128]),
                    ("dftIc", [2, 128, 2, NCTX]), ("altcol", [128, 1]), ("altrow", [1, NX]), ("ident", [128, 128])]:
        I[nm] = din(nm, shp, BF16)
    OUT = T(nc.dram_tensor("out", [NX, D], F32, kind="ExternalOutput").ap(), "out")

    S = {}
    S["dv"] = dscr("S_dv", [DEPTH, 2, 6, D])
    S["modT"] = dscr("S_modT", [D, NTOK], BF16)
    S["hxT"] = dscr("S_hxT", [1536, NTOK])
    S["cqT"] = dscr("S_cqT", [Q_LORA, NTOK], BF16)
    S["ckvT"] = dscr("S_ckvT", [KV_LORA, NTOK], BF16)
    S["krT"] = dscr("S_krT", [64, NTOK], BF16)
    S["qnaT"] = dscr("S_qnaT", [512, NTOK], BF16)
    S["knaT"] = dscr("S_knaT", [512, NTOK], BF16)
    S["vna"] = dscr("S_vna", [NTOK, 512], BF16)
    S["catT"] = dscr("S_catT", [D, NTOK], BF16)
    S["X1"] = dscr("S_X1", [NTOK, D])
    S["X2"] = dscr("S_X2", [NTOK, D])

    es = contextlib.ExitStack()
    AR_WORDS = 52000
    arena_h = es.enter_context(nc.sbuf_tensor("arena", [128, AR_WORDS], F32))
    psum_h = es.enter_context(nc.psum_tensor("psum", [128, 4096], F32))
    A = Arena(arena_h, AR_WORDS)
    banks = [T(psum_h[:, i * 512:(i + 1) * 512], "bank%d" % i) for i in range(8)]
    for b_ in banks:
        b_.excl = True

    def bk(i, dt=F32):
        t = banks[i]
        return V(t, t.ap if dt == F32 else t.ap.bitcast(dt))

    def bcast_rows(t, row_off, n, parts=128):
        return V(t, bass.AP(t.ap.tensor, row_off, [[0, parts], [1, n]]))

    epsc_box = [None]

    def new_phase(base=0):
        A.reset(base)
        e_ = A.alloc([1], F32, "epsc")
        P.memset("pool", e_[:, :], EPS)
        epsc_box[0] = e_

    class _Eps:
        def __getitem__(self, idx):
            return epsc_box[0][idx]

    epsc = _Eps()

    def want(ph):
        return phases is None or ph in phases

    ADA_BASE = 52000 - 20640
    ada_end = [0]

    def adaln_gen(l, base, bank=6, nrow=2):
        save = A.off
        A.off = base
        ccx = A.alloc([16], F32, "ccx")
        ccc = A.alloc([16], F32, "ccc")
        scT = A.alloc([16, 2], BF16, "scT")
        wb = [A.alloc([16, 512], BF16, "wada%d" % i) for i in range(2)]
        mrow = [A.alloc([D], F32, "mrow%d" % i) for i in range(nrow)]
        grow = [A.alloc([D], F32, "grow%d" % i) for i in range(nrow)]
        drow = [A.alloc([D], F32, "drow%d" % i) for i in range(nrow)]
        ada_end[0] = A.off
        A.off = save
        P.dma("sp", ccx[:, :], V(I["cc"], I["cc"].ap[0, :].rearrange("(p j) -> p j", j=16)))
        P.dma("sp", ccc[:, :], V(I["cc"], I["cc"].ap[1, :].rearrange("(p j) -> p j", j=16)))
        P.act(scT[:, :, 0], ccx[:, :], AF.Silu)
        P.act(scT[:, :, 1], ccc[:, :], AF.Silu)
        wsrc = I["w_ada"].ap[l].rearrange("(p j) n -> p j n", j=16)
        plan = {0: (1, None, False), 1: (0, "g_attn_pre", True), 2: (2, "g_attn_post", False),
                3: (4, None, False), 4: (3, "g_ffn_pre", True), 5: (5, "g_ffn_post", False)}
        ps = bk(bank)
        for mi in range(6):
            mr, gr, dr = mrow[mi % nrow], grow[mi % nrow], drow[mi % nrow]
            slot, gname, addone = plan[mi]
            P.dma("sp", mr[0:2, :], bcast_rows(I["b_ada"], l * 6 * D + mi * D, D, 2))
            if gname is not None:
                P.dma("sp", gr[0:2, :], bcast_rows(I[gname], l * D, D, 2))
            for q in range(4):
                nb = mi * 4 + q
                w = wb[nb % 2]
                P.dma("pool", w[:, :, :], V(I["w_ada"], wsrc[:, :, nb * 512:(nb + 1) * 512]))
                for j in range(16):
                    P.mm(ps[0:2, :], scT[:, j, :], w[:, j, :], start=(j == 0), stop=(j == 15))
                P.tt("dve", mr[0:2, q * 512:(q + 1) * 512], ps[0:2, :], mr[0:2, q * 512:(q + 1) * 512], ALU.add)
                yield
            if gname is None:
                src_row = mr
            else:
                if addone:
                    P.stt("dve", dr[0:2, :], mr[0:2, :], 1.0, gr[0:2, :], ALU.add, ALU.mult)
                else:
                    P.tt("dve", dr[0:2, :], mr[0:2, :], gr[0:2, :], ALU.mult)
                src_row = dr
            P.dma("sp", V(S["dv"], S["dv"].ap[l, :, slot, :]), src_row[0:2, :])
            yield

    def phase_adaln(l):
        new_phase()
        for _ in adaln_gen(l, A.off):
            pass

    def load_dv(l, kind, idx, dst):
        off = ((l * 2 + kind) * 6 + idx) * D
        P.dma("sp", dst, bcast_rows(S["dv"], off, D, 128))

    def rstd_from_ss(ss, rstd, n):
        P.act(rstd, ss, AF.Sqrt, scale=1.0 / n, bias=epsc[:, :])
        P.recip(rstd, rstd)

    def load_cols(dst, src_rows, r, npart, ident_bf, bank=6):
        rowbuf = A.alloc([128], F32, "rowbuf")
        identf = A.alloc([128], F32, "identf")
        P.copy("dve", identf[:, :], ident_bf[:, :])
        P.dma("sp", rowbuf[0:r, 0:npart], src_rows)
        ps = bk(bank)
        P.mm(ps[0:npart, 0:r], rowbuf[0:r, 0:npart], identf[0:r, 0:r])
        P.copy("dve", dst, ps[0:npart, 0:r])

    def phase_norm(l, src, a_idx, b_idx, tiles, direct=None, base=0):
        new_phase(base)
        ident = A.alloc([128], BF16, "ident")
        P.dma("sp", ident[:, :], I["ident"][:, :])
        Ab = [A.alloc([D], F32, "Ab%d" % k) for k in range(2)]
        Bb = [A.alloc([D], F32, "Bb%d" % k) for k in range(2)]
        for k in range(2):
            load_dv(l, k, a_idx, Ab[k][:, :])
            load_dv(l, k, b_idx, Bb[k][:, :])
        xt = [A.alloc([D], F32, "xt%d" % i) for i in range(3)]
        xm = [A.alloc([D], F32, "xm%d" % i) for i in range(2)]
        xb = [A.alloc([D], BF16, "xb%d" % i) for i in range(2)]
        junk = A.alloc([D], BF16, "junk")
        ss = [A.alloc([1], F32, "ss%d" % i) for i in range(3)]
        rs = [A.alloc([1], F32, "rs%d" % i) for i in range(3)]
        stage = [A.alloc([16, 512], BF16, "stage%d" % i) for i in range(2)]
        dstT = S["modT"].ap.rearrange("(kc p) t -> p kc t", p=128)
        groups = [tiles[i:i + 4] for i in range(0, len(tiles), 4)]
        seq = []
        for gi, grp in enumerate(groups):
            for ti, tile in enumerate(grp):
                seq.append((gi, ti, tile, ti == len(grp) - 1, grp))

        def s1(n):
            (gi, ti, tile, lastg, grp) = seq[n]
            x = xt[n % 3]
            P.dma("sp", x[:, :], src[tile * 128:(tile + 1) * 128, :])
            P.memset("pool", ss[n % 3][:, :], 0.0)
            P.act(junk[:, :], x[:, :], AF.Square, accum=ss[n % 3][:, :])
            P.act(rs[n % 3][:, :], ss[n % 3][:, :], AF.Sqrt, scale=1.0 / D, bias=epsc[:, :])

        def s2(n):
            (gi, ti, tile, lastg, grp) = seq[n]
            st = stage[gi % 2]
            k = 0 if tile < 16 else 1
            x = xt[n % 3]
            b = xb[n % 2]
            xm_ = xm[n % 2]
            P.recip(rs[n % 3][:, :], rs[n % 3][:, :])
            P.stt("dve", xm_[:, :], x[:, :], rs[n % 3][:, :], Ab[k][:, :], ALU.mult, ALU.mult)
            P.tt("dve", b[:, 0:1024], xm_[:, 0:1024], Bb[k][:, 0:1024], ALU.add)
            P.tt("pool", b[:, 1024:2048], xm_[:, 1024:2048], Bb[k][:, 1024:2048], ALU.add)
            for g in range(4):
                pb = bk((n * 4 + g) % 4 + 4, BF16)
                for c in range(4):
                    kc = 4 * g + c
                    P.transpose(pb[:, c * 128:(c + 1) * 128], b[:, kc * 128:(kc + 1) * 128], ident[:, :])
                src_ps = V(pb.t, pb.ap[:, 0:512].rearrange("p (c t) -> p c t", c=4))
                if direct is not None:
                    P.copy("act", direct[:, 4 * g:4 * g + 4, tile * 128:(tile + 1) * 128], src_ps)
                else:
                    P.copy("act", st[:, 4 * g:4 * g + 4, ti * 128:(ti + 1) * 128], src_ps)
            if lastg and direct is None:
                t0 = grp[0] * 128
                nt = len(grp) * 128
                P.dma("pool", V(S["modT"], dstT[:, :, t0:t0 + nt]), st[:, :, 0:nt])

        assert direct is None or A.off <= 52000 - 18432 - 64, A.off
        s1(0)
        if len(seq) > 1:
            s1(1)
        for n in range(len(seq)):
            if n + 2 < len(seq):
                s1(n + 2)
            s2(n)

    def phase_inproj(l, mT, base):
        new_phase(base)
        wb = [A.alloc([16, 512], BF16, "win%d" % i) for i in range(2)]
        wsrc = I["w_in"].ap[l].rearrange("(kc p) n -> p kc n", p=128)
        ones = A.alloc([128], BF16, "ones")
        P.memset("dve", ones[:, :], 1.0)
        identb = A.alloc([128], BF16, "identb")
        P.dma("sp", identb[:, :], I["ident"][:, :])
        st32 = [A.alloc([NTOK], F32, "st32_%d" % i) for i in range(2)]
        raw = A.alloc([6, NTOK], BF16, "raw")
        sq = [A.alloc([512], BF16, "sq%d" % i) for i in range(2)]
        rstd = A.alloc([NTOK], F32, "rstdb")
        gcols = {"mla_g_q": A.alloc([8], F32, "gcolq"), "mla_g_kv": A.alloc([8], F32, "gcolkv")}
        for gname_, nch_ in (("mla_g_q", 6), ("mla_g_kv", 4)):
            load_cols(gcols[gname_][:, 0:nch_], V(I[gname_], I[gname_].ap[l].rearrange("(c p) -> c p", p=128)), nch_, 128, identb)
        rope = A.alloc([2, NX], F32, "ropek")
        P.dma("sp", rope[0:64, :, :], I["rope"][:, 0:2, :])
        stb = [A.alloc([NTOK], BF16, "stb%d" % i) for i in range(2)]
        vst = [A.alloc([512], BF16, "vst%d" % i) for i in range(2)]
        tmp = A.alloc([512], F32, "tmpc")
        tblocks = [(i * 512, 512) for i in range(4)] + [(2048, 256)]
        wcnt = [0]
        pcnt = [0]
        assert A.off + 1024 <= 52000 - 18432 - 64, A.off

        def load_w(c0, ncols, dst_c0=0, w=None):
            if w is None:
                w = wb[wcnt[0] % 2]
                wcnt[0] += 1
            P.dma("pool", w[:, :, dst_c0:dst_c0 + ncols], V(I["w_in"], wsrc[:, :, c0:c0 + ncols]))
            return w

        def proj_fm(w, wc0, m, tb0, tbn):
            ps = bk(pcnt[0] % 2)
            pcnt[0] += 1
            for kc in range(16):
                P.mm(ps[0:m, 0:tbn], w[:, kc, wc0:wc0 + m], mT[:, kc, tb0:tb0 + tbn], start=(kc == 0), stop=(kc == 15))
            return ps[0:m, 0:tbn]

        n = 0
        for g in (range(3) if want("ip_hx") else []):
            w = load_w(C_HX + g * 512, 512)
            for cc in range(4):
                st = st32[n % 2]
                for (tb0, tbn) in tblocks:
                    ps = proj_fm(w, cc * 128, 128, tb0, tbn)
                    P.copy("act" if (tb0 // 512) % 2 == 0 else "dve", st[:, tb0:tb0 + tbn], ps)
                r0 = (g * 4 + cc) * 128
                P.dma("sp", S["hxT"][r0:r0 + 128, :], st[:, :])
                n += 1

        def latent(c0, nch, gname, dst, nfeat):
            gcol = gcols[gname]
            ws = []
            for g in range((nch + 3) // 4):
                ncols = min(512, nch * 128 - g * 512)
                ws.append(load_w(c0 + g * 512, ncols))
            for (tb0, tbn) in tblocks:
                acc = bk(2 + (tb0 // 512) % 2)
                pend = None
                for ch in range(nch):
                    ps = proj_fm(ws[ch // 4], (ch % 4) * 128, 128, tb0, tbn)
                    s = sq[ch % 2]
                    P.copy("dve", raw[:, ch, tb0:tb0 + tbn], ps)
                    P.act(s[:, 0:tbn], raw[:, ch, tb0:tb0 + tbn], AF.Square)
                    if pend is not None:
                        P.mm(acc[:, 0:tbn], ones[:, :], pend[0][:, 0:tbn], start=(pend[1] == 0), stop=False)
                    pend = (s, ch)
                P.mm(acc[:, 0:tbn], ones[:, :], pend[0][:, 0:tbn], start=(pend[1] == 0), stop=True)
                P.ts("dve", rstd[:, tb0:tb0 + tbn], acc[:, 0:tbn], 1.0 / nfeat, ALU.mult, EPS, ALU.add)
                P.act(rstd[:, tb0:tb0 + tbn], rstd[:, tb0:tb0 + tbn], AF.Sqrt)
                P.recip(rstd[:, tb0:tb0 + tbn], rstd[:, tb0:tb0 + tbn])
            for ch in range(nch):
                sb_ = stb[ch % 2]
                P.stt("dve", sb_[:, :], raw[:, ch, :], gcol[:, ch:ch + 1], rstd[:, :], ALU.mult, ALU.mult)
                P.dma("sp", dst[ch * 128:(ch + 1) * 128, :], sb_[:, :])

        if want("ip_lat"):
            latent(C_CQ, 6, "mla_g_q", S["cqT"], Q_LORA)
            latent(C_CKV, 4, "mla_g_kv", S["ckvT"], KV_LORA)
        if not want("ip_rest"):
            return

        w = wb[wcnt[0] % 2]
        wcnt[0] += 1
        load_w(C_KR, 64, 0, w)
        load_w(C_KR + 32, 32, 64, w)
        load_w(C_KR, 32, 96, w)
        sb_ = stb[0]
        for (tb0, tbn) in tblocks:
            pa = proj_fm(w, 0, 64, tb0, tbn)
            if tb0 < NX:
                pbb = proj_fm(w, 64, 64, tb0, tbn)
                P.tt("dve", tmp[0:64, 0:tbn], pa, rope[0:64, 0, tb0:tb0 + tbn], ALU.mult)
                P.tt("dve", st32[0][0:64, 0:tbn], pbb, rope[0:64, 1, tb0:tb0 + tbn], ALU.mult)
                P.tt("dve", sb_[0:64, tb0:tb0 + tbn], tmp[0:64, 0:tbn], st32[0][0:64, 0:tbn], ALU.add)
            else:
                P.copy("dve", sb_[0:64, tb0:tb0 + tbn], pa)
        P.dma("sp", S["krT"][:, :], sb_[0:64, :])

        n = 0
        for (c0, dst, scale) in [(C_QNA, S["qnaT"], NA_SCALE), (C_KNA, S["knaT"], 1.0)]:
            w = load_w(c0, 512)
            for cc in range(4):
                sb_ = stb[n % 2]
                n += 1
                for (tb0, tbn) in tblocks:
                    ps = proj_fm(w, cc * 128, 128, tb0, tbn)
                    P.act(sb_[:, tb0:tb0 + tbn], ps, AF.Identity, scale=scale)
                P.dma("sp", dst[cc * 128:(cc + 1) * 128, :], sb_[:, :])

        w = load_w(C_VNA, 512)
        for tile in range(NTOK // 128):
            ps = bk(tile % 2)
            for kc in range(16):
                P.mm(ps[:, :], mT[:, kc, tile * 128:(tile + 1) * 128], w[:, kc, :], start=(kc == 0), stop=(kc == 15))
            vs = vst[tile % 2]
            P.copy("act" if tile % 2 == 0 else "dve", vs[:, :], ps[:, :])
            P.dma("sp", S["vna"][tile * 128:(tile + 1) * 128, :], vs[:, :])

    def sin_wrapped(dst, arg, tmpv):
        for _ in range(2):
            P.ts("dve", tmpv, arg, math.pi, ALU.is_gt, -TWO_PI, ALU.mult)
            P.tt("dve", arg, arg, tmpv, ALU.add)
            P.ts("dve", tmpv, arg, -math.pi, ALU.is_lt, TWO_PI, ALU.mult)
            P.tt("dve", arg, arg, tmpv, ALU.add)
        P.act(dst, arg, AF.Sin)

    def phase_hyena(l, n, tok0, ctxmode):
        new_phase()
        nt = n // 128
        NN = 2 * n
        zt_in = I["ztabc"] if ctxmode else I["ztab"]
        negt_in = I["negtc"] if ctxmode else I["negt"]
        wgt_in = I["wgtc"] if ctxmode else I["wgt"]
        dF = I["dftFc"] if ctxmode else I["dftF"]
        dI = I["dftIc"] if ctxmode else I["dftI"]
        nb_cols = min(512, n)
        nblk = n // nb_cols

        ident = A.alloc([128], BF16, "ident")
        P.dma("sp", ident[:, :], I["ident"][:, :])
        G = A.alloc([nt, 512], BF16, "G")
        Dd = A.alloc([nt, 512], BF16, "Dd")
        zin = A.alloc([nt, 512], BF16, "zin")
        Y = A.alloc([nt, 2, 512], BF16, "Y")
        x2u = A.alloc([4, n], F32, "x2u")
        altc = A.alloc([1], BF16, "altc")
        altr = A.alloc([n], BF16, "altr")
        yny = A.alloc([512], BF16, "yny")
        wgt = A.alloc([nt], F32, "wgt")
        negt = A.alloc([nt], F32, "negt")
        P.dma("sp", altc[:, :], I["altcol"][:, :])
        P.dma("sp", altr[0:1, :], I["altrow"][:, 0:n])
        P.dma("sp", wgt[:, :], wgt_in[:, :])
        P.dma("sp", negt[:, :], negt_in[:, :])
        mark = A.off

        zt = A.alloc([n], F32, "zt")
        h1 = A.alloc([n], F32, "h1")
        h2 = A.alloc([n], F32, "h2")
        w1 = A.alloc([64], F32, "w1")
        w2 = A.alloc([64], F32, "w2")
        w3 = A.alloc([1024], F32, "w3")
        vec = A.alloc([8], F32, "vec")
        arg = A.alloc([512], F32, "arg")
        tmpv = A.alloc([512], F32, "tmpv")
        delt = A.alloc([512], F32, "delt")
        dec = A.alloc([512], F32, "dec")
        hf = A.alloc([512], F32, "hf")
        hb = A.alloc([512], F32, "hb")
        brow = A.alloc([512], F32, "brow")
        P.dma("sp", zt[0:33, :], zt_in[:, :])
        P.dma("sp", w1[0:33, :], V(I["hy_f_w1"], I["hy_f_w1"].ap[l]))
        P.dma("sp", w2[0:64, :], V(I["hy_f_w2"], I["hy_f_w2"].ap[l]))
        P.dma("sp", w3[0:64, :], V(I["hy_f_w3"], I["hy_f_w3"].ap[l]))
        for i, nm in enumerate(["hy_f_freq", "hy_f_b1", "hy_f_b2"]):
            load_cols(vec[0:64, i:i + 1], V(I[nm], I[nm].ap[l:l + 1, :]), 1, 64, ident)
        P.tt("dve", vec[0:64, 3:4], vec[0:64, 0:1], vec[0:64, 1:2], ALU.mult)
        P.tt("dve", vec[0:64, 4:5], vec[0:64, 0:1], vec[0:64, 2:3], ALU.mult)
        P.dma("sp", delt[:, :], I["deltab"][:, :])
        P.dma("sp", brow[0:1, :], V(I["hy_bias"], I["hy_bias"].ap[l:l + 1, :]))
        for (wm, kdim, src, dst, bcol) in [(w1, 33, zt, h1, 3), (w2, 64, h1, h2, 4)]:
            for b in range(nblk):
                ps = bk(b % 2)
                cs = slice(b * nb_cols, (b + 1) * nb_cols)
                P.mm(ps[0:64, 0:nb_cols], wm[0:kdim, 0:64], src[0:kdim, cs])
                P.ts("dve", arg[0:64, 0:nb_cols], ps[0:64, 0:nb_cols], vec[0:64, 0:1], ALU.mult, vec[0:64, bcol:bcol + 1], ALU.add)
                sin_wrapped(dst[0:64, cs], arg[0:64, 0:nb_cols], tmpv[0:64, 0:nb_cols])
        for jt in range(nt):
            pf, pb_ = bk(2), bk(3)
            P.mm(pf[:, :], h2[0:64, jt * 128:(jt + 1) * 128], w3[0:64, 0:512])
            P.mm(pb_[:, :], h2[0:64, jt * 128:(jt + 1) * 128], w3[0:64, 512:1024])
            P.act(dec[:, :], delt[:, :], AF.Exp, scale=negt[:, jt:jt + 1])
            P.tt("dve", hf[:, :], pf[:, :], dec[:, :], ALU.mult)
            P.tt("dve", hb[:, :], pb_[:, :], dec[:, :], ALU.mult)
            P.tt("dve", G[:, jt, :], hf[:, :], hb[:, :], ALU.add)
            P.tt("dve", Dd[:, jt, :], hb[:, :], hf[:, :], ALU.subtract)
            if jt == 0:
                P.tt("dve", G[0:1, 0, :], hf[0:1, :], brow[0:1, :], ALU.add)
        if "S_G" in dbg and not ctxmode:
            S["G"] = dscr("S_G", [128, nt, 512], BF16)
            P.dma("sp", S["G"][:, :, :], G[:, :, :])

        A.off = mark
        cw = A.alloc([3, 12], F32, "cw")
        cb = A.alloc([12], F32, "cb")
        load_cols(V(cw, cw.ap.rearrange("p k m -> p (k m)")),
                  V(I["hy_conv_w"], I["hy_conv_w"].ap[l].rearrange("k (m p) -> (k m) p", p=128)), 36, 128, ident)
        load_cols(cb[:, :], V(I["hy_conv_b"], I["hy_conv_b"].ap[l].rearrange("(m p) -> m p", p=128)), 12, 128, ident)
        xr = [A.alloc([n], F32, "xr%d" % i) for i in range(2)]
        uv = A.alloc([n], F32, "uv")
        u1 = A.alloc([n], F32, "u1")
        zT = A.alloc([n], BF16, "zT")

        def sconv(dst, m, src):
            P.act(dst, src[:, :], AF.Identity, scale=cw[:, 1, m:m + 1], bias=cb[:, m:m + 1])
            P.stt("dve", dst[:, 1:n], src[:, 0:n - 1], cw[:, 0, m:m + 1], dst[:, 1:n], ALU.mult, ALU.add)
            P.stt("dve", dst[:, 0:n - 1], src[:, 1:n], cw[:, 2, m:m + 1], dst[:, 0:n - 1], ALU.mult, ALU.add)

        cnt = 0
        for c in range(4):
            for part, dst in [(0, uv[:, :]), (1, u1[:, :]), (2, x2u[:, c, :])]:
                m = part * 4 + c
                x = xr[cnt % 2]
                cnt += 1
                P.dma("sp", x[:, :], S["hxT"][m * 128:(m + 1) * 128, tok0:tok0 + n])
                sconv(dst, m, x)
            P.tt("dve", zT[:, :], u1[:, :], uv[:, :], ALU.mult)
            for st in range(nt):
                pb = bk(4 + st % 4, BF16)
                P.transpose(pb[:, 0:128], zT[:, st * 128:(st + 1) * 128], ident[:, :])
                P.copy("act", zin[:, st, c * 128:(c + 1) * 128], pb[:, 0:128])

        A.off = mark
        Fb = [A.alloc([2, nt, 128], BF16, "Fb%d" % i) for i in range(2)]
        kc_ = A.alloc([512], F32, "kc")
        ks_ = A.alloc([512], F32, "ks")
        t1 = A.alloc([512], F32, "t1")
        t2 = A.alloc([512], F32, "t2")
        t3 = A.alloc([512], F32, "t3")
        t4 = A.alloc([512], F32, "t4")
        for ft in range(nt):
            Fb_ = Fb[ft % 2]
            P.dma("sp", Fb_[:, :, :, :], V(dF, dF.ap[ft]))
            b0_ = (ft % 2) * 4
            zc, zs, kcp, ksp = bk(b0_), bk(b0_ + 1), bk(b0_ + 2), bk(b0_ + 3)
            for st in range(nt):
                f, la = (st == 0), (st == nt - 1)
                P.mm(zc[:, :], Fb_[:, 0, st, :], zin[:, st, :], start=f, stop=la)
                P.mm(kcp[:, :], Fb_[:, 0, st, :], G[:, st, :], start=f, stop=la)
                P.mm(zs[:, :], Fb_[:, 1, st, :], zin[:, st, :], start=f, stop=la)
                P.mm(ksp[:, :], Fb_[:, 1, st, :], Dd[:, st, :], start=f, stop=la)
            P.act(kc_[:, :], kcp[:, :], AF.Identity, scale=wgt[:, ft:ft + 1])
            P.act(ks_[:, :], ksp[:, :], AF.Identity, scale=wgt[:, ft:ft + 1])
            P.tt("dve", t1[:, :], zc[:, :], kc_[:, :], ALU.mult)
            P.tt("dve", t2[:, :], zs[:, :], ks_[:, :], ALU.mult)
            P.tt("dve", t3[:, :], zs[:, :], kc_[:, :], ALU.mult)
            P.tt("dve", t4[:, :], zc[:, :], ks_[:, :], ALU.mult)
            P.tt("pool", Y[:, ft, 0, :], t1[:, :], t2[:, :], ALU.add)
            P.tt("pool", Y[:, ft, 1, :], t3[:, :], t4[:, :], ALU.subtract)
        zn, kn = bk(0), bk(1)
        for st in range(nt):
            P.mm(zn[0:1, :], altc[:, 0:1], zin[:, st, :], start=(st == 0), stop=(st == nt - 1))
            P.mm(kn[0:1, :], altc[:, 0:1], G[:, st, :], start=(st == 0), stop=(st == nt - 1))
        P.act(t1[0:1, :], kn[0:1, :], AF.Identity, scale=1.0 / NN)
        P.tt("dve", yny[0:1, :], zn[0:1, :], t1[0:1, :], ALU.mult)

        A.off = mark
        Ib = [A.alloc([2, n], BF16, "Ib%d" % i) for i in range(2)]
        ost = [A.alloc([n], BF16, "ost%d" % i) for i in range(2)]
        ntb = n // nb_cols
        per_pass = max(1, 8 // ntb)
        per_pass = min(per_pass, 4)
        for p0 in range(0, 4, per_pass):
            cl_list = list(range(p0, min(4, p0 + per_pass)))
            for ft in range(nt):
                Ib_ = Ib[ft % 2]
                P.dma("sp", Ib_[:, :, :], V(dI, dI.ap[ft]))
                for ci, c in enumerate(cl_list):
                    for tb in range(ntb):
                        acc = bk(ci * ntb + tb)
                        ts_ = slice(tb * nb_cols, (tb + 1) * nb_cols)
                        P.mm(acc[:, 0:nb_cols], Y[:, ft, 0, c * 128:(c + 1) * 128], Ib_[:, 0, ts_], start=(ft == 0), stop=False)
                        P.mm(acc[:, 0:nb_cols], Y[:, ft, 1, c * 128:(c + 1) * 128], Ib_[:, 1, ts_], start=False, stop=False)
            for ci, c in enumerate(cl_list):
                o = ost[c % 2]
                for tb in range(ntb):
                    acc = bk(ci * ntb + tb)
                    ts_ = slice(tb * nb_cols, (tb + 1) * nb_cols)
                    P.mm(acc[:, 0:nb_cols], yny[0:1, c * 128:(c + 1) * 128], altr[0:1, ts_], start=False, stop=True)
                    P.tt("dve", o[:, ts_], acc[:, 0:nb_cols], x2u[:, c, ts_], ALU.mult)
                P.dma("pool", S["catT"][c * 128:(c + 1) * 128, tok0:tok0 + n], o[:, :])

    def attn_fin_a(po, obuf, rcp):
        P.recip(rcp[:, :], po[:, 128:129])
        P.act(obuf[:, :], po[:, 0:128], AF.Identity, scale=rcp[:, :])

    def attn_fin_b(obuf, dst_col, ident, stage):
        pt = bk(7, BF16)
        P.transpose(pt[:, 0:128], obuf[:, :], ident[:, :])
        P.copy("dve", stage[:, dst_col:dst_col + 128], pt[:, 0:128])

    def phase_mla(l, with_ctx_q, side_fn=None):
        new_phase()
        nq_tot = NTOK if with_ctx_q else NX
        ident = A.alloc([128], BF16, "ident")
        P.dma("sp", ident[:, :], I["ident"][:, :])
        cq = A.alloc([6, NTOK], BF16, "cq")
        ckv = A.alloc([4, NTOK], BF16, "ckv")
        kr = A.alloc([NTOK], BF16, "kr")
        rope = A.alloc([2, NX], F32, "ropeq")
        P.dma("sp", cq[:, :, :], V(S["cqT"], S["cqT"].ap.rearrange("(c p) t -> p c t", p=128)))
        P.dma("sp", ckv[:, :, :], V(S["ckvT"], S["ckvT"].ap.rearrange("(c p) t -> p c t", p=128)))
        P.dma("sp", kr[0:64, :], S["krT"][:, :])
        P.dma("sp", rope[0:64, :, :], I["rope"][:, 2:4, :])
        wq = [A.alloc([6, 256], BF16, "wq%d" % i) for i in range(2)]
        wkv = [A.alloc([4, 256], BF16, "wkv%d" % i) for i in range(2)]
        qn = A.alloc([NTOK], BF16, "qn")
        qr = A.alloc([NTOK], BF16, "qr")
        kn = A.alloc([NTOK], BF16, "kn")
        vv = A.alloc([18, 132], BF16, "vv")
        P.memset("dve", vv[:, :, 128:129], 1.0)
        PT = [A.alloc([18, 512], BF16, "PT%d" % i) for i in range(2)]
        tA = A.alloc([512], F32, "tA")
        tB = A.alloc([512], F32, "tB")
        obufs = [A.alloc([128], BF16, "obuf%d" % i) for i in range(4)]
        rcps = [A.alloc([1], F32, "rcp%d" % i) for i in range(4)]
        fcnt = [0]
        pend_fin = [None]
        stage = [A.alloc([NTOK], BF16, "ostage%d" % i) for i in range(2)]
        side = side_fn(A.off) if side_fn is not None else None
        uq_src = I["mla_w_uq"].ap[l].rearrange("(kc p) n -> p kc n", p=128)
        ukv_src = I["mla_w_ukv"].ap[l].rearrange("(kc p) n -> p kc n", p=128)
        tblocks = [(i * 512, 512) for i in range(4)] + ([(2048, 256)] if with_ctx_q else [])
        kblocks = [(i * 512, 512) for i in range(4)] + [(2048, 256)]
        pc = [0]
        ptc = [0]

        def pbank():
            pc[0] += 1
            return bk(pc[0] % 2)

        for h in range(MLA_H):
            wq_, wkv_ = wq[h % 2], wkv[h % 2]
            q0 = h * 192
            P.dma("pool", wq_[:, :, 0:192], V(I["mla_w_uq"], uq_src[:, :, q0:q0 + 192]))
            P.dma("pool", wq_[:, :, 192:224], V(I["mla_w_uq"], uq_src[:, :, q0 + 160:q0 + 192]))
            P.dma("pool", wq_[:, :, 224:256], V(I["mla_w_uq"], uq_src[:, :, q0 + 128:q0 + 160]))
            P.dma("pool", wkv_[:, :, :], V(I["mla_w_ukv"], ukv_src[:, :, h * 256:(h + 1) * 256]))
            for (tb0, tbn) in tblocks:
                ps = pbank()
                for kc in range(6):
                    P.mm(ps[:, 0:tbn], wq_[:, kc, 0:128], cq[:, kc, tb0:tb0 + tbn], start=(kc == 0), stop=(kc == 5))
                P.act(qn[:, tb0:tb0 + tbn], ps[:, 0:tbn], AF.Identity, scale=MLA_SCALE)
                pa = pbank()
                for kc in range(6):
                    P.mm(pa[0:64, 0:tbn], wq_[:, kc, 128:192], cq[:, kc, tb0:tb0 + tbn], start=(kc == 0), stop=(kc == 5))
                if tb0 < NX:
                    pb_ = pbank()
                    for kc in range(6):
                        P.mm(pb_[0:64, 0:tbn], wq_[:, kc, 192:256], cq[:, kc, tb0:tb0 + tbn], start=(kc == 0), stop=(kc == 5))
                    P.tt("dve", tA[0:64, 0:tbn], pa[0:64, 0:tbn], rope[0:64, 0, tb0:tb0 + tbn], ALU.mult)
                    P.tt("dve", tB[0:64, 0:tbn], pb_[0:64, 0:tbn], rope[0:64, 1, tb0:tb0 + tbn], ALU.mult)
                    P.tt("dve", qr[0:64, tb0:tb0 + tbn], tA[0:64, 0:tbn], tB[0:64, 0:tbn], ALU.add)
                else:
                    P.act(qr[0:64, tb0:tb0 + tbn], pa[0:64, 0:tbn], AF.Identity, scale=MLA_SCALE)
            for (tb0, tbn) in kblocks:
                ps = pbank()
                for kc in range(4):
                    P.mm(ps[:, 0:tbn], wkv_[:, kc, 0:128], ckv[:, kc, tb0:tb0 + tbn], start=(kc == 0), stop=(kc == 3))
                P.copy("act", kn[:, tb0:tb0 + tbn], ps[:, 0:tbn])
            for kt in range(18):
                ps = pbank()
                for kc in range(4):
                    P.mm(ps[:, 0:128], ckv[:, kc, kt * 128:(kt + 1) * 128], wkv_[:, kc, 128:256], start=(kc == 0), stop=(kc == 3))
                P.copy("dve", vv[:, kt, 0:128], ps[:, 0:128])
            stg = stage[h % 2]
            jobs = [(qb * 512, 512, list(range(18))) for qb in range(4)]
            if with_ctx_q:
                jobs.append((2048, 256, [16, 17]))
            def qk_stage(job):
                (q0_, qn_, ktiles) = job
                pt_ = PT[ptc[0] % 2]
                ptc[0] += 1
                for ki, kt in enumerate(ktiles):
                    ps = bk(2 + ki % 3)
                    P.mm(ps[:, 0:qn_], kn[:, kt * 128:(kt + 1) * 128], qn[:, q0_:q0_ + qn_], start=True, stop=False)
                    P.mm(ps[:, 0:qn_], kr[0:64, kt * 128:(kt + 1) * 128], qr[0:64, q0_:q0_ + qn_], start=False, stop=True)
                    P.act(pt_[:, ki, 0:qn_], ps[:, 0:qn_], AF.Exp)
                return pt_

            def pv_stage(job, pt_):
                (q0_, qn_, ktiles) = job
                for qb in range(qn_ // 128):
                    po = bk(5 + qb % 2)
                    for ki, kt in enumerate(ktiles):
                        P.mm(po[:, 0:129], pt_[:, ki, qb * 128:(qb + 1) * 128], vv[:, kt, 0:129],
                             start=(ki == 0), stop=(ki == len(ktiles) - 1))
                    ob = obufs[fcnt[0] % 4]
                    attn_fin_a(po, ob, rcps[fcnt[0] % 4])
                    fcnt[0] += 1
                    if pend_fin[0] is not None:
                        attn_fin_b(*pend_fin[0])
                    pend_fin[0] = (ob, q0_ + qb * 128, ident, stg)

            prev = None
            for job in jobs:
                cur = qk_stage(job)
                if prev is not None:
                    pv_stage(*prev)
                prev = (job, cur)
                if side is not None:
                    next(side, None)
            pv_stage(*prev)
            attn_fin_b(*pend_fin[0])
            pend_fin[0] = None
            r0 = 512 + h * 128
            P.dma("sp", S["catT"][r0:r0 + 128, 0:nq_tot], stg[:, 0:nq_tot])
        if side is not None:
            for _ in side:
                pass

    def phase_na(l, with_ctx_q, side=None):
        new_phase()
        nq_tot = NTOK if with_ctx_q else NX
        ident = A.alloc([128], BF16, "ident")
        P.dma("sp", ident[:, :], I["ident"][:, :])
        mask = A.alloc([3200], F32, "mask")
        P.dma("sp", mask[:, :], I["namask"][:, :])
        braw = A.alloc([3200], F32, "braw")
        bias = [A.alloc([5, 5, 128], BF16, "bias%d" % i) for i in range(2)]
        qT = [A.alloc([NTOK], BF16, "qT%d" % i) for i in range(2)]
        kT = [A.alloc([NTOK], BF16, "kT%d" % i) for i in range(2)]
        vv = [A.alloc([18, 132], BF16, "vv%d" % i) for i in range(2)]
        for i in range(2):
            P.memset("dve", vv[i][:, :, 128:129], 1.0)
        PT = [A.alloc([7, 128], BF16, "PT%d" % i) for i in range(2)]
        obufs = [A.alloc([128], BF16, "obuf%d" % i) for i in range(4)]
        rcps = [A.alloc([1], F32, "rcp%d" % i) for i in range(4)]
        fcnt = [0]
        pend_fin = [None]

        def fin(po, dst_col, stg_):
            ob = obufs[fcnt[0] % 4]
            attn_fin_a(po, ob, rcps[fcnt[0] % 4])
            fcnt[0] += 1
            if pend_fin[0] is not None:
                attn_fin_b(*pend_fin[0])
            pend_fin[0] = (ob, dst_col, ident, stg_)

        def fin_flush():
            if pend_fin[0] is not None:
                attn_fin_b(*pend_fin[0])
            pend_fin[0] = None

        stage = [A.alloc([NTOK], BF16, "ostage%d" % i) for i in range(2)]
        vsrc = S["vna"].ap.rearrange("(t p) c -> p t c", p=128)
        gi = 0
        def na_load(h):
            q_, k_, v_, b_ = qT[h % 2], kT[h % 2], vv[h % 2], bias[h % 2]
            P.dma("sp", q_[:, :], S["qnaT"][h * 128:(h + 1) * 128, :])
            P.dma("sp", k_[:, :], S["knaT"][h * 128:(h + 1) * 128, :])
            P.dma("sp", v_[:, :, 0:128], V(S["vna"], vsrc[:, :, h * 128:(h + 1) * 128]))
            P.dma("sp", braw[:, :], V(I["rpbg"], I["rpbg"].ap[l, h]))
            P.tt("pool", V(b_, b_.ap.rearrange("p a b c -> p (a b c)")), braw[:, :], mask[:, :], ALU.add)

        na_load(0)
        for h in range(NA_H):
            q_, k_, v_, b_ = qT[h % 2], kT[h % 2], vv[h % 2], bias[h % 2]
            if h + 1 < NA_H:
                na_load(h + 1)
            stg = stage[h % 2]
            def na_qk(i):
                cls, j0 = _na_cls(i), _na_j0(i)
                g = gic[0]
                gic[0] += 1
                pa, pb_ = bk(2 * (g % 2)), bk(2 * (g % 2) + 1)
                pt_ = PT[g % 2]
                qs = q_[:, i * 128:(i + 1) * 128]
                ktiles = [j0 + c for c in range(5)] + [16, 17]
                for c in range(7):
                    dst = pa[:, c * 128:(c + 1) * 128] if c < 4 else pb_[:, (c - 4) * 128:(c - 3) * 128]
                    kt = ktiles[c]
                    if c < 5:
                        P.mm(dst, k_[:, kt * 128:(kt + 1) * 128], qs, start=True, stop=False)
                        P.mm(dst, ident[:, :], b_[:, cls, c, :], start=False, stop=True)
                    else:
                        P.mm(dst, k_[:, kt * 128:(kt + 1) * 128], qs, start=True, stop=True)
                P.act(V(pt_, pt_.ap[:, 0:4, :].rearrange("p a b -> p (a b)")), pa[:, 0:512], AF.Exp)
                P.act(V(pt_, pt_.ap[:, 4:7, :].rearrange("p a b -> p (a b)")), pb_[:, 0:384], AF.Exp)
                return (i, g, pt_, ktiles)

            def na_pv(i, g, pt_, ktiles):
                po = bk(4 + g % 2)
                for c in range(7):
                    P.mm(po[:, 0:129], pt_[:, c, :], v_[:, ktiles[c], 0:129], start=(c == 0), stop=(c == 6))
                fin(po, i * 128, stg)

            gic = [gi]
            prev = None
            for i in range(16):
                cur = na_qk(i)
                if prev is not None:
                    na_pv(*prev)
                prev = cur
                if side is not None and i % 2 == 1:
                    next(side, None)
            na_pv(*prev)
            gi = gic[0]
            if with_ctx_q:
                for qt in (16, 17):
                    pa = bk(2 * (gi % 2))
                    pt_ = PT[gi % 2]
                    gi += 1
                    for c, kt in enumerate((16, 17)):
                        P.mm(pa[:, c * 128:(c + 1) * 128], k_[:, kt * 128:(kt + 1) * 128], q_[:, qt * 128:(qt + 1) * 128])
                    P.act(V(pt_, pt_.ap[:, 0:2, :].rearrange("p a b -> p (a b)")), pa[:, 0:256], AF.Exp)
                    po = bk(4 + gi % 2)
                    for c, kt in enumerate((16, 17)):
                        P.mm(po[:, 0:129], pt_[:, c, :], v_[:, kt, 0:129], start=(c == 0), stop=(c == 1))
                    fin(po, qt * 128, stg)
            fin_flush()
            r0 = 1536 + h * 128
            P.dma("sp", S["catT"][r0:r0 + 128, 0:nq_tot], stg[:, 0:nq_tot])
        if side is not None:
            for _ in side:
                pass

    def post_residual(y_views, Ab, xres_src, dst, sqj, ssv, rs, xt, ot, tmpc=None, preloaded=False, stq="pool"):
        P.memset("dve", ssv[:, 0:4], 0.0)
        for q, yv in enumerate(y_views):
            P.act(sqj[:, :], yv, AF.Square, accum=ssv[:, q:q + 1])
        P.tt("dve", ssv[:, 4:5], ssv[:, 0:1], ssv[:, 1:2], ALU.add)
        P.tt("dve", ssv[:, 5:6], ssv[:, 2:3], ssv[:, 3:4], ALU.add)
        P.tt("dve", ssv[:, 6:7], ssv[:, 4:5], ssv[:, 5:6], ALU.add)
        rstd_from_ss(ssv[:, 6:7], rs[:, :], D)
        if not preloaded:
            P.dma("sp", xt[:, :], xres_src)
        for q, yv in enumerate(y_views):
            cs = slice(q * 512, (q + 1) * 512)
            if ot is None:
                tc_ = tmpc[q % 2]
                P.stt("dve", tc_[:, :], yv, rs[:, :], Ab[:, cs], ALU.mult, ALU.mult)
                P.tt("pool", xt[:, cs], tc_[:, :], xt[:, cs], ALU.add)
            else:
                P.stt("dve", ot[:, cs], yv, rs[:, :], Ab[:, cs], ALU.mult, ALU.mult)
                P.tt("pool", ot[:, cs], ot[:, cs], xt[:, cs], ALU.add)
        P.dma(stq, dst, (xt if ot is None else ot)[:, :])

    def phase_outproj(l, xsrc, tiles):
        new_phase()
        wo = A.alloc([16, D], BF16, "wo")
        wsrc = I["w_out"].ap[l].rearrange("(kc p) n -> p kc n", p=128)
        for q in range(4):
            P.dma("pool", wo[:, :, q * 512:(q + 1) * 512], V(I["w_out"], wsrc[:, :, q * 512:(q + 1) * 512]))
        Ab = [A.alloc([D], F32, "A2_%d" % k) for k in range(2)]
        for k in range(2):
            load_dv(l, k, 2, Ab[k][:, :])
        cat = [A.alloc([16, 128], BF16, "cat%d" % i) for i in range(2)]
        xt = [A.alloc([D], F32, "xt%d" % i) for i in range(2)]
        ot = [A.alloc([D], F32, "ot%d" % i) for i in range(2)]
        sqj = A.alloc([512], BF16, "sqj")
        ssv = [A.alloc([8], F32, "ssv%d" % i) for i in range(2)]
        rs = [A.alloc([1], F32, "rs%d" % i) for i in range(2)]
        csrc = S["catT"].ap.rearrange("(kc p) t -> p kc t", p=128)
        for n, tile in enumerate(tiles):
            k = 0 if tile < 16 else 1
            c_ = cat[n % 2]
            P.dma("sp", c_[:, :, :], V(S["catT"], csrc[:, :, tile * 128:(tile + 1) * 128]))
            P.dma("sp", xt[n % 2][:, :], xsrc[tile * 128:(tile + 1) * 128, :])
            ys = []
            for q in range(4):
                ps = bk((n % 2) * 4 + q)
                for kc in range(16):
                    P.mm(ps[:, :], c_[:, kc, :], wo[:, kc, q * 512:(q + 1) * 512], start=(kc == 0), stop=(kc == 15))
                ys.append(ps[:, :])
            rows = slice(tile * 128, (tile + 1) * 128)
            post_residual(ys, Ab[k], xsrc[rows, :], S["X1"][rows, :], sqj, ssv[n % 2], rs[n % 2], xt[n % 2], ot[n % 2],
                          preloaded=True)

    def phase_ffn(l, blocks, final):
        new_phase()
        Ab = [A.alloc([D], F32, "A4_%d" % k) for k in range(2)]
        for k in range(2):
            load_dv(l, k, 5, Ab[k][:, :])
        TBMAX = max(b[1] for b in blocks)
        hT = A.alloc([44, TBMAX], BF16, "hT")
        xt = [A.alloc([D], F32, "xt%d" % i) for i in range(2)]
        tmpc = [A.alloc([512], F32, "tmpc%d" % i) for i in range(2)]
        sqj = A.alloc([512], BF16, "sqj")
        ssv = [A.alloc([8], F32, "ssv%d" % i) for i in range(2)]
        rs = [A.alloc([1], F32, "rs%d" % i) for i in range(2)]
        mT = A.alloc([16, TBMAX], BF16, "mT")
        ssq = A.alloc([32], F32, "ssq")
        mark = A.off
        msrc = S["modT"].ap.rearrange("(kc p) t -> p kc t", p=128)

        def load_mT(t0_, tbn_):
            for q in range(4):
                P.dma("sp", mT[:, 4 * q:4 * q + 4, 0:tbn_], V(S["modT"], msrc[:, 4 * q:4 * q + 4, t0_:t0_ + tbn_]))

        load_mT(blocks[0][0], blocks[0][1])
        gsrc = I["w_ffn_gate"].ap[l].rearrange("(kc p) n -> p kc n", p=128)
        usrc = I["w_ffn_up"].ap[l].rearrange("(kc p) n -> p kc n", p=128)
        dsrc = I["w_ffn_down"].ap[l].rearrange("(fc p) n -> p fc n", p=128)
        WSZ = 16 * 512 * 2 // 4
        off_pair = [mark, mark + 2 * WSZ]
        off_sg = mark + 4 * WSZ
        post_jobs = []

        def alloc_at(off, shape, dt, name):
            A.off = off
            return A.alloc(shape, dt, name)

        for bi, (t0, tbn, sub) in enumerate(blocks):
            ntile = tbn // 128
            sg = [alloc_at(off_sg + i * 512, [512], F32, "sg%d" % i) for i in range(4)]
            pairs = {}
            pcn = 0
            for fg in range(11):
                par = fg % 2
                if par not in pairs or fg < 2:
                    pairs[par] = (alloc_at(off_pair[par], [16, 512], BF16, "wg%d" % par),
                                  alloc_at(off_pair[par] + WSZ, [16, 512], BF16, "wu%d" % par))
                g_, u_ = pairs[par]
                P.dma("pool", g_[:, :, :], V(I["w_ffn_gate"], gsrc[:, :, fg * 512:(fg + 1) * 512]))
                P.dma("pool", u_[:, :, :], V(I["w_ffn_up"], usrc[:, :, fg * 512:(fg + 1) * 512]))
                for q4 in range(4):
                    fc = fg * 4 + q4
                    for (s0, sn) in sub:
                        pg, pu = bk((pcn % 4) * 2), bk((pcn % 4) * 2 + 1)
                        sg_ = sg[pcn % 4]
                        pcn += 1
                        cs = slice(s0, s0 + sn)
                        for kc in range(16):
                            P.mm(pg[:, 0:sn], g_[:, kc, q4 * 128:(q4 + 1) * 128], mT[:, kc, cs], start=(kc == 0), stop=(kc == 15))
                        for kc in range(16):
                            P.mm(pu[:, 0:sn], u_[:, kc, q4 * 128:(q4 + 1) * 128], mT[:, kc, cs], start=(kc == 0), stop=(kc == 15))
                        P.act(sg_[:, 0:sn], pg[:, 0:sn], AF.Silu)
                        P.tt("dve", hT[:, fc, cs], sg_[:, 0:sn], pu[:, 0:sn], ALU.mult)
                        if fg == 0:
                            for _ in range(2):
                                if post_jobs:
                                    post_jobs.pop(0)()
                if fg == 0:
                    while post_jobs:
                        post_jobs.pop(0)()
            if bi + 1 < len(blocks):
                load_mT(blocks[bi + 1][0], blocks[bi + 1][1])
            wd = [alloc_at(off_pair[1] + i * 1024, [4, 512], BF16, "wd%d" % i) for i in range(2)]
            ybuf = alloc_at(off_pair[1] + 2048, [ntile, D], BF16, "ybuf")
            assert A.off <= off_sg
            dcn = 0
            P.memset("pool", ssq[:, :], 0.0)
            for nq in range(4):
                for f4 in range(11):
                    d_ = wd[dcn % 2]
                    dcn += 1
                    P.dma("pool", d_[:, :, :], V(I["w_ffn_down"], dsrc[:, f4 * 4:(f4 + 1) * 4, nq * 512:(nq + 1) * 512]))
                    for fi in range(4):
                        fc = f4 * 4 + fi
                        for tl in range(ntile):
                            P.mm(bk((nq * ntile + tl) % 8)[:, :], hT[:, fc, tl * 128:(tl + 1) * 128], d_[:, fi, :],
                                 start=(fc == 0), stop=(fc == 43))
                for tl in range(ntile):
                    k_ = 0 if (t0 + tl * 128) < NX else 1
                    acc_ = bk((nq * ntile + tl) % 8)
                    P.act(sqj[:, :], acc_[:, :], AF.Square, accum=ssq[:, tl * 4 + nq:tl * 4 + nq + 1])
                    P.tt("dve", ybuf[:, tl, nq * 512:(nq + 1) * 512], acc_[:, :], Ab[k_][:, nq * 512:(nq + 1) * 512], ALU.mult)

            def mk_job(tl, t0=t0, ntile=ntile, ybuf=ybuf):
                def job():
                    tok = t0 + tl * 128
                    k = 0 if tok < NX else 1
                    ys = [ybuf[:, tl, q * 512:(q + 1) * 512] for q in range(4)]
                    rows = slice(tok, tok + 128)
                    dst = OUT[rows, :] if final else S["X2"][rows, :]
                    if tl == 0:
                        P.dma("sp", xt[0][:, :], S["X1"][t0:t0 + 128, :])
                    if tl + 1 < ntile:
                        P.dma("sp", xt[(tl + 1) % 2][:, :], S["X1"][tok + 128:tok + 256, :])
                    sv, r_, x_ = ssv[tl % 2], rs[tl % 2], xt[tl % 2]
                    P.tt("dve", sv[:, 4:5], ssq[:, tl * 4:tl * 4 + 1], ssq[:, tl * 4 + 1:tl * 4 + 2], ALU.add)
                    P.tt("dve", sv[:, 5:6], ssq[:, tl * 4 + 2:tl * 4 + 3], ssq[:, tl * 4 + 3:tl * 4 + 4], ALU.add)
                    P.tt("dve", sv[:, 6:7], sv[:, 4:5], sv[:, 5:6], ALU.add)
                    rstd_from_ss(sv[:, 6:7], r_[:, :], D)
                    P.stt("dve", x_[:, :], ybuf[:, tl, :], r_[:, :], x_[:, :], ALU.mult, ALU.add)
                    P.dma("sp", dst, x_[:, :])
                return job

            post_jobs = [mk_job(tl) for tl in range(ntile)]
        while post_jobs:
            post_jobs.pop(0)()

    SUB768 = [(0, 512), (512, 256)]
    all_tiles = list(range(18))
    x_tiles = list(range(16))
    for l in range(n_layers):
        last = (l == DEPTH - 1)
        src = I["xall"] if l == 0 else S["X2"]
        if want("adaln") and l == 0:
            phase_adaln(l)
        MT_OFF = 52000 - 18432 - 64
        A.reset(MT_OFF)
        mTd = A.alloc([16, NTOK], BF16, "mTd")
        nbase = 0
        if want("norm1"):
            phase_norm(l, src, 0, 1, all_tiles, direct=mTd, base=nbase)
        if want("inproj"):
            phase_inproj(l, mTd, nbase)
        if want("hyena"):
            phase_hyena(l, NX, 0, False)
            if not last:
                phase_hyena(l, NCTX, NX, True)
        if want("mla"):
            sf = (lambda base, l=l: adaln_gen(l + 1, base, bank=7, nrow=1)) if (l + 1 < n_layers and want("adaln")) else None
            phase_mla(l, not last, sf)
        if want("na"):
            phase_na(l, not last, None)
        if want("outproj"):
            phase_outproj(l, src, x_tiles if last else all_tiles)
        if want("ffn"):
            phase_norm(l, S["X1"], 3, 4, x_tiles if last else all_tiles)
            if last:
                phase_ffn(l, [(0, 768, SUB768), (768, 768, SUB768), (1536, 512, [(0, 512)])], True)
            else:
                phase_ffn(l, [(0, 768, SUB768), (768, 768, SUB768), (1536, 768, SUB768)], False)
    P.barrier()
    P.emit()
    es.close()
    return nc


_WEIGHT_NAMES = ["w_ada", "b_ada", "g_attn_pre", "g_attn_post", "g_ffn_pre", "g_ffn_post", "w_in", "hy_conv_w",
                 "hy_conv_b", "hy_f_w1", "hy_f_b1", "hy_f_w2", "hy_f_b2", "hy_f_w3", "hy_f_freq", "hy_bias",
                 "mla_g_q", "mla_w_uq", "mla_g_kv", "mla_w_ukv", "w_out", "w_ffn_gate", "w_ffn_up", "w_ffn_down"]


def make_in_maps(inputs, cores):
    consts = _const_tables()
    shared = {k: np.ascontiguousarray(np.asarray(inputs[k], dtype=np.float32)) for k in _WEIGHT_NAMES}
    shared["rpbg"] = _na_gather(np.asarray(inputs["na_rpb"], dtype=np.float32))
    shared.update(consts)
    maps = []
    for b in cores:
        m = dict(shared)
        m["xall"] = np.ascontiguousarray(np.concatenate([inputs["x"][b], inputs["ctx"][b]], axis=0).astype(np.float32))
        m["cc"] = np.ascontiguousarray(np.stack([inputs["c"][b], inputs["c_ctx"]], axis=0).astype(np.float32))
        maps.append(m)
    return maps


def kernel(**inputs):
    inputs = {k: np.asarray(v) for k, v in inputs.items()}
    nc = build_program()
    maps = make_in_maps(inputs, list(range(N_CORES)))
    res = run_bass_kernel_spmd(nc, maps, core_ids=list(range(N_CORES)))
    return np.stack([np.asarray(r["out"], dtype=np.float32) for r in res.results], axis=0)
```

```python
import math
import contextlib
import numpy as np
import ml_dtypes
import concourse.bass as bass
import concourse.mybir as mybir
from concourse.bass_utils import run_bass_kernel_spmd

F32 = mybir.dt.float32
BF16 = mybir.dt.bfloat16
AF = mybir.ActivationFunctionType
ALU = mybir.AluOpType

N_DMA_SEMS = 40
N_CORES = 4

D = 2048
NX = 2048
NCTX = 256
NTOK = NX + NCTX
DEPTH = 2
GRID_W = 64
HY_W = 512
Q_LORA = 768
KV_LORA = 512
MLA_H = 8
NA_H = 4
FFN = 5632
IN_COLS = 4416
MLA_SCALE = (128 + 64) ** -0.5
NA_SCALE = 128 ** -0.5
EPS = 1e-6
C_HX, C_CQ, C_CKV, C_KR, C_QNA, C_KNA, C_VNA = 0, 1536, 2304, 2816, 2880, 3392, 3904
MASKV = -30000.0
TWO_PI = 2.0 * math.pi


class V:
    __slots__ = ("t", "ap")

    def __init__(self, t, ap):
        self.t = t
        self.ap = ap

    def __getitem__(self, idx):
        return V(self.t, self.ap[idx])


class T:
    def __init__(self, ap, name=""):
        self.ap = ap
        self.name = name
        self.w = None
        self.r = []
        self.excl = False

    def __getitem__(self, idx):
        return V(self, self.ap[idx])

    @property
    def v(self):
        return V(self, self.ap)


class Op:
    __slots__ = ("eng", "fn", "deps", "idx", "is_dma", "flag", "semval", "dsem", "dval", "nop")

    def __init__(self, eng, fn, deps, idx, is_dma):
        self.eng = eng
        self.fn = fn
        self.deps = deps
        self.idx = idx
        self.is_dma = is_dma
        self.flag = False
        self.semval = 0
        self.dsem = None
        self.dval = 0
        self.nop = False


ENGS = ["pe", "act", "dve", "pool", "sp"]


class Prog:
    def __init__(self, nc):
        self.nc = nc
        self.ops = {e: [] for e in ENGS}
        self.n_dma = 0
        self.dma_last = [None] * N_DMA_SEMS
        self.dma_cnt = [0] * N_DMA_SEMS
        self.dma_since_bar = []

    def add(self, eng, fn, reads=(), writes=(), is_dma=False):
        deps = set()
        for v in reads:
            t = v.t
            if t.w is not None:
                deps.add(t.w)
            if t.excl:
                deps.update(r for r in t.r if r.eng != eng)
        for v in writes:
            t = v.t
            if t.w is not None:
                deps.add(t.w)
            deps.update(t.r)
        op = Op(eng, fn, deps, len(self.ops[eng]), is_dma)
        if is_dma:
            s = self.n_dma % N_DMA_SEMS
            self.n_dma += 1
            if self.dma_last[s] is not None:
                op.deps.add(self.dma_last[s])
            self.dma_last[s] = op
            self.dma_cnt[s] += 1
            op.dsem = s
            op.dval = 16 * self.dma_cnt[s]
            self.dma_since_bar.append(op)
        for v in reads:
            v.t.r.append(op)
        for v in writes:
            v.t.w = op
            v.t.r = []
        self.ops[eng].append(op)
        return op

    def wait_ops(self, eng, toks):
        op = self.add(eng, lambda e: None)
        op.nop = True
        op.deps.update(toks)
        return op

    def barrier(self):
        toks = []
        for e in ENGS:
            for op in reversed(self.ops[e]):
                if not op.is_dma and not op.nop:
                    toks.append(op)
                    break
        toks.extend(self.dma_since_bar)
        self.dma_since_bar = []
        for e in ENGS:
            self.wait_ops(e, toks)

    def mm(self, out, lhsT, rhs, start=True, stop=True):
        return self.add("pe", lambda e: e.matmul(out.ap, lhsT.ap, rhs.ap, start=start, stop=stop),
                        reads=[lhsT, rhs], writes=[out])

    def transpose(self, out, in_, ident):
        return self.add("pe", lambda e: e.transpose(out.ap, in_.ap, ident.ap),
                        reads=[in_, ident], writes=[out])

    def act(self, out, in_, func, bias=None, scale=None, accum=None):
        reads = [in_]
        writes = [out]
        kw = {}
        if bias is not None:
            if isinstance(bias, V):
                reads.append(bias)
                kw["bias"] = bias.ap
            else:
                kw["bias"] = float(bias)
        if scale is not None:
            if isinstance(scale, V):
                reads.append(scale)
                kw["scale"] = scale.ap
            else:
                kw["scale"] = float(scale)
        if accum is not None:
            writes.append(accum)
            kw["accum_out"] = accum.ap
        return self.add("act", lambda e: e.activation(out.ap, in_.ap, func, **kw), reads=reads, writes=writes)

    def tt(self, eng, out, in0, in1, op):
        return self.add(eng, lambda e: e.tensor_tensor(out.ap, in0.ap, in1.ap, op), reads=[in0, in1], writes=[out])

    def ts(self, eng, out, in0, s1, op0, s2=None, op1=None):
        reads = [in0]
        a1 = s1.ap if isinstance(s1, V) else float(s1)
        if isinstance(s1, V):
            reads.append(s1)
        a2 = None
        if s2 is not None:
            a2 = s2.ap if isinstance(s2, V) else float(s2)
            if isinstance(s2, V):
                reads.append(s2)
        kw = {}
        if op1 is not None:
            kw["op1"] = op1
        return self.add(eng, lambda e: e.tensor_scalar(out.ap, in0.ap, a1, a2, op0, **kw), reads=reads, writes=[out])

    def stt(self, eng, out, in0, scalar, in1, op0, op1):
        reads = [in0, in1]
        a = scalar.ap if isinstance(scalar, V) else float(scalar)
        if isinstance(scalar, V):
            reads.append(scalar)
        return self.add(eng, lambda e: e.scalar_tensor_tensor(out.ap, in0.ap, a, in1.ap, op0, op1),
                        reads=reads, writes=[out])

    def copy(self, eng, out, in_):
        if eng == "act":
            return self.add("act", lambda e: e.copy(out.ap, in_.ap), reads=[in_], writes=[out])
        return self.add(eng, lambda e: e.tensor_copy(out.ap, in_.ap), reads=[in_], writes=[out])

    def memset(self, eng, out, val):
        return self.add(eng, lambda e: e.memset(out.ap, val), writes=[out])

    def recip(self, out, in_):
        return self.add("dve", lambda e: e.reciprocal(out.ap, in_.ap), reads=[in_], writes=[out])

    def dma(self, q, out, in_, **kw):
        return self.add(q, lambda e: e.dma_start(out.ap, in_.ap, **kw), reads=[in_], writes=[out], is_dma=True)

    def emit(self):
        nc = self.nc
        ops = self.ops
        for e in ENGS:
            for op in ops[e]:
                for d in op.deps:
                    if d.is_dma:
                        continue
                    if d.eng == e and e == "pe" and not op.is_dma:
                        continue
                    d.flag = True
        for e in ENGS:
            c = 0
            for op in ops[e]:
                if op.flag and not op.is_dma:
                    c += 1
                op.semval = c
        with contextlib.ExitStack() as es:
            esem = {e: es.enter_context(nc.semaphore("s_" + e)) for e in ENGS if e != "sp"}
            dsem = [es.enter_context(nc.semaphore("d%d" % i)) for i in range(N_DMA_SEMS)]
            block = es.enter_context(nc.Block())

            def run(e, eng):
                seen = {}
                for op in ops[e]:
                    waits = {}
                    for d in op.deps:
                        if d.is_dma:
                            key = ("d", d.dsem)
                            val = d.dval
                            sem = dsem[d.dsem]
                        else:
                            if d.eng == e and e == "pe" and not op.is_dma:
                                continue
                            if d.eng == e and d.idx >= op.idx:
                                continue
                            key = ("e", d.eng)
                            val = d.semval
                            sem = esem[d.eng]
                        if seen.get(key, 0) >= val:
                            continue
                        if key not in waits or waits[key][1] < val:
                            waits[key] = (sem, val)
                    for key, (sem, val) in waits.items():
                        eng.wait_ge(sem, val)
                        seen[key] = val
                    ins = op.fn(eng)
                    if ins is None:
                        continue
                    if op.is_dma:
                        ins.then_inc(dsem[op.dsem], 16)
                    elif op.flag:
                        ins.then_inc(esem[e], 1)

            @block.tensor
            def _(eng):
                run("pe", eng)

            @block.scalar
            def _(eng):
                run("act", eng)

            @block.vector
            def _(eng):
                run("dve", eng)

            @block.gpsimd
            def _(eng):
                run("pool", eng)

            @block.sync
            def _(eng):
                run("sp", eng)


class Arena:
    def __init__(self, ap, nwords):
        self.ap = ap
        self.n = nwords
        self.off = 0
        self.live = []

    def reset(self, mark=0):
        self.off = mark

    def alloc(self, shape, dt, name=""):
        nel = int(np.prod(shape))
        nbytes = nel * (4 if dt == F32 else 2)
        nw = (nbytes + 31) // 32 * 8
        assert self.off + nw <= self.n, ("SBUF arena overflow", name, self.off, nw, self.n)
        s0, e0 = self.off, self.off + nw
        a = self.ap[:, s0:e0]
        self.off += nw
        if dt != F32:
            a = a.bitcast(dt)
        a = a[:, 0:nel]
        if len(shape) > 1:
            names = ["d%d" % i for i in range(len(shape))]
            kw = {n: int(s) for n, s in zip(names[:-1], shape[:-1])}
            a = a.rearrange("p (%s) -> p %s" % (" ".join(names), " ".join(names)), **kw)
        t = T(a, name)
        keep = []
        for (s1, e1, t1) in self.live:
            if s1 < e0 and s0 < e1:
                t.r.extend(t1.r)
                if t1.w is not None:
                    t.r.append(t1.w)
                if s0 <= s1 and e1 <= e0:
                    continue
            keep.append((s1, e1, t1))
        keep.append((s0, e0, t))
        self.live = keep
        return t


def _bf(a):
    return np.ascontiguousarray(a.astype(ml_dtypes.bfloat16))


def _dft_tables(n):
    N = 2 * n
    nt = n // 128
    idx = np.arange(n, dtype=np.int64)
    prod = (idx[:, None] * idx[None, :]) % N
    ang = prod.astype(np.float64) * (2.0 * np.pi / N)
    Cm = np.cos(ang)
    Sm = np.sin(ang)
    F = np.stack([Cm, Sm], 0).reshape(2, nt, 128, nt, 128)
    F = F.transpose(3, 2, 0, 1, 4)
    I = np.stack([Cm, Sm], 0).reshape(2, nt, 128, n).transpose(1, 2, 0, 3)
    wgt = np.full((128, nt), 2.0 / N, np.float32)
    wgt[0, 0] = 1.0 / N
    return _bf(F), _bf(I), wgt


def _filter_tables(n):
    pos = np.arange(n, dtype=np.float32)
    t = np.linspace(0.0, 1.0, n, dtype=np.float32)
    bands = np.linspace(1e-4, 15, 16, dtype=np.float32)
    ang = np.float32(2.0 * math.pi / n) * pos[:, None] * bands[None, :]
    z = np.concatenate([t[:, None], np.cos(ang), -np.sin(ang)], axis=-1).astype(np.float32)
    negt = (-t).reshape(n // 128, 128).T.copy()
    return np.ascontiguousarray(z.T), np.ascontiguousarray(negt.astype(np.float32))


def _const_tables():
    c = {}
    c["dftF"], c["dftI"], c["wgt"] = _dft_tables(NX)
    c["dftFc"], c["dftIc"], c["wgtc"] = _dft_tables(NCTX)
    c["ztab"], c["negt"] = _filter_tables(NX)
    c["ztabc"], c["negtc"] = _filter_tables(NCTX)
    deltas = np.abs(np.linspace(math.log(1e-2) / 0.3, math.log(1e-2) / 1.5, HY_W, dtype=np.float32))
    c["deltab"] = np.ascontiguousarray(np.broadcast_to(deltas[None, :], (128, HY_W)).astype(np.float32))
    alt = np.where(np.arange(128) % 2 == 0, 1.0, -1.0).astype(np.float32)
    c["altcol"] = _bf(alt.reshape(128, 1))
    c["altrow"] = _bf(np.where(np.arange(NX) % 2 == 0, 1.0, -1.0).astype(np.float32).reshape(1, NX))
    tt_ = np.arange(NX)
    row = (tt_ // GRID_W).astype(np.float32)
    col = (tt_ % GRID_W).astype(np.float32)
    inv = (10000.0 ** (-np.arange(16, dtype=np.float32) / 16)).astype(np.float32)
    ang = np.concatenate([row[:, None] * inv, col[:, None] * inv], axis=-1)
    cs, sn = np.cos(ang).T, np.sin(ang).T
    cos2 = np.concatenate([cs, cs], 0).astype(np.float32)
    sin2 = np.concatenate([-sn, sn], 0).astype(np.float32)
    c["rope"] = np.ascontiguousarray(np.stack([cos2, sin2, cos2 * MLA_SCALE, sin2 * MLA_SCALE], 1).astype(np.float32))
    c["ident"] = _bf(np.eye(128, dtype=np.float32))
    c["namask"] = _na_index()[3]
    return c


_NA_CLASSES = [0, 1, 2, 14, 15]


def _na_cls(i):
    if i <= 1:
        return i
    if i <= 13:
        return 2
    return i - 11


def _na_j0(i):
    return min(max(i - 2, 0), 11)


_NA_IDX = None


def _na_index():
    global _NA_IDX
    if _NA_IDX is not None:
        return _NA_IDX
    ri = np.zeros((128, 5, 5, 128), np.int64)
    ci = np.zeros((128, 5, 5, 128), np.int64)
    ok = np.zeros((128, 5, 5, 128), bool)
    for cls, i in enumerate(_NA_CLASSES):
        j0 = _na_j0(i)
        for qr in range(2):
            r = 2 * i + qr
            r0 = min(max(r - 4, 0), 24)
            for c in range(5):
                for kr2 in range(2):
                    R = 2 * (j0 + c) + kr2
                    if not (r0 <= R <= r0 + 7):
                        continue
                    for qc in range(64):
                        cs = min(max(qc - 8, 0), 48)
                        kc = np.arange(cs, cs + 16)
                        ri[kr2 * 64 + kc, cls, c, qr * 64 + qc] = R - r + 7
                        ci[kr2 * 64 + kc, cls, c, qr * 64 + qc] = kc - qc + 15
                        ok[kr2 * 64 + kc, cls, c, qr * 64 + qc] = True
    mask = np.where(ok, 0.0, MASKV).astype(np.float32).reshape(128, 3200)
    _NA_IDX = (ri, ci, ok, np.ascontiguousarray(mask))
    return _NA_IDX


def _na_gather(rpb):
    ri, ci, ok, _ = _na_index()
    g = rpb[:, :, ri, ci] * ok[None, None].astype(np.float32)
    return np.ascontiguousarray(g.reshape(rpb.shape[0], 4, 128, 3200).astype(np.float32))


def build_program(n_layers=DEPTH, dbg=(), phases=None):
    nc = bass.Bass("TRN2", target_bir_lowering=False)
    P = Prog(nc)
    dbg = set(dbg)

    def din(name, shape, dt=F32):
        h = nc.dram_tensor(name, list(shape), dt, kind="ExternalInput")
        return T(h.ap(), name)

    def dscr(name, shape, dt=F32):
        kind = "ExternalOutput" if name in dbg else "Internal"
        h = nc.dram_tensor(name, list(shape), dt, kind=kind)
        return T(h.ap(), name)

    I = {}
    I["xall"] = din("xall", [NTOK, D])
    I["cc"] = din("cc", [2, D])
    for nm, shp in [("w_ada", [DEPTH, D, 6 * D]), ("b_ada", [DEPTH, 6 * D]), ("g_attn_pre", [DEPTH, D]),
                    ("g_attn_post", [DEPTH, D]), ("g_ffn_pre", [DEPTH, D]), ("g_ffn_post", [DEPTH, D]),
                    ("w_in", [DEPTH, D, IN_COLS]), ("hy_conv_w", [DEPTH, 3, 1536]), ("hy_conv_b", [DEPTH, 1536]),
                    ("hy_f_w1", [DEPTH, 33, 64]), ("hy_f_b1", [DEPTH, 64]), ("hy_f_w2", [DEPTH, 64, 64]),
                    ("hy_f_b2", [DEPTH, 64]), ("hy_f_w3", [DEPTH, 64, 1024]), ("hy_f_freq", [DEPTH, 64]),
                    ("hy_bias", [DEPTH, 512]), ("mla_g_q", [DEPTH, Q_LORA]), ("mla_w_uq", [DEPTH, Q_LORA, 1536]),
                    ("mla_g_kv", [DEPTH, KV_LORA]), ("mla_w_ukv", [DEPTH, KV_LORA, 2048]),
                    ("w_out", [DEPTH, D, D]), ("w_ffn_gate", [DEPTH, D, FFN]), ("w_ffn_up", [DEPTH, D, FFN]),
                    ("w_ffn_down", [DEPTH, FFN, D]),
                    ("rpbg", [DEPTH, 4, 128, 3200]), ("namask", [128, 3200]),
                    ("ztab", [33, NX]), ("ztabc", [33, NCTX]), ("negt", [128, 16]), ("negtc", [128, 2]),
                    ("deltab", [128, HY_W]), ("wgt", [128, 16]), ("wgtc", [128, 2]), ("rope", [64, 4, NX])]:
        I[nm] = din(nm, shp)
    for nm, shp in [("dftF", [16, 128, 2, 16, 128]), ("dftI", [16, 128, 2, NX]), ("dftFc", [2, 128, 2, 2, 128]),
                    ("dftIc", [2, 128, 2, NCTX]), ("altcol", [128, 1]), ("altrow", [1, NX]), ("ident", [128, 128])]:
        I[nm] = din(nm, shp, BF16)
    OUT = T(nc.dram_tensor("out", [NX, D], F32, kind="ExternalOutput").ap(), "out")

    S = {}
    S["dv"] = dscr("S_dv", [DEPTH, 2, 6, D])
    S["modT"] = dscr("S_modT", [D, NTOK], BF16)
    S["hxT"] = dscr("S_hxT", [1536, NTOK])
    S["cqT"] = dscr("S_cqT", [Q_LORA, NTOK], BF16)
    S["ckvT"] = dscr("S_ckvT", [KV_LORA, NTOK], BF16)
    S["krT"] = dscr("S_krT", [64, NTOK], BF16)
    S["qnaT"] = dscr("S_qnaT", [512, NTOK], BF16)
    S["knaT"] = dscr("S_knaT", [512, NTOK], BF16)
    S["vna"] = dscr("S_vna", [NTOK, 512], BF16)
    S["catT"] = dscr("S_catT", [D, NTOK], BF16)
    S["X1"] = dscr("S_X1", [NTOK, D])
    S["X2"] = dscr("S_X2", [NTOK, D])

    es = contextlib.ExitStack()
    AR_WORDS = 52000
    arena_h = es.enter_context(nc.sbuf_tensor("arena", [128, AR_WORDS], F32))
    psum_h = es.enter_context(nc.psum_tensor("psum", [128, 4096], F32))
    A = Arena(arena_h, AR_WORDS)
    banks = [T(psum_h[:, i * 512:(i + 1) * 512], "bank%d" % i) for i in range(8)]
    for b_ in banks:
        b_.excl = True

    def bk(i, dt=F32):
        t = banks[i]
        return V(t, t.ap if dt == F32 else t.ap.bitcast(dt))

    def bcast_rows(t, row_off, n, parts=128):
        return V(t, bass.AP(t.ap.tensor, row_off, [[0, parts], [1, n]]))

    epsc_box = [None]

    def new_phase(base=0):
        A.reset(base)
        e_ = A.alloc([1], F32, "epsc")
        P.memset("pool", e_[:, :], EPS)
        epsc_box[0] = e_

    class _Eps:
        def __getitem__(self, idx):
            return epsc_box[0][idx]

    epsc = _Eps()

    def want(ph):
        return phases is None or ph in phases

    ADA_BASE = 52000 - 20640
    ada_end = [0]

    def adaln_gen(l, base, bank=6, nrow=2):
        save = A.off
        A.off = base
        ccx = A.alloc([16], F32, "ccx")
        ccc = A.alloc([16], F32, "ccc")
        scT = A.alloc([16, 2], BF16, "scT")
        wb = [A.alloc([16, 512], BF16, "wada%d" % i) for i in range(2)]
        mrow = [A.alloc([D], F32, "mrow%d" % i) for i in range(nrow)]
        grow = [A.alloc([D], F32, "grow%d" % i) for i in range(nrow)]
        drow = [A.alloc([D], F32, "drow%d" % i) for i in range(nrow)]
        ada_end[0] = A.off
        A.off = save
        P.dma("sp", ccx[:, :], V(I["cc"], I["cc"].ap[0, :].rearrange("(p j) -> p j", j=16)))
        P.dma("sp", ccc[:, :], V(I["cc"], I["cc"].ap[1, :].rearrange("(p j) -> p j", j=16)))
        P.act(scT[:, :, 0], ccx[:, :], AF.Silu)
        P.act(scT[:, :, 1], ccc[:, :], AF.Silu)
        wsrc = I["w_ada"].ap[l].rearrange("(p j) n -> p j n", j=16)
        plan = {0: (1, None, False), 1: (0, "g_attn_pre", True), 2: (2, "g_attn_post", False),
                3: (4, None, False), 4: (3, "g_ffn_pre", True), 5: (5, "g_ffn_post", False)}
        ps = bk(bank)
        for mi in range(6):
            mr, gr, dr = mrow[mi % nrow], grow[mi % nrow], drow[mi % nrow]
            slot, gname, addone = plan[mi]
            P.dma("sp", mr[0:2, :], bcast_rows(I["b_ada"], l * 6 * D + mi * D, D, 2))
            if gname is not None:
                P.dma("sp", gr[0:2, :], bcast_rows(I[gname], l * D, D, 2))
            for q in range(4):
                nb = mi * 4 + q
                w = wb[nb % 2]
                P.dma("pool", w[:, :, :], V(I["w_ada"], wsrc[:, :, nb * 512:(nb + 1) * 512]))
                for j in range(16):
                    P.mm(ps[0:2, :], scT[:, j, :], w[:, j, :], start=(j == 0), stop=(j == 15))
                P.tt("dve", mr[0:2, q * 512:(q + 1) * 512], ps[0:2, :], mr[0:2, q * 512:(q + 1) * 512], ALU.add)
                yield
            if gname is None:
                src_row = mr
            else:
                if addone:
                    P.stt("dve", dr[0:2, :], mr[0:2, :], 1.0, gr[0:2, :], ALU.add, ALU.mult)
                else:
                    P.tt("dve", dr[0:2, :], mr[0:2, :], gr[0:2, :], ALU.mult)
                src_row = dr
            P.dma("sp", V(S["dv"], S["dv"].ap[l, :, slot, :]), src_row[0:2, :])
            yield

    def phase_adaln(l):
        new_phase()
        for _ in adaln_gen(l, A.off):
            pass

    def load_dv(l, kind, idx, dst):
        off = ((l * 2 + kind) * 6 + idx) * D
        P.dma("sp", dst, bcast_rows(S["dv"], off, D, 128))

    def rstd_from_ss(ss, rstd, n):
        P.act(rstd, ss, AF.Sqrt, scale=1.0 / n, bias=epsc[:, :])
        P.recip(rstd, rstd)

    def load_cols(dst, src_rows, r, npart, ident_bf, bank=6):
        rowbuf = A.alloc([128], F32, "rowbuf")
        identf = A.alloc([128], F32, "identf")
        P.copy("dve", identf[:, :], ident_bf[:, :])
        P.dma("sp", rowbuf[0:r, 0:npart], src_rows)
        ps = bk(bank)
        P.mm(ps[0:npart, 0:r], rowbuf[0:r, 0:npart], identf[0:r, 0:r])
        P.copy("dve", dst, ps[0:npart, 0:r])

    def phase_norm(l, src, a_idx, b_idx, tiles, direct=None, base=0):
        new_phase(base)
        ident = A.alloc([128], BF16, "ident")
        P.dma("sp", ident[:, :], I["ident"][:, :])
        Ab = [A.alloc([D], F32, "Ab%d" % k) for k in range(2)]
        Bb = [A.alloc([D], F32, "Bb%d" % k) for k in range(2)]
        for k in range(2):
            load_dv(l, k, a_idx, Ab[k][:, :])
            load_dv(l, k, b_idx, Bb[k][:, :])
        xt = [A.alloc([D], F32, "xt%d" % i) for i in range(3)]
        xm = [A.alloc([D], F32, "xm%d" % i) for i in range(2)]
        xb = [A.alloc([D], BF16, "xb%d" % i) for i in range(2)]
        junk = A.alloc([D], BF16, "junk")
        ss = [A.alloc([1], F32, "ss%d" % i) for i in range(3)]
        rs = [A.alloc([1], F32, "rs%d" % i) for i in range(3)]
        stage = [A.alloc([16, 512], BF16, "stage%d" % i) for i in range(2)]
        dstT = S["modT"].ap.rearrange("(kc p) t -> p kc t", p=128)
        groups = [tiles[i:i + 4] for i in range(0, len(tiles), 4)]
        seq = []
        for gi, grp in enumerate(groups):
            for ti, tile in enumerate(grp):
                seq.append((gi, ti, tile, ti == len(grp) - 1, grp))

        def s1(n):
            (gi, ti, tile, lastg, grp) = seq[n]
            x = xt[n % 3]
            P.dma("sp", x[:, :], src[tile * 128:(tile + 1) * 128, :])
            P.memset("pool", ss[n % 3][:, :], 0.0)
            P.act(junk[:, :], x[:, :], AF.Square, accum=ss[n % 3][:, :])
            P.act(rs[n % 3][:, :], ss[n % 3][:, :], AF.Sqrt, scale=1.0 / D, bias=epsc[:, :])

        def s2(n):
            (gi, ti, tile, lastg, grp) = seq[n]
            st = stage[gi % 2]
            k = 0 if tile < 16 else 1
            x = xt[n % 3]
            b = xb[n % 2]
            xm_ = xm[n % 2]
            P.recip(rs[n % 3][:, :], rs[n % 3][:, :])
            P.stt("dve", xm_[:, :], x[:, :], rs[n % 3][:, :], Ab[k][:, :], ALU.mult, ALU.mult)
            P.tt("dve", b[:, 0:1024], xm_[:, 0:1024], Bb[k][:, 0:1024], ALU.add)
            P.tt("pool", b[:, 1024:2048], xm_[:, 1024:2048], Bb[k][:, 1024:2048], ALU.add)
            for g in range(4):
                pb = bk((n * 4 + g) % 4 + 4, BF16)
                for c in range(4):
                    kc = 4 * g + c
                    P.transpose(pb[:, c * 128:(c + 1) * 128], b[:, kc * 128:(kc + 1) * 128], ident[:, :])
                src_ps = V(pb.t, pb.ap[:, 0:512].rearrange("p (c t) -> p c t", c=4))
                if direct is not None:
                    P.copy("act", direct[:, 4 * g:4 * g + 4, tile * 128:(tile + 1) * 128], src_ps)
                else:
                    P.copy("act", st[:, 4 * g:4 * g + 4, ti * 128:(ti + 1) * 128], src_ps)
            if lastg and direct is None:
                t0 = grp[0] * 128
                nt = len(grp) * 128
                P.dma("pool", V(S["modT"], dstT[:, :, t0:t0 + nt]), st[:, :, 0:nt])

        assert direct is None or A.off <= 52000 - 18432 - 64, A.off
        s1(0)
        if len(seq) > 1:
            s1(1)
        for n in range(len(seq)):
            if n + 2 < len(seq):
                s1(n + 2)
            s2(n)

    def phase_inproj(l, mT, base):
        new_phase(base)
        wb = [A.alloc([16, 512], BF16, "win%d" % i) for i in range(2)]
        wsrc = I["w_in"].ap[l].rearrange("(kc p) n -> p kc n", p=128)
        ones = A.alloc([128], BF16, "ones")
        P.memset("dve", ones[:, :], 1.0)
        identb = A.alloc([128], BF16, "identb")
        P.dma("sp", identb[:, :], I["ident"][:, :])
        st32 = [A.alloc([NTOK], F32, "st32_%d" % i) for i in range(2)]
        raw = A.alloc([6, NTOK], BF16, "raw")
        sq = [A.alloc([512], BF16, "sq%d" % i) for i in range(2)]
        rstd = A.alloc([NTOK], F32, "rstdb")
        gcols = {"mla_g_q": A.alloc([8], F32, "gcolq"), "mla_g_kv": A.alloc([8], F32, "gcolkv")}
        for gname_, nch_ in (("mla_g_q", 6), ("mla_g_kv", 4)):
            load_cols(gcols[gname_][:, 0:nch_], V(I[gname_], I[gname_].ap[l].rearrange("(c p) -> c p", p=128)), nch_, 128, identb)
        rope = A.alloc([2, NX], F32, "ropek")
        P.dma("sp", rope[0:64, :, :], I["rope"][:, 0:2, :])
        stb = [A.alloc([NTOK], BF16, "stb%d" % i) for i in range(2)]
        vst = [A.alloc([512], BF16, "vst%d" % i) for i in range(2)]
        tmp = A.alloc([512], F32, "tmpc")
        tblocks = [(i * 512, 512) for i in range(4)] + [(2048, 256)]
        wcnt = [0]
        pcnt = [0]
        assert A.off + 1024 <= 52000 - 18432 - 64, A.off

        def load_w(c0, ncols, dst_c0=0, w=None):
            if w is None:
                w = wb[wcnt[0] % 2]
                wcnt[0] += 1
            P.dma("pool", w[:, :, dst_c0:dst_c0 + ncols], V(I["w_in"], wsrc[:, :, c0:c0 + ncols]))
            return w

        def proj_fm(w, wc0, m, tb0, tbn):
            ps = bk((0, 1, 4, 5)[pcnt[0] % 4])
            pcnt[0] += 1
            for kc in range(16):
                P.mm(ps[0:m, 0:tbn], w[:, kc, wc0:wc0 + m], mT[:, kc, tb0:tb0 + tbn], start=(kc == 0), stop=(kc == 15))
            return ps[0:m, 0:tbn]

        n = 0
        for g in (range(3) if want("ip_hx") else []):
            w = load_w(C_HX + g * 512, 512)
            for cc in range(4):
                st = st32[n % 2]
                for (tb0, tbn) in tblocks:
                    ps = proj_fm(w, cc * 128, 128, tb0, tbn)
                    P.copy("act" if (tb0 // 512) % 2 == 0 else "dve", st[:, tb0:tb0 + tbn], ps)
                r0 = (g * 4 + cc) * 128
                P.dma("sp", S["hxT"][r0:r0 + 128, :], st[:, :])
                n += 1

        def latent(c0, nch, gname, dst, nfeat):
            gcol = gcols[gname]
            ws = []
            for g in range((nch + 3) // 4):
                ncols = min(512, nch * 128 - g * 512)
                ws.append(load_w(c0 + g * 512, ncols))
            for (tb0, tbn) in tblocks:
                acc = bk(2 + (tb0 // 512) % 2)
                pend = None
                for ch in range(nch):
                    ps = proj_fm(ws[ch // 4], (ch % 4) * 128, 128, tb0, tbn)
                    s = sq[ch % 2]
                    P.copy("dve", raw[:, ch, tb0:tb0 + tbn], ps)
                    P.act(s[:, 0:tbn], raw[:, ch, tb0:tb0 + tbn], AF.Square)
                    if pend is not None:
                        P.mm(acc[:, 0:tbn], ones[:, :], pend[0][:, 0:tbn], start=(pend[1] == 0), stop=False)
                    pend = (s, ch)
                P.mm(acc[:, 0:tbn], ones[:, :], pend[0][:, 0:tbn], start=(pend[1] == 0), stop=True)
                P.ts("dve", rstd[:, tb0:tb0 + tbn], acc[:, 0:tbn], 1.0 / nfeat, ALU.mult, EPS, ALU.add)
                P.act(rstd[:, tb0:tb0 + tbn], rstd[:, tb0:tb0 + tbn], AF.Sqrt)
                P.recip(rstd[:, tb0:tb0 + tbn], rstd[:, tb0:tb0 + tbn])
            for ch in range(nch):
                sb_ = stb[ch % 2]
                P.stt("dve", sb_[:, :], raw[:, ch, :], gcol[:, ch:ch + 1], rstd[:, :], ALU.mult, ALU.mult)
                P.dma("sp", dst[ch * 128:(ch + 1) * 128, :], sb_[:, :])

        if want("ip_lat"):
            latent(C_CQ, 6, "mla_g_q", S["cqT"], Q_LORA)
            latent(C_CKV, 4, "mla_g_kv", S["ckvT"], KV_LORA)
        if not want("ip_rest"):
            return

        w = wb[wcnt[0] % 2]
        wcnt[0] += 1
        load_w(C_KR, 64, 0, w)
        load_w(C_KR + 32, 32, 64, w)
        load_w(C_KR, 32, 96, w)
        sb_ = stb[0]
        for (tb0, tbn) in tblocks:
            pa = proj_fm(w, 0, 64, tb0, tbn)
            if tb0 < NX:
                pbb = proj_fm(w, 64, 64, tb0, tbn)
                P.tt("dve", tmp[0:64, 0:tbn], pa, rope[0:64, 0, tb0:tb0 + tbn], ALU.mult)
                P.tt("dve", st32[0][0:64, 0:tbn], pbb, rope[0:64, 1, tb0:tb0 + tbn], ALU.mult)
                P.tt("dve", sb_[0:64, tb0:tb0 + tbn], tmp[0:64, 0:tbn], st32[0][0:64, 0:tbn], ALU.add)
            else:
                P.copy("dve", sb_[0:64, tb0:tb0 + tbn], pa)
        P.dma("sp", S["krT"][:, :], sb_[0:64, :])

        n = 0
        for (c0, dst, scale) in [(C_QNA, S["qnaT"], NA_SCALE), (C_KNA, S["knaT"], 1.0)]:
            w = load_w(c0, 512)
            for cc in range(4):
                sb_ = stb[n % 2]
                n += 1
                for (tb0, tbn) in tblocks:
                    ps = proj_fm(w, cc * 128, 128, tb0, tbn)
                    P.act(sb_[:, tb0:tb0 + tbn], ps, AF.Identity, scale=scale)
                P.dma("sp", dst[cc * 128:(cc + 1) * 128, :], sb_[:, :])

        w = load_w(C_VNA, 512)
        for tile in range(NTOK // 128):
            ps = bk((0, 1, 4, 5)[tile % 4])
            for kc in range(16):
                P.mm(ps[:, :], mT[:, kc, tile * 128:(tile + 1) * 128], w[:, kc, :], start=(kc == 0), stop=(kc == 15))
            vs = vst[tile % 2]
            P.copy("act" if tile % 2 == 0 else "dve", vs[:, :], ps[:, :])
            P.dma("sp", S["vna"][tile * 128:(tile + 1) * 128, :], vs[:, :])

    def sin_wrapped(dst, arg, tmpv):
        for _ in range(2):
            P.ts("dve", tmpv, arg, math.pi, ALU.is_gt, -TWO_PI, ALU.mult)
            P.tt("dve", arg, arg, tmpv, ALU.add)
            P.ts("dve", tmpv, arg, -math.pi, ALU.is_lt, TWO_PI, ALU.mult)
            P.tt("dve", arg, arg, tmpv, ALU.add)
        P.act(dst, arg, AF.Sin)

    def phase_hyena(l, n, tok0, ctxmode):
        new_phase()
        nt = n // 128
        NN = 2 * n
        zt_in = I["ztabc"] if ctxmode else I["ztab"]
        negt_in = I["negtc"] if ctxmode else I["negt"]
        wgt_in = I["wgtc"] if ctxmode else I["wgt"]
        dF = I["dftFc"] if ctxmode else I["dftF"]
        dI = I["dftIc"] if ctxmode else I["dftI"]
        nb_cols = min(512, n)
        nblk = n // nb_cols

        ident = A.alloc([128], BF16, "ident")
        P.dma("sp", ident[:, :], I["ident"][:, :])
        G = A.alloc([nt, 512], BF16, "G")
        Dd = A.alloc([nt, 512], BF16, "Dd")
        zin = A.alloc([nt, 512], BF16, "zin")
        Y = A.alloc([nt, 2, 512], BF16, "Y")
        x2u = A.alloc([4, n], F32, "x2u")
        altc = A.alloc([1], BF16, "altc")
        altr = A.alloc([n], BF16, "altr")
        yny = A.alloc([512], BF16, "yny")
        wgt = A.alloc([nt], F32, "wgt")
        negt = A.alloc([nt], F32, "negt")
        P.dma("sp", altc[:, :], I["altcol"][:, :])
        P.dma("sp", altr[0:1, :], I["altrow"][:, 0:n])
        P.dma("sp", wgt[:, :], wgt_in[:, :])
        P.dma("sp", negt[:, :], negt_in[:, :])
        mark = A.off

        zt = A.alloc([n], F32, "zt")
        h1 = A.alloc([n], F32, "h1")
        h2 = A.alloc([n], F32, "h2")
        w1 = A.alloc([64], F32, "w1")
        w2 = A.alloc([64], F32, "w2")
        w3 = A.alloc([1024], F32, "w3")
        vec = A.alloc([8], F32, "vec")
        arg = A.alloc([512], F32, "arg")
        tmpv = A.alloc([512], F32, "tmpv")
        delt = A.alloc([512], F32, "delt")
        dec = A.alloc([512], F32, "dec")
        hf = A.alloc([512], F32, "hf")
        hb = A.alloc([512], F32, "hb")
        brow = A.alloc([512], F32, "brow")
        P.dma("sp", zt[0:33, :], zt_in[:, :])
        P.dma("sp", w1[0:33, :], V(I["hy_f_w1"], I["hy_f_w1"].ap[l]))
        P.dma("sp", w2[0:64, :], V(I["hy_f_w2"], I["hy_f_w2"].ap[l]))
        P.dma("sp", w3[0:64, :], V(I["hy_f_w3"], I["hy_f_w3"].ap[l]))
        for i, nm in enumerate(["hy_f_freq", "hy_f_b1", "hy_f_b2"]):
            load_cols(vec[0:64, i:i + 1], V(I[nm], I[nm].ap[l:l + 1, :]), 1, 64, ident)
        P.tt("dve", vec[0:64, 3:4], vec[0:64, 0:1], vec[0:64, 1:2], ALU.mult)
        P.tt("dve", vec[0:64, 4:5], vec[0:64, 0:1], vec[0:64, 2:3], ALU.mult)
        P.dma("sp", delt[:, :], I["deltab"][:, :])
        P.dma("sp", brow[0:1, :], V(I["hy_bias"], I["hy_bias"].ap[l:l + 1, :]))
        for (wm, kdim, src, dst, bcol) in [(w1, 33, zt, h1, 3), (w2, 64, h1, h2, 4)]:
            for b in range(nblk):
                ps = bk(b % 2)
                cs = slice(b * nb_cols, (b + 1) * nb_cols)
                P.mm(ps[0:64, 0:nb_cols], wm[0:kdim, 0:64], src[0:kdim, cs])
                P.ts("dve", arg[0:64, 0:nb_cols], ps[0:64, 0:nb_cols], vec[0:64, 0:1], ALU.mult, vec[0:64, bcol:bcol + 1], ALU.add)
                sin_wrapped(dst[0:64, cs], arg[0:64, 0:nb_cols], tmpv[0:64, 0:nb_cols])
        for jt in range(nt):
            pf, pb_ = bk(2), bk(3)
            P.mm(pf[:, :], h2[0:64, jt * 128:(jt + 1) * 128], w3[0:64, 0:512])
            P.mm(pb_[:, :], h2[0:64, jt * 128:(jt + 1) * 128], w3[0:64, 512:1024])
            P.act(dec[:, :], delt[:, :], AF.Exp, scale=negt[:, jt:jt + 1])
            P.tt("dve", hf[:, :], pf[:, :], dec[:, :], ALU.mult)
            P.tt("dve", hb[:, :], pb_[:, :], dec[:, :], ALU.mult)
            P.tt("dve", G[:, jt, :], hf[:, :], hb[:, :], ALU.add)
            P.tt("dve", Dd[:, jt, :], hb[:, :], hf[:, :], ALU.subtract)
            if jt == 0:
                P.tt("dve", G[0:1, 0, :], hf[0:1, :], brow[0:1, :], ALU.add)
        if "S_G" in dbg and not ctxmode:
            S["G"] = dscr("S_G", [128, nt, 512], BF16)
            P.dma("sp", S["G"][:, :, :], G[:, :, :])

        A.off = mark
        cw = A.alloc([3, 12], F32, "cw")
        cb = A.alloc([12], F32, "cb")
        load_cols(V(cw, cw.ap.rearrange("p k m -> p (k m)")),
                  V(I["hy_conv_w"], I["hy_conv_w"].ap[l].rearrange("k (m p) -> (k m) p", p=128)), 36, 128, ident)
        load_cols(cb[:, :], V(I["hy_conv_b"], I["hy_conv_b"].ap[l].rearrange("(m p) -> m p", p=128)), 12, 128, ident)
        xr = [A.alloc([n], F32, "xr%d" % i) for i in range(2)]
        uv = A.alloc([n], F32, "uv")
        u1 = A.alloc([n], F32, "u1")
        zT = A.alloc([n], BF16, "zT")

        def sconv(dst, m, src):
            P.act(dst, src[:, :], AF.Identity, scale=cw[:, 1, m:m + 1], bias=cb[:, m:m + 1])
            P.stt("dve", dst[:, 1:n], src[:, 0:n - 1], cw[:, 0, m:m + 1], dst[:, 1:n], ALU.mult, ALU.add)
            P.stt("dve", dst[:, 0:n - 1], src[:, 1:n], cw[:, 2, m:m + 1], dst[:, 0:n - 1], ALU.mult, ALU.add)

        cnt = 0
        for c in range(4):
            for part, dst in [(0, uv[:, :]), (1, u1[:, :]), (2, x2u[:, c, :])]:
                m = part * 4 + c
                x = xr[cnt % 2]
                cnt += 1
                P.dma("sp", x[:, :], S["hxT"][m * 128:(m + 1) * 128, tok0:tok0 + n])
                sconv(dst, m, x)
            P.tt("dve", zT[:, :], u1[:, :], uv[:, :], ALU.mult)
            for st in range(nt):
                pb = bk(4 + st % 4, BF16)
                P.transpose(pb[:, 0:128], zT[:, st * 128:(st + 1) * 128], ident[:, :])
                P.copy("act", zin[:, st, c * 128:(c + 1) * 128], pb[:, 0:128])

        A.off = mark
        Fb = [A.alloc([2, nt, 128], BF16, "Fb%d" % i) for i in range(2)]
        kc_ = A.alloc([512], F32, "kc")
        ks_ = A.alloc([512], F32, "ks")
        t1 = A.alloc([512], F32, "t1")
        t2 = A.alloc([512], F32, "t2")
        t3 = A.alloc([512], F32, "t3")
        t4 = A.alloc([512], F32, "t4")
        for ft in range(nt):
            Fb_ = Fb[ft % 2]
            P.dma("sp", Fb_[:, :, :, :], V(dF, dF.ap[ft]))
            b0_ = (ft % 2) * 4
            zc, zs, kcp, ksp = bk(b0_), bk(b0_ + 1), bk(b0_ + 2), bk(b0_ + 3)
            for st in range(nt):
                f, la = (st == 0), (st == nt - 1)
                P.mm(zc[:, :], Fb_[:, 0, st, :], zin[:, st, :], start=f, stop=la)
                P.mm(kcp[:, :], Fb_[:, 0, st, :], G[:, st, :], start=f, stop=la)
                P.mm(zs[:, :], Fb_[:, 1, st, :], zin[:, st, :], start=f, stop=la)
                P.mm(ksp[:, :], Fb_[:, 1, st, :], Dd[:, st, :], start=f, stop=la)
            P.act(kc_[:, :], kcp[:, :], AF.Identity, scale=wgt[:, ft:ft + 1])
            P.act(ks_[:, :], ksp[:, :], AF.Identity, scale=wgt[:, ft:ft + 1])
            P.tt("dve", t1[:, :], zc[:, :], kc_[:, :], ALU.mult)
            P.tt("dve", t2[:, :], zs[:, :], ks_[:, :], ALU.mult)
            P.tt("dve", t3[:, :], zs[:, :], kc_[:, :], ALU.mult)
            P.tt("dve", t4[:, :], zc[:, :], ks_[:, :], ALU.mult)
            P.tt("pool", Y[:, ft, 0, :], t1[:, :], t2[:, :], ALU.add)
            P.tt("pool", Y[:, ft, 1, :], t3[:, :], t4[:, :], ALU.subtract)
        zn, kn = bk(0), bk(1)
        for st in range(nt):
            P.mm(zn[0:1, :], altc[:, 0:1], zin[:, st, :], start=(st == 0), stop=(st == nt - 1))
            P.mm(kn[0:1, :], altc[:, 0:1], G[:, st, :], start=(st == 0), stop=(st == nt - 1))
        P.act(t1[0:1, :], kn[0:1, :], AF.Identity, scale=1.0 / NN)
        P.tt("dve", yny[0:1, :], zn[0:1, :], t1[0:1, :], ALU.mult)

        A.off = mark
        Ib = [A.alloc([2, n], BF16, "Ib%d" % i) for i in range(2)]
        ost = [A.alloc([n], BF16, "ost%d" % i) for i in range(2)]
        ntb = n // nb_cols
        per_pass = max(1, 8 // ntb)
        per_pass = min(per_pass, 4)
        for p0 in range(0, 4, per_pass):
            cl_list = list(range(p0, min(4, p0 + per_pass)))
            for ft in range(nt):
                Ib_ = Ib[ft % 2]
                P.dma("sp", Ib_[:, :, :], V(dI, dI.ap[ft]))
                for ci, c in enumerate(cl_list):
                    for tb in range(ntb):
                        acc = bk(ci * ntb + tb)
                        ts_ = slice(tb * nb_cols, (tb + 1) * nb_cols)
                        P.mm(acc[:, 0:nb_cols], Y[:, ft, 0, c * 128:(c + 1) * 128], Ib_[:, 0, ts_], start=(ft == 0), stop=False)
                        P.mm(acc[:, 0:nb_cols], Y[:, ft, 1, c * 128:(c + 1) * 128], Ib_[:, 1, ts_], start=False, stop=False)
            for ci, c in enumerate(cl_list):
                o = ost[c % 2]
                for tb in range(ntb):
                    acc = bk(ci * ntb + tb)
                    ts_ = slice(tb * nb_cols, (tb + 1) * nb_cols)
                    P.mm(acc[:, 0:nb_cols], yny[0:1, c * 128:(c + 1) * 128], altr[0:1, ts_], start=False, stop=True)
                    P.tt("dve", o[:, ts_], acc[:, 0:nb_cols], x2u[:, c, ts_], ALU.mult)
                P.dma("pool", S["catT"][c * 128:(c + 1) * 128, tok0:tok0 + n], o[:, :])

    def attn_fin_a(po, obuf, rcp):
        P.recip(rcp[:, :], po[:, 128:129])
        P.act(obuf[:, :], po[:, 0:128], AF.Identity, scale=rcp[:, :])

    def attn_fin_b(obuf, dst_col, ident, stage):
        pt = bk(7, BF16)
        P.transpose(pt[:, 0:128], obuf[:, :], ident[:, :])
        P.copy("dve", stage[:, dst_col:dst_col + 128], pt[:, 0:128])

    def phase_mla(l, with_ctx_q, side_fn=None):
        new_phase()
        nq_tot = NTOK if with_ctx_q else NX
        ident = A.alloc([128], BF16, "ident")
        P.dma("sp", ident[:, :], I["ident"][:, :])
        cq = A.alloc([6, NTOK], BF16, "cq")
        ckv = A.alloc([4, NTOK], BF16, "ckv")
        kr = A.alloc([NTOK], BF16, "kr")
        rope = A.alloc([2, NX], F32, "ropeq")
        P.dma("sp", cq[:, :, :], V(S["cqT"], S["cqT"].ap.rearrange("(c p) t -> p c t", p=128)))
        P.dma("sp", ckv[:, :, :], V(S["ckvT"], S["ckvT"].ap.rearrange("(c p) t -> p c t", p=128)))
        P.dma("sp", kr[0:64, :], S["krT"][:, :])
        P.dma("sp", rope[0:64, :, :], I["rope"][:, 2:4, :])
        wq = [A.alloc([6, 256], BF16, "wq%d" % i) for i in range(2)]
        wkv = [A.alloc([4, 256], BF16, "wkv%d" % i) for i in range(2)]
        qn = A.alloc([NTOK], BF16, "qn")
        qr = A.alloc([NTOK], BF16, "qr")
        kn = A.alloc([NTOK], BF16, "kn")
        vv = A.alloc([18, 132], BF16, "vv")
        P.memset("dve", vv[:, :, 128:129], 1.0)
        PT = [A.alloc([18, 512], BF16, "PT%d" % i) for i in range(2)]
        tA = A.alloc([512], F32, "tA")
        tB = A.alloc([512], F32, "tB")
        obufs = [A.alloc([128], BF16, "obuf%d" % i) for i in range(4)]
        rcps = [A.alloc([1], F32, "rcp%d" % i) for i in range(4)]
        fcnt = [0]
        pend_fin = [None]
        stage = [A.alloc([NTOK], BF16, "ostage%d" % i) for i in range(2)]
        side = side_fn(A.off) if side_fn is not None else None
        uq_src = I["mla_w_uq"].ap[l].rearrange("(kc p) n -> p kc n", p=128)
        ukv_src = I["mla_w_ukv"].ap[l].rearrange("(kc p) n -> p kc n", p=128)
        tblocks = [(i * 512, 512) for i in range(4)] + ([(2048, 256)] if with_ctx_q else [])
        kblocks = [(i * 512, 512) for i in range(4)] + [(2048, 256)]
        pc = [0]
        ptc = [0]

        def pbank():
            pc[0] += 1
            return bk(pc[0] % 2)

        for h in range(MLA_H):
            wq_, wkv_ = wq[h % 2], wkv[h % 2]
            q0 = h * 192
            P.dma("pool", wq_[:, :, 0:192], V(I["mla_w_uq"], uq_src[:, :, q0:q0 + 192]))
            P.dma("pool", wq_[:, :, 192:224], V(I["mla_w_uq"], uq_src[:, :, q0 + 160:q0 + 192]))
            P.dma("pool", wq_[:, :, 224:256], V(I["mla_w_uq"], uq_src[:, :, q0 + 128:q0 + 160]))
            P.dma("pool", wkv_[:, :, :], V(I["mla_w_ukv"], ukv_src[:, :, h * 256:(h + 1) * 256]))
            for (tb0, tbn) in tblocks:
                ps = pbank()
                for kc in range(6):
                    P.mm(ps[:, 0:tbn], wq_[:, kc, 0:128], cq[:, kc, tb0:tb0 + tbn], start=(kc == 0), stop=(kc == 5))
                P.act(qn[:, tb0:tb0 + tbn], ps[:, 0:tbn], AF.Identity, scale=MLA_SCALE)
                pa = pbank()
                for kc in range(6):
                    P.mm(pa[0:64, 0:tbn], wq_[:, kc, 128:192], cq[:, kc, tb0:tb0 + tbn], start=(kc == 0), stop=(kc == 5))
                if tb0 < NX:
                    pb_ = pbank()
                    for kc in range(6):
                        P.mm(pb_[0:64, 0:tbn], wq_[:, kc, 192:256], cq[:, kc, tb0:tb0 + tbn], start=(kc == 0), stop=(kc == 5))
                    P.tt("dve", tA[0:64, 0:tbn], pa[0:64, 0:tbn], rope[0:64, 0, tb0:tb0 + tbn], ALU.mult)
                    P.tt("dve", tB[0:64, 0:tbn], pb_[0:64, 0:tbn], rope[0:64, 1, tb0:tb0 + tbn], ALU.mult)
                    P.tt("dve", qr[0:64, tb0:tb0 + tbn], tA[0:64, 0:tbn], tB[0:64, 0:tbn], ALU.add)
                else:
                    P.act(qr[0:64, tb0:tb0 + tbn], pa[0:64, 0:tbn], AF.Identity, scale=MLA_SCALE)
            for (tb0, tbn) in kblocks:
                ps = pbank()
                for kc in range(4):
                    P.mm(ps[:, 0:tbn], wkv_[:, kc, 0:128], ckv[:, kc, tb0:tb0 + tbn], start=(kc == 0), stop=(kc == 3))
                P.copy("act", kn[:, tb0:tb0 + tbn], ps[:, 0:tbn])
            for kt in range(18):
                ps = pbank()
                for kc in range(4):
                    P.mm(ps[:, 0:128], ckv[:, kc, kt * 128:(kt + 1) * 128], wkv_[:, kc, 128:256], start=(kc == 0), stop=(kc == 3))
                P.copy("dve", vv[:, kt, 0:128], ps[:, 0:128])
            stg = stage[h % 2]
            jobs = [(qb * 512, 512, list(range(18))) for qb in range(4)]
            if with_ctx_q:
                jobs.append((2048, 256, [16, 17]))
            def qk_stage(job):
                (q0_, qn_, ktiles) = job
                pt_ = PT[ptc[0] % 2]
                ptc[0] += 1
                for ki, kt in enumerate(ktiles):
                    ps = bk(2 + ki % 3)
                    P.mm(ps[:, 0:qn_], kn[:, kt * 128:(kt + 1) * 128], qn[:, q0_:q0_ + qn_], start=True, stop=False)
                    P.mm(ps[:, 0:qn_], kr[0:64, kt * 128:(kt + 1) * 128], qr[0:64, q0_:q0_ + qn_], start=False, stop=True)
                    P.act(pt_[:, ki, 0:qn_], ps[:, 0:qn_], AF.Exp)
                return pt_

            def pv_stage(job, pt_):
                (q0_, qn_, ktiles) = job
                for qb in range(qn_ // 128):
                    po = bk(5 + qb % 2)
                    for ki, kt in enumerate(ktiles):
                        P.mm(po[:, 0:129], pt_[:, ki, qb * 128:(qb + 1) * 128], vv[:, kt, 0:129],
                             start=(ki == 0), stop=(ki == len(ktiles) - 1))
                    ob = obufs[fcnt[0] % 4]
                    attn_fin_a(po, ob, rcps[fcnt[0] % 4])
                    fcnt[0] += 1
                    if pend_fin[0] is not None:
                        attn_fin_b(*pend_fin[0])
                    pend_fin[0] = (ob, q0_ + qb * 128, ident, stg)

            prev = None
            for job in jobs:
                cur = qk_stage(job)
                if prev is not None:
                    pv_stage(*prev)
                prev = (job, cur)
                if side is not None:
                    next(side, None)
            pv_stage(*prev)
            attn_fin_b(*pend_fin[0])
            pend_fin[0] = None
            r0 = 512 + h * 128
            P.dma("sp", S["catT"][r0:r0 + 128, 0:nq_tot], stg[:, 0:nq_tot])
        if side is not None:
            for _ in side:
                pass

    def phase_na(l, with_ctx_q, side=None):
        new_phase()
        nq_tot = NTOK if with_ctx_q else NX
        ident = A.alloc([128], BF16, "ident")
        P.dma("sp", ident[:, :], I["ident"][:, :])
        mask = A.alloc([3200], F32, "mask")
        P.dma("sp", mask[:, :], I["namask"][:, :])
        braw = A.alloc([3200], F32, "braw")
        bias = [A.alloc([5, 5, 128], BF16, "bias%d" % i) for i in range(2)]
        qT = [A.alloc([NTOK], BF16, "qT%d" % i) for i in range(2)]
        kT = [A.alloc([NTOK], BF16, "kT%d" % i) for i in range(2)]
        vv = [A.alloc([18, 132], BF16, "vv%d" % i) for i in range(2)]
        for i in range(2):
            P.memset("dve", vv[i][:, :, 128:129], 1.0)
        PT = [A.alloc([7, 128], BF16, "PT%d" % i) for i in range(2)]
        obufs = [A.alloc([128], BF16, "obuf%d" % i) for i in range(4)]
        rcps = [A.alloc([1], F32, "rcp%d" % i) for i in range(4)]
        fcnt = [0]
        pend_fin = [None]

        def fin(po, dst_col, stg_):
            ob = obufs[fcnt[0] % 4]
            attn_fin_a(po, ob, rcps[fcnt[0] % 4])
            fcnt[0] += 1
            if pend_fin[0] is not None:
                attn_fin_b(*pend_fin[0])
            pend_fin[0] = (ob, dst_col, ident, stg_)

        def fin_flush():
            if pend_fin[0] is not None:
                attn_fin_b(*pend_fin[0])
            pend_fin[0] = None

        stage = [A.alloc([NTOK], BF16, "ostage%d" % i) for i in range(2)]
        vsrc = S["vna"].ap.rearrange("(t p) c -> p t c", p=128)
        gi = 0
        def na_load(h):
            q_, k_, v_, b_ = qT[h % 2], kT[h % 2], vv[h % 2], bias[h % 2]
            P.dma("sp", q_[:, :], S["qnaT"][h * 128:(h + 1) * 128, :])
            P.dma("sp", k_[:, :], S["knaT"][h * 128:(h + 1) * 128, :])
            P.dma("sp", v_[:, :, 0:128], V(S["vna"], vsrc[:, :, h * 128:(h + 1) * 128]))
            P.dma("sp", braw[:, :], V(I["rpbg"], I["rpbg"].ap[l, h]))
            P.tt("pool", V(b_, b_.ap.rearrange("p a b c -> p (a b c)")), braw[:, :], mask[:, :], ALU.add)

        na_load(0)
        for h in range(NA_H):
            q_, k_, v_, b_ = qT[h % 2], kT[h % 2], vv[h % 2], bias[h % 2]
            if h + 1 < NA_H:
                na_load(h + 1)
            stg = stage[h % 2]
            def na_qk(i):
                cls, j0 = _na_cls(i), _na_j0(i)
                g = gic[0]
                gic[0] += 1
                pa, pb_ = bk(2 * (g % 2)), bk(2 * (g % 2) + 1)
                pt_ = PT[g % 2]
                qs = q_[:, i * 128:(i + 1) * 128]
                ktiles = [j0 + c for c in range(5)] + [16, 17]
                for c in range(7):
                    dst = pa[:, c * 128:(c + 1) * 128] if c < 4 else pb_[:, (c - 4) * 128:(c - 3) * 128]
                    kt = ktiles[c]
                    if c < 5:
                        P.mm(dst, k_[:, kt * 128:(kt + 1) * 128], qs, start=True, stop=False)
                        P.mm(dst, ident[:, :], b_[:, cls, c, :], start=False, stop=True)
                    else:
                        P.mm(dst, k_[:, kt * 128:(kt + 1) * 128], qs, start=True, stop=True)
                P.act(V(pt_, pt_.ap[:, 0:4, :].rearrange("p a b -> p (a b)")), pa[:, 0:512], AF.Exp)
                P.act(V(pt_, pt_.ap[:, 4:7, :].rearrange("p a b -> p (a b)")), pb_[:, 0:384], AF.Exp)
                return (i, g, pt_, ktiles)

            def na_pv(i, g, pt_, ktiles):
                po = bk(4 + g % 2)
                for c in range(7):
                    P.mm(po[:, 0:129], pt_[:, c, :], v_[:, ktiles[c], 0:129], start=(c == 0), stop=(c == 6))
                fin(po, i * 128, stg)

            gic = [gi]
            prev = None
            for i in range(16):
                cur = na_qk(i)
                if prev is not None:
                    na_pv(*prev)
                prev = cur
                if side is not None and i % 2 == 1:
                    next(side, None)
            na_pv(*prev)
            gi = gic[0]
            if with_ctx_q:
                for qt in (16, 17):
                    pa = bk(2 * (gi % 2))
                    pt_ = PT[gi % 2]
                    gi += 1
                    for c, kt in enumerate((16, 17)):
                        P.mm(pa[:, c * 128:(c + 1) * 128], k_[:, kt * 128:(kt + 1) * 128], q_[:, qt * 128:(qt + 1) * 128])
                    P.act(V(pt_, pt_.ap[:, 0:2, :].rearrange("p a b -> p (a b)")), pa[:, 0:256], AF.Exp)
                    po = bk(4 + gi % 2)
                    for c, kt in enumerate((16, 17)):
                        P.mm(po[:, 0:129], pt_[:, c, :], v_[:, kt, 0:129], start=(c == 0), stop=(c == 1))
                    fin(po, qt * 128, stg)
            fin_flush()
            r0 = 1536 + h * 128
            P.dma("sp", S["catT"][r0:r0 + 128, 0:nq_tot], stg[:, 0:nq_tot])
        if side is not None:
            for _ in side:
                pass

    def post_residual(y_views, Ab, xres_src, dst, sqj, ssv, rs, xt, ot, tmpc=None, preloaded=False, stq="pool"):
        P.memset("dve", ssv[:, 0:4], 0.0)
        for q, yv in enumerate(y_views):
            P.act(sqj[:, :], yv, AF.Square, accum=ssv[:, q:q + 1])
        P.tt("dve", ssv[:, 4:5], ssv[:, 0:1], ssv[:, 1:2], ALU.add)
        P.tt("dve", ssv[:, 5:6], ssv[:, 2:3], ssv[:, 3:4], ALU.add)
        P.tt("dve", ssv[:, 6:7], ssv[:, 4:5], ssv[:, 5:6], ALU.add)
        rstd_from_ss(ssv[:, 6:7], rs[:, :], D)
        if not preloaded:
            P.dma("sp", xt[:, :], xres_src)
        for q, yv in enumerate(y_views):
            cs = slice(q * 512, (q + 1) * 512)
            if ot is None:
                tc_ = tmpc[q % 2]
                P.stt("dve", tc_[:, :], yv, rs[:, :], Ab[:, cs], ALU.mult, ALU.mult)
                P.tt("pool", xt[:, cs], tc_[:, :], xt[:, cs], ALU.add)
            else:
                P.stt("dve", ot[:, cs], yv, rs[:, :], Ab[:, cs], ALU.mult, ALU.mult)
                P.tt("pool", ot[:, cs], ot[:, cs], xt[:, cs], ALU.add)
        P.dma(stq, dst, (xt if ot is None else ot)[:, :])

    def phase_outproj(l, xsrc, tiles):
        new_phase()
        wo = A.alloc([16, D], BF16, "wo")
        wsrc = I["w_out"].ap[l].rearrange("(kc p) n -> p kc n", p=128)
        for q in range(4):
            P.dma("pool", wo[:, :, q * 512:(q + 1) * 512], V(I["w_out"], wsrc[:, :, q * 512:(q + 1) * 512]))
        Ab = [A.alloc([D], F32, "A2_%d" % k) for k in range(2)]
        for k in range(2):
            load_dv(l, k, 2, Ab[k][:, :])
        cat = [A.alloc([16, 128], BF16, "cat%d" % i) for i in range(2)]
        xt = [A.alloc([D], F32, "xt%d" % i) for i in range(2)]
        ot = [A.alloc([D], F32, "ot%d" % i) for i in range(2)]
        sqj = A.alloc([512], BF16, "sqj")
        ssv = [A.alloc([8], F32, "ssv%d" % i) for i in range(2)]
        rs = [A.alloc([1], F32, "rs%d" % i) for i in range(2)]
        csrc = S["catT"].ap.rearrange("(kc p) t -> p kc t", p=128)
        for n, tile in enumerate(tiles):
            k = 0 if tile < 16 else 1
            c_ = cat[n % 2]
            P.dma("sp", c_[:, :, :], V(S["catT"], csrc[:, :, tile * 128:(tile + 1) * 128]))
            P.dma("sp", xt[n % 2][:, :], xsrc[tile * 128:(tile + 1) * 128, :])
            ys = []
            for q in range(4):
                ps = bk((n % 2) * 4 + q)
                for kc in range(16):
                    P.mm(ps[:, :], c_[:, kc, :], wo[:, kc, q * 512:(q + 1) * 512], start=(kc == 0), stop=(kc == 15))
                ys.append(ps[:, :])
            rows = slice(tile * 128, (tile + 1) * 128)
            post_residual(ys, Ab[k], xsrc[rows, :], S["X1"][rows, :], sqj, ssv[n % 2], rs[n % 2], xt[n % 2], ot[n % 2],
                          preloaded=True)

    def phase_ffn(l, blocks, final):
        new_phase()
        Ab = [A.alloc([D], F32, "A4_%d" % k) for k in range(2)]
        for k in range(2):
            load_dv(l, k, 5, Ab[k][:, :])
        TBMAX = max(b[1] for b in blocks)
        hT = A.alloc([44, TBMAX], BF16, "hT")
        xt = [A.alloc([D], F32, "xt%d" % i) for i in range(2)]
        tmpc = [A.alloc([512], F32, "tmpc%d" % i) for i in range(2)]
        sqj = A.alloc([512], BF16, "sqj")
        ssv = [A.alloc([8], F32, "ssv%d" % i) for i in range(2)]
        rs = [A.alloc([1], F32, "rs%d" % i) for i in range(2)]
        mT = A.alloc([16, TBMAX], BF16, "mT")
        ssq = A.alloc([32], F32, "ssq")
        mark = A.off
        msrc = S["modT"].ap.rearrange("(kc p) t -> p kc t", p=128)

        def load_mT(t0_, tbn_):
            for q in range(4):
                P.dma("sp", mT[:, 4 * q:4 * q + 4, 0:tbn_], V(S["modT"], msrc[:, 4 * q:4 * q + 4, t0_:t0_ + tbn_]))

        load_mT(blocks[0][0], blocks[0][1])
        gsrc = I["w_ffn_gate"].ap[l].rearrange("(kc p) n -> p kc n", p=128)
        usrc = I["w_ffn_up"].ap[l].rearrange("(kc p) n -> p kc n", p=128)
        dsrc = I["w_ffn_down"].ap[l].rearrange("(fc p) n -> p fc n", p=128)
        WSZ = 16 * 512 * 2 // 4
        off_pair = [mark, mark + 2 * WSZ]
        off_sg = mark + 4 * WSZ
        post_jobs = []

        def alloc_at(off, shape, dt, name):
            A.off = off
            return A.alloc(shape, dt, name)

        for bi, (t0, tbn, sub) in enumerate(blocks):
            ntile = tbn // 128
            sg = [alloc_at(off_sg + i * 512, [512], F32, "sg%d" % i) for i in range(4)]
            pairs = {}
            pcn = 0
            for fg in range(11):
                par = fg % 2
                if par not in pairs or fg < 2:
                    pairs[par] = (alloc_at(off_pair[par], [16, 512], BF16, "wg%d" % par),
                                  alloc_at(off_pair[par] + WSZ, [16, 512], BF16, "wu%d" % par))
                g_, u_ = pairs[par]
                P.dma("pool", g_[:, :, :], V(I["w_ffn_gate"], gsrc[:, :, fg * 512:(fg + 1) * 512]))
                P.dma("pool", u_[:, :, :], V(I["w_ffn_up"], usrc[:, :, fg * 512:(fg + 1) * 512]))
                for q4 in range(4):
                    fc = fg * 4 + q4
                    for (s0, sn) in sub:
                        pg, pu = bk((pcn % 4) * 2), bk((pcn % 4) * 2 + 1)
                        sg_ = sg[pcn % 4]
                        pcn += 1
                        cs = slice(s0, s0 + sn)
                        for kc in range(16):
                            P.mm(pg[:, 0:sn], g_[:, kc, q4 * 128:(q4 + 1) * 128], mT[:, kc, cs], start=(kc == 0), stop=(kc == 15))
                        for kc in range(16):
                            P.mm(pu[:, 0:sn], u_[:, kc, q4 * 128:(q4 + 1) * 128], mT[:, kc, cs], start=(kc == 0), stop=(kc == 15))
                        P.act(sg_[:, 0:sn], pg[:, 0:sn], AF.Silu)
                        P.tt("dve", hT[:, fc, cs], sg_[:, 0:sn], pu[:, 0:sn], ALU.mult)
                        if fg == 0:
                            for _ in range(2):
                                if post_jobs:
                                    post_jobs.pop(0)()
                if fg == 0:
                    while post_jobs:
                        post_jobs.pop(0)()
            if bi + 1 < len(blocks):
                load_mT(blocks[bi + 1][0], blocks[bi + 1][1])
            wd = [alloc_at(off_pair[1] + i * 1024, [4, 512], BF16, "wd%d" % i) for i in range(2)]
            ybuf = alloc_at(off_pair[1] + 2048, [ntile, D], BF16, "ybuf")
            assert A.off <= off_sg
            dcn = 0
            P.memset("pool", ssq[:, :], 0.0)
            for nq in range(4):
                for f4 in range(11):
                    d_ = wd[dcn % 2]
                    dcn += 1
                    P.dma("pool", d_[:, :, :], V(I["w_ffn_down"], dsrc[:, f4 * 4:(f4 + 1) * 4, nq * 512:(nq + 1) * 512]))
                    for fi in range(4):
                        fc = f4 * 4 + fi
                        for tl in range(ntile):
                            P.mm(bk((nq * ntile + tl) % 8)[:, :], hT[:, fc, tl * 128:(tl + 1) * 128], d_[:, fi, :],
                                 start=(fc == 0), stop=(fc == 43))
                for tl in range(ntile):
                    k_ = 0 if (t0 + tl * 128) < NX else 1
                    acc_ = bk((nq * ntile + tl) % 8)
                    P.act(sqj[:, :], acc_[:, :], AF.Square, accum=ssq[:, tl * 4 + nq:tl * 4 + nq + 1])
                    P.tt("dve", ybuf[:, tl, nq * 512:(nq + 1) * 512], acc_[:, :], Ab[k_][:, nq * 512:(nq + 1) * 512], ALU.mult)

            def mk_job(tl, t0=t0, ntile=ntile, ybuf=ybuf):
                def job():
                    tok = t0 + tl * 128
                    k = 0 if tok < NX else 1
                    ys = [ybuf[:, tl, q * 512:(q + 1) * 512] for q in range(4)]
                    rows = slice(tok, tok + 128)
                    dst = OUT[rows, :] if final else S["X2"][rows, :]
                    if tl == 0:
                        P.dma("sp", xt[0][:, :], S["X1"][t0:t0 + 128, :])
                    if tl + 1 < ntile:
                        P.dma("sp", xt[(tl + 1) % 2][:, :], S["X1"][tok + 128:tok + 256, :])
                    sv, r_, x_ = ssv[tl % 2], rs[tl % 2], xt[tl % 2]
                    P.tt("dve", sv[:, 4:5], ssq[:, tl * 4:tl * 4 + 1], ssq[:, tl * 4 + 1:tl * 4 + 2], ALU.add)
                    P.tt("dve", sv[:, 5:6], ssq[:, tl * 4 + 2:tl * 4 + 3], ssq[:, tl * 4 + 3:tl * 4 + 4], ALU.add)
                    P.tt("dve", sv[:, 6:7], sv[:, 4:5], sv[:, 5:6], ALU.add)
                    rstd_from_ss(sv[:, 6:7], r_[:, :], D)
                    P.stt("dve", x_[:, :], ybuf[:, tl, :], r_[:, :], x_[:, :], ALU.mult, ALU.add)
                    P.dma("sp", dst, x_[:, :])
                return job

            post_jobs = [mk_job(tl) for tl in range(ntile)]
        while post_jobs:
            post_jobs.pop(0)()

    SUB768 = [(0, 512), (512, 256)]
    all_tiles = list(range(18))
    x_tiles = list(range(16))
    for l in range(n_layers):
        last = (l == DEPTH - 1)
        src = I["xall"] if l == 0 else S["X2"]
        if want("adaln") and l == 0:
            phase_adaln(l)
        MT_OFF = 52000 - 18432 - 64
        A.reset(MT_OFF)
        mTd = A.alloc([16, NTOK], BF16, "mTd")
        nbase = 0
        if want("norm1"):
            phase_norm(l, src, 0, 1, all_tiles, direct=mTd, base=nbase)
        if want("inproj"):
            phase_inproj(l, mTd, nbase)
        if want("hyena"):
            phase_hyena(l, NX, 0, False)
            if not last:
                phase_hyena(l, NCTX, NX, True)
        if want("mla"):
            sf = (lambda base, l=l: adaln_gen(l + 1, base, bank=7, nrow=1)) if (l + 1 < n_layers and want("adaln")) else None
            phase_mla(l, not last, sf)
        if want("na"):
            phase_na(l, not last, None)
        if want("outproj"):
            phase_outproj(l, src, x_tiles if last else all_tiles)
        if want("ffn"):
            phase_norm(l, S["X1"], 3, 4, x_tiles if last else all_tiles)
            if last:
                phase_ffn(l, [(0, 768, SUB768), (768, 768, SUB768), (1536, 512, [(0, 512)])], True)
            else:
                phase_ffn(l, [(0, 768, SUB768), (768, 768, SUB768), (1536, 768, SUB768)], False)
    P.barrier()
    P.emit()
    es.close()
    return nc


_WEIGHT_NAMES = ["w_ada", "b_ada", "g_attn_pre", "g_attn_post", "g_ffn_pre", "g_ffn_post", "w_in", "hy_conv_w",
                 "hy_conv_b", "hy_f_w1", "hy_f_b1", "hy_f_w2", "hy_f_b2", "hy_f_w3", "hy_f_freq", "hy_bias",
                 "mla_g_q", "mla_w_uq", "mla_g_kv", "mla_w_ukv", "w_out", "w_ffn_gate", "w_ffn_up", "w_ffn_down"]


def make_in_maps(inputs, cores):
    consts = _const_tables()
    shared = {k: np.ascontiguousarray(np.asarray(inputs[k], dtype=np.float32)) for k in _WEIGHT_NAMES}
    shared["rpbg"] = _na_gather(np.asarray(inputs["na_rpb"], dtype=np.float32))
    shared.update(consts)
    maps = []
    for b in cores:
        m = dict(shared)
        m["xall"] = np.ascontiguousarray(np.concatenate([inputs["x"][b], inputs["ctx"][b]], axis=0).astype(np.float32))
        m["cc"] = np.ascontiguousarray(np.stack([inputs["c"][b], inputs["c_ctx"]], axis=0).astype(np.float32))
        maps.append(m)
    return maps


def kernel(**inputs):
    inputs = {k: np.asarray(v) for k, v in inputs.items()}
    nc = build_program()
    maps = make_in_maps(inputs, list(range(N_CORES)))
    res = run_bass_kernel_spmd(nc, maps, core_ids=list(range(N_CORES)))
    return np.stack([np.asarray(r["out"], dtype=np.float32) for r in res.results], axis=0)
```
